# Optimizing a Trainium2 kernel written in Bass

```python
import math
import jax, jax.numpy as jnp
from jax import lax
import numpy as np

D_MODEL = 1024
BATCH = 8
SEQ = 4096
DEPTH = 2
DEC_BATCH = 16
DEC_SEQ = 4096
PAST_LEN = 128

HEAD_DIM = 64
A_Q_HEADS = 4
A_KV_HEADS = 2
A_GROUP = A_Q_HEADS // A_KV_HEADS
A_HALF_WINDOW = 128
A_BLOCK = 128
D_A = A_Q_HEADS * HEAD_DIM
D_A_KV = A_KV_HEADS * HEAD_DIM
B_HEADS = 6
B_PATTERNS = ((128, 1), (512, 4), (2048, 16))
B_BLOCK = 64
D_B = B_HEADS * HEAD_DIM
C_CH = 384
C_CONV_WIDTH = 31
D_MIX = D_A + D_B + C_CH
IN_WIDTH = D_A + 2 * D_A_KV + 3 * D_B + 2 * C_CH
FF = 2816
FF_CONV_WIDTH = 3
ROPE_THETA = 10000.0
EPS = 1e-6
NEG = -1e30

kernel_name = "hymba_style_hybrid_encoder_two_batches"


def rmsnorm(x, g):
    xf = x.astype(jnp.float32)
    y = xf * lax.rsqrt(jnp.mean(xf * xf, axis=-1, keepdims=True) + EPS)
    return (y * g.astype(jnp.float32)).astype(x.dtype)


def layernorm(x, g, b):
    xf = x.astype(jnp.float32)
    mu = jnp.mean(xf, axis=-1, keepdims=True)
    var = jnp.mean(jnp.square(xf - mu), axis=-1, keepdims=True)
    y = (xf - mu) * lax.rsqrt(var + EPS)
    return (y * g.astype(jnp.float32) + b.astype(jnp.float32)).astype(x.dtype)


def rope(x):
    S, dh = x.shape[1], x.shape[-1]
    half = dh // 2
    inv = 1.0 / (jnp.float32(ROPE_THETA) ** (jnp.arange(half, dtype=jnp.float32) / half))
    ang = jnp.arange(S, dtype=jnp.float32)[:, None] * inv[None, :]
    cos = jnp.cos(ang)[None, :, None, :]
    sin = jnp.sin(ang)[None, :, None, :]
    xf = x.astype(jnp.float32)
    x1, x2 = xf[..., :half], xf[..., half:]
    return jnp.concatenate([x1 * cos - x2 * sin, x2 * cos + x1 * sin], axis=-1).astype(x.dtype)


def dwconv(x, w, b):
    k = w.shape[0]
    y = lax.conv_general_dilated(x, w[:, None, :].astype(x.dtype), window_strides=(1,),
                                 padding=[((k - 1) // 2, (k - 1) // 2)],
                                 dimension_numbers=('NWC', 'WIO', 'NWC'),
                                 feature_group_count=x.shape[-1])
    return y + b.astype(x.dtype)


def banded_attention(q, k, v, half, blk, sink=None):
    N, L, Hk, G, Dh = q.shape
    nb = -(-L // blk)
    Lp = nb * blk
    qp = jnp.pad(q, ((0, 0), (0, Lp - L), (0, 0), (0, 0), (0, 0)))
    kv_pad = ((0, 0), (blk, Lp - L + blk), (0, 0), (0, 0))
    kp = jnp.pad(k, kv_pad).reshape(N, nb + 2, blk, Hk, Dh)
    vp = jnp.pad(v, kv_pad).reshape(N, nb + 2, blk, Hk, Dh)
    kb = jnp.concatenate([kp[:, :-2], kp[:, 1:-1], kp[:, 2:]], axis=2)
    vb = jnp.concatenate([vp[:, :-2], vp[:, 1:-1], vp[:, 2:]], axis=2)
    qb = qp.reshape(N, nb, blk, Hk, G, Dh)
    s = jnp.einsum('nbqhgd,nbkhd->nbhgqk', qb, kb,
                   preferred_element_type=jnp.float32) * (1.0 / math.sqrt(Dh))
    qpos = jnp.arange(nb)[:, None] * blk + jnp.arange(blk)[None, :]
    kpos = jnp.arange(nb)[:, None] * blk - blk + jnp.arange(3 * blk)[None, :]
    valid = ((jnp.abs(kpos[:, None, :] - qpos[:, :, None]) <= half)
             & (kpos >= 0)[:, None, :] & (kpos < L)[:, None, :])
    s = jnp.where(valid[None, :, None, None], s, NEG)
    m = jnp.max(s, axis=-1, keepdims=True)
    if sink is not None:
        sk = sink.astype(jnp.float32)[None, None, :, :, None, None]
        m = jnp.maximum(m, sk)
    p = jnp.exp(s - m)
    denom = jnp.sum(p, axis=-1, keepdims=True)
    if sink is not None:
        denom = denom + jnp.exp(sk - m)
    o = jnp.einsum('nbhgqk,nbkhd->nbqhgd', p / denom, vb.astype(jnp.float32))
    lse = (jnp.log(denom) + m)[..., 0]
    lse = jnp.transpose(lse, (0, 1, 4, 2, 3)).reshape(N, Lp, Hk, G)[:, :L]
    o = o.reshape(N, Lp, Hk, G, Dh)[:, :L]
    return o.astype(q.dtype), lse


def head_rmsnorm(t, n_heads, g):
    N, S, _ = t.shape
    return rmsnorm(t.reshape(N, S, n_heads, HEAD_DIM), g)


def token_mixers(h, w_in, qn_a, kn_a, sink_a, qn_b, kn_b,
                 conv_c_w, conv_c_b, ln_c_g, ln_c_b, w_out):
    N, S, _ = h.shape
    proj = h @ w_in
    sizes = [D_A, D_A_KV, D_A_KV, D_B, D_B, D_B, C_CH, C_CH]
    points = [int(p) for p in np.cumsum(sizes)[:-1]]
    qa, ka, va, qb, kb, vb, ca, cg = jnp.split(proj, points, axis=-1)

    qa = rope(head_rmsnorm(qa, A_Q_HEADS, qn_a)).reshape(N, S, A_KV_HEADS, A_GROUP, HEAD_DIM)
    ka = rope(head_rmsnorm(ka, A_KV_HEADS, kn_a))
    va = va.reshape(N, S, A_KV_HEADS, HEAD_DIM)
    oa, _ = banded_attention(qa, ka, va, A_HALF_WINDOW, A_BLOCK,
                             sink_a.reshape(A_KV_HEADS, A_GROUP))
    oa = oa.reshape(N, S, D_A)

    qb = rope(head_rmsnorm(qb, B_HEADS, qn_b))
    kb = rope(head_rmsnorm(kb, B_HEADS, kn_b))
    vb = vb.reshape(N, S, B_HEADS, HEAD_DIM)
    outs, lses = [], []
    for window, dil in B_PATTERNS:
        half = window // 2 // dil
        Ls = S // dil

        def to_strided(t):
            return jnp.transpose(t.reshape(N, Ls, dil, B_HEADS, HEAD_DIM),
                                 (0, 2, 1, 3, 4)).reshape(N * dil, Ls, B_HEADS, HEAD_DIM)

        o, lse = banded_attention(to_strided(qb)[:, :, :, None, :], to_strided(kb),
                                  to_strided(vb), half, B_BLOCK)
        o = jnp.transpose(o.reshape(N, dil, Ls, B_HEADS, HEAD_DIM), (0, 2, 1, 3, 4)).reshape(N, S, B_HEADS, HEAD_DIM)
        lse = jnp.transpose(lse.reshape(N, dil, Ls, B_HEADS), (0, 2, 1, 3)).reshape(N, S, B_HEADS)
        outs.append(o)
        lses.append(lse)
    wts = jax.nn.softmax(jnp.stack(lses, axis=0), axis=0)
    ob = jnp.sum(wts[..., None] * jnp.stack(outs, axis=0).astype(jnp.float32), axis=0)
    ob = ob.astype(h.dtype).reshape(N, S, D_B)

    u = ca * jax.nn.sigmoid(cg)
    u = dwconv(u, conv_c_w, conv_c_b)
    oc = jax.nn.silu(layernorm(u, ln_c_g, ln_c_b))

    return jnp.concatenate([oa, ob, oc], axis=-1) @ w_out


def channel_mixer(h, w_up, conv_f_w, conv_f_b, w_down):
    g, u = jnp.split(h @ w_up, 2, axis=-1)
    g = dwconv(g, conv_f_w, conv_f_b)
    return (jax.nn.silu(g) * u) @ w_down


def trunk(x, c, norm1_g, norm2_g, w_mod, b_mod, w_in, qn_a, kn_a, sink_a, qn_b, kn_b,
          conv_c_w, conv_c_b, ln_c_g, ln_c_b, w_out, w_up, conv_f_w, conv_f_b, w_down):
    for l in range(DEPTH):
        mod = jax.nn.silu(c) @ w_mod[l] + b_mod[l]
        sh1, sc1, g1, sh2, sc2, g2 = jnp.split(mod[:, None, :], 6, axis=-1)
        h = rmsnorm(x, norm1_g[l]) * (1 + sc1) + sh1
        x = x + g1 * token_mixers(h, w_in[l], qn_a[l], kn_a[l], sink_a[l], qn_b[l], kn_b[l],
                                  conv_c_w[l], conv_c_b[l], ln_c_g[l], ln_c_b[l], w_out[l])
        h = rmsnorm(x, norm2_g[l]) * (1 + sc2) + sh2
        x = x + g2 * channel_mixer(h, w_up[l], conv_f_w[l], conv_f_b[l], w_down[l])
    return x


def setup_inputs(seed: int = 0) -> dict:
    key = jax.random.key(seed)
    ks = jax.random.split(key, 26)
    f32 = jnp.float32

    def nrm(k, shape, scale):
        return jax.random.normal(k, shape, f32) * scale

    return {
        "x_prompt": nrm(ks[0], (BATCH, SEQ, D_MODEL), 1.0),
        "x_sample": nrm(ks[1], (DEC_BATCH, DEC_SEQ, D_MODEL), 1.0),
        "c_prompt": nrm(ks[2], (BATCH, D_MODEL), 1.0),
        "c_sample": nrm(ks[3], (DEC_BATCH, D_MODEL), 1.0),
        "norm1_g": 1.0 + nrm(ks[4], (DEPTH, D_MODEL), 0.02),
        "norm2_g": 1.0 + nrm(ks[5], (DEPTH, D_MODEL), 0.02),
        "w_mod": nrm(ks[6], (DEPTH, D_MODEL, 6 * D_MODEL), 0.5 * D_MODEL ** -0.5),
        "b_mod": nrm(ks[7], (DEPTH, 6 * D_MODEL), 0.02),
        "w_in": nrm(ks[8], (DEPTH, D_MODEL, IN_WIDTH), D_MODEL ** -0.5),
        "qn_a": 1.0 + nrm(ks[9], (DEPTH, HEAD_DIM), 0.02),
        "kn_a": 1.0 + nrm(ks[10], (DEPTH, HEAD_DIM), 0.02),
        "sink_a": nrm(ks[11], (DEPTH, A_Q_HEADS), 0.5),
        "qn_b": 1.0 + nrm(ks[12], (DEPTH, HEAD_DIM), 0.02),
        "kn_b": 1.0 + nrm(ks[13], (DEPTH, HEAD_DIM), 0.02),
        "conv_c_w": nrm(ks[14], (DEPTH, C_CONV_WIDTH, C_CH), C_CONV_WIDTH ** -0.5),
        "conv_c_b": nrm(ks[15], (DEPTH, C_CH), 0.02),
        "ln_c_g": 1.0 + nrm(ks[16], (DEPTH, C_CH), 0.02),
        "ln_c_b": nrm(ks[17], (DEPTH, C_CH), 0.02),
        "w_out": nrm(ks[18], (DEPTH, D_MIX, D_MODEL), D_MIX ** -0.5),
        "w_up": nrm(ks[19], (DEPTH, D_MODEL, 2 * FF), D_MODEL ** -0.5),
        "conv_f_w": nrm(ks[20], (DEPTH, FF_CONV_WIDTH, FF), FF_CONV_WIDTH ** -0.5),
        "conv_f_b": nrm(ks[21], (DEPTH, FF), 0.02),
        "w_down": nrm(ks[22], (DEPTH, FF, D_MODEL), FF ** -0.5),
    }


def reference(x_prompt, x_sample, c_prompt, c_sample, norm1_g, norm2_g, w_mod, b_mod, w_in,
              qn_a, kn_a, sink_a, qn_b, kn_b, conv_c_w, conv_c_b, ln_c_g, ln_c_b, w_out,
              w_up, conv_f_w, conv_f_b, w_down):
    y_prompt = trunk(x_prompt, c_prompt, norm1_g, norm2_g, w_mod, b_mod, w_in, qn_a, kn_a, sink_a,
                     qn_b, kn_b, conv_c_w, conv_c_b, ln_c_g, ln_c_b, w_out, w_up, conv_f_w,
                     conv_f_b, w_down)
    y_sample = trunk(x_sample, c_sample, norm1_g, norm2_g, w_mod, b_mod, w_in, qn_a, kn_a, sink_a,
                     qn_b, kn_b, conv_c_w, conv_c_b, ln_c_g, ln_c_b, w_out, w_up, conv_f_w,
                     conv_f_b, w_down)
    return (y_prompt, y_sample)
```

```python
import math
from contextlib import ExitStack
import numpy as np
import concourse.bass as bass
import concourse.mybir as mybir
from concourse.bass_utils import run_bass_kernel_spmd

F32 = mybir.dt.float32
BF16 = mybir.dt.bfloat16
ALU = mybir.AluOpType
AF = mybir.ActivationFunctionType
AX = mybir.AxisListType

S = 4096
D = 1024
DEPTH = 2
NSEQ = 3
INW = 2432
FF = 2816
NFB = 22
EPS = 1e-6
PADK = 1024
SRC_COL = [0, 128, 256, 512, 640, 768, 896, 1024, 1152, 384, 1280, 1408, 1536,
           1664, 1792, 1920, 2048, 2176, 2304]


class Tk:
    __slots__ = ("w", "r", "excl")

    def __init__(self, excl=False):
        self.w = {}
        self.r = {}
        self.excl = excl


class Sched:
    ENGS = ("pe", "act", "dve", "pool", "sp")
    NDSEM = 24

    def __init__(self):
        self.q = {e: [] for e in self.ENGS}
        self.dcount = [0] * self.NDSEM
        self.dn = 0
        self.pending = {e: {} for e in self.ENGS}

    def _deps(self, eng, rd, wr):
        deps = self.pending[eng]
        self.pending[eng] = {}

        def need(k, v):
            if deps.get(k, -1) < v:
                deps[k] = v

        for t in rd:
            for k, v in t.w.items():
                if k == eng and eng in ("pe", "sp"):
                    continue
                need(k, v)
            if t.excl:
                for k, v in t.r.items():
                    if k != eng:
                        need(k, v)
        for t in wr:
            for k, v in t.w.items():
                if k != eng:
                    need(k, v)
            for k, v in t.r.items():
                if k != eng:
                    need(k, v)
        for k, v in deps.items():
            if not isinstance(k, tuple):
                self.q[k][v][2] = True
        return deps

    def op(self, eng, fn, rd=(), wr=()):
        deps = self._deps(eng, rd, wr)
        idx = len(self.q[eng])
        self.q[eng].append([fn, deps, False, 0])
        for t in rd:
            t.r[eng] = idx
        for t in wr:
            t.w = {eng: idx}
            t.r = {}

    def dma(self, fn, rd=(), wr=(), q="sp"):
        i = self.dn % self.NDSEM
        self.dn += 1
        deps = self._deps(q, rd, wr)
        key = ("d", i)
        prev = self.dcount[i] * 16
        if prev > 0 and deps.get(key, -1) < prev:
            deps[key] = prev
        self.dcount[i] += 1
        val = self.dcount[i] * 16
        self.q[q].append([fn, deps, key, 0])
        for t in rd:
            t.r[key] = val
        for t in wr:
            t.w = {key: val}
            t.r = {}

    def barrier(self):
        last = {e: len(self.q[e]) - 1 for e in ("pe", "act", "dve", "pool")}
        for e in self.ENGS:
            p = self.pending[e]
            for k, v in last.items():
                if k != e and v >= 0 and p.get(k, -1) < v:
                    p[k] = v
            for i in range(self.NDSEM):
                if self.dcount[i] > 0:
                    p[("d", i)] = max(p.get(("d", i), 0), self.dcount[i] * 16)

    def finalize(self):
        for e in self.ENGS:
            for k, v in self.pending[e].items():
                if not isinstance(k, tuple):
                    self.q[k][v][2] = True


class Prog:
    def __init__(self, nseq=NSEQ, depth=DEPTH, debug=False):
        self.nseq = nseq
        self.depth = depth
        self.debug = debug
        self.nc = bass.Bass("TRN2", target_bir_lowering=False)
        self.s = Sched()
        self.uid = 0
        self.psi = 0
        self.psc = {}
        self.bg = []

    def name(self, n):
        self.uid += 1
        return "%s_%d" % (n, self.uid)

    def A(self, es, n, shape, dt):
        return es.enter_context(self.nc.sbuf_tensor(self.name(n), shape, dt))

    def ps(self):
        i = self.psi % 8
        self.psi += 1
        return self.PS[i], self.PSB[i], self.PST[i]

    def bg_step(self, n=1):
        for _ in range(n):
            if self.bg:
                self.bg.pop(0)()

    def bg_flush(self):
        while self.bg:
            self.bg.pop(0)()

    def psx(self, lo, hi):
        c = self.psc.get((lo, hi), 0)
        self.psc[(lo, hi)] = c + 1
        i = lo + c % (hi - lo)
        return self.PS[i], self.PSB[i], self.PST[i]

    def act(self, out, in_, func, rd, wr, **kw):
        self.s.op("act", lambda e: e.activation(out=out, in_=in_, func=func, **kw), rd, wr)

    def tt(self, eng, out, in0, in1, op, rd, wr):
        self.s.op(eng, lambda e: e.tensor_tensor(out=out, in0=in0, in1=in1, op=op), rd, wr)

    def ts(self, eng, out, in0, s1, s2, op0, op1, rd, wr):
        if s2 is None:
            self.s.op(eng, lambda e: e.tensor_scalar(out=out, in0=in0, scalar1=s1, scalar2=None, op0=op0), rd, wr)
        else:
            self.s.op(eng, lambda e: e.tensor_scalar(out=out, in0=in0, scalar1=s1, scalar2=s2, op0=op0, op1=op1), rd, wr)

    def stt(self, out, in0, scalar, in1, op0, op1, rd, wr):
        self.s.op("dve", lambda e: e.scalar_tensor_tensor(out=out, in0=in0, scalar=scalar, in1=in1, op0=op0, op1=op1), rd, wr)

    def cp(self, eng, out, in_, rd, wr):
        if eng == "act":
            self.s.op("act", lambda e: e.activation(out=out, in_=in_, func=AF.Copy), rd, wr)
        else:
            self.s.op(eng, lambda e: e.tensor_copy(out=out, in_=in_), rd, wr)

    def memset(self, eng, ap, val, wr):
        self.s.op(eng, lambda e: e.memset(ap, val), (), wr)

    def mm(self, out, lhsT, rhs, start, stop, rd, wr):
        self.s.op("pe", lambda e: e.matmul(out, lhsT=lhsT, rhs=rhs, start=start, stop=stop), rd, wr)

    def tr(self, out, in_, ident, rd, wr):
        self.s.op("pe", lambda e: e.transpose(out=out, in_=in_, identity=ident), rd, wr)

    def dma(self, out, in_, rd, wr, q="sp"):
        self.s.dma(lambda e: e.dma_start(out=out, in_=in_), rd, wr, q=q)

    def build(self):
        nc = self.nc
        ns, dp = self.nseq, self.depth
        dt = nc.dram_tensor
        self.x_d = dt("x", [ns, S, D], F32, kind="ExternalInput").ap()
        self.c_d = dt("c", [ns, D], F32, kind="ExternalInput").ap()
        self.cc_d = dt("ropec", [S, 64], F32, kind="ExternalInput").ap()
        self.ss_d = dt("ropes", [S, 64], F32, kind="ExternalInput").ap()
        W = {}
        for nm, shp in (("norm1_g", [DEPTH, D]), ("norm2_g", [DEPTH, D]), ("w_mod", [DEPTH, D, 6 * D]),
                        ("b_mod", [DEPTH, 6 * D]), ("w_in", [DEPTH, D, INW]), ("qn_a", [DEPTH, 64]),
                        ("kn_a", [DEPTH, 64]), ("sink_a", [DEPTH, 4]), ("qn_b", [DEPTH, 64]),
                        ("kn_b", [DEPTH, 64]), ("conv_c_w", [DEPTH, 31, 384]), ("conv_c_b", [DEPTH, 384]),
                        ("ln_c_g", [DEPTH, 384]), ("ln_c_b", [DEPTH, 384]), ("w_out", [DEPTH, D, D]),
                        ("w_up", [DEPTH, D, 2 * FF]), ("conv_f_w", [DEPTH, 3, FF]), ("conv_f_b", [DEPTH, FF]),
                        ("w_down", [DEPTH, FF, D])):
            W[nm] = dt(nm, shp, F32, kind="ExternalInput").ap()
        self.W = W
        self.y_d = dt("y", [ns, S, D], F32, kind="ExternalOutput").ap()
        kd = "ExternalOutput" if self.debug else "Internal"
        self.xa_d = dt("xa", [ns, S, D], F32, kind=kd).ap()
        self.xb_d = dt("xb", [ns, S, D], F32, kind="Internal").ap()
        self.qk_d = dt("qkT", [ns, 1152, S], BF16, kind=kd).ap()
        self.vs_d = dt("vsc", [ns, S, 8, 128], BF16, kind=kd).ap()
        self.u_d = dt("uT", [ns, 384, S], BF16, kind=kd).ap()
        self.mx_d = dt("mixA", [ns, 640, S], BF16, kind=kd).ap()
        self.gs_d = dt("gsc", [DEPTH * ns * 2, D], F32, kind="Internal").ap()

        with ExitStack() as top:
            self.PS, self.PSB, self.PST = [], [], []
            self.psall = top.enter_context(nc.psum_tensor(self.name("psall"), [128, 8 * 512], F32))
            for i in range(8):
                p = self.psall[:, i * 512:(i + 1) * 512]
                self.PS.append(p)
                self.PSB.append(p.bitcast(BF16))
                self.PST.append(Tk(excl=True))
            sems = {e: top.enter_context(nc.semaphore(self.name("s" + e))) for e in ("pe", "act", "dve", "pool")}
            dsems = [top.enter_context(nc.semaphore(self.name("d"))) for _ in range(Sched.NDSEM)]
            block = top.enter_context(nc.Block())
            self.persist(top)
            self.prologue()
            for l in range(dp):
                xin = [self.x_d[s] if l == 0 else self.xb_d[s] for s in range(ns)]
                xout = [self.y_d[s] if l == dp - 1 else self.xb_d[s] for s in range(ns)]
                self.layer(l, xin, xout)
            self.s.barrier()
            self.s.finalize()
            sch = self.s

            def mk(eng_name):
                def f(h):
                    sch_emit_engine(sch, eng_name, h, sems, dsems)
                return f

            for bname, en in (("tensor", "pe"), ("scalar", "act"), ("vector", "dve"), ("gpsimd", "pool"), ("sync", "sp")):
                getattr(block, bname)(mk(en))
        return nc

    def persist(self, es):
        A = self.A
        self.ident = A(es, "ident", [128, 128], F32)
        self.identb = A(es, "identb", [128, 128], BF16)
        self.epsT = A(es, "eps", [128, 1], F32)
        self.mhalf = A(es, "mhalf", [128, 1], F32)
        self.maskA = A(es, "maskA", [128, 384], BF16)
        self.maskB2 = [A(es, "maskB2", [128, 512], BF16) for _ in range(3)]
        self.vecs = A(es, "vecs", [128, DEPTH, 2, 8], F32)
        self.cvec = A(es, "cvec", [128, DEPTH, 3, 3], F32)
        self.wc = A(es, "wc", [128, DEPTH, 3, 31], F32)
        self.wf = A(es, "wf", [128, DEPTH, NFB, 3], F32)
        self.bf = A(es, "bf", [128, DEPTH, NFB], F32)
        self.modT = A(es, "modT", [128, DEPTH, 48, self.nseq], F32)
        self.ab = A(es, "ab", [128, self.nseq, 4, 8], F32)
        self.const_tk = Tk()
        self.ab_tk = Tk()

    def load_T(self, es_stage, stage, stage_tk, dst, src, R, C):
        nb = C // 128
        Rp = R + (R % 2)
        self.dma(stage[0:R, 0:C], src, (), (stage_tk,))
        ps, _, pt = self.ps()
        for j in range(nb):
            self.tr(ps[:, j * Rp:(j + 1) * Rp], stage[0:Rp, j * 128:(j + 1) * 128], self.ident[0:Rp, 0:Rp],
                    (stage_tk, self.const_tk), (pt,))
        self.cp("dve", dst, ps[:, 0:nb * Rp].rearrange("p (j r) -> p j r", r=Rp)[:, :, 0:R], (pt,), (self.const_tk,))

    def prologue(self):
        nc, W, ns = self.nc, self.W, self.nseq
        ctk = self.const_tk
        self.memset("pool", self.ident[:], 1.0, (ctk,))
        self.s.op("pool", lambda e: e.affine_select(out=self.ident[:], in_=self.ident[:], pattern=[[-1, 128]],
                                                    compare_op=ALU.is_equal, fill=0.0, base=0, channel_multiplier=1),
                  (ctk,), (ctk,))
        self.cp("pool", self.identb[:], self.ident[:], (ctk,), (ctk,))
        self.memset("pool", self.epsT[:], EPS, (ctk,))
        self.memset("pool", self.mhalf[:], -0.5, (ctk,))
        mA = self.maskA
        self.memset("pool", mA[:], 1.0, (ctk,))

        def sel(ap, pattern, cm, base):
            self.s.op("pool", lambda e: e.affine_select(out=ap, in_=ap, pattern=pattern, compare_op=ALU.is_ge,
                                                        fill=0.0, base=base, channel_multiplier=cm), (ctk,), (ctk,))
        sel(mA[:, 0:128], [[-1, 128]], 1, 0)
        sel(mA[:, 256:384], [[1, 128]], -1, 0)
        for v in range(3):
            mB = self.maskB2[v]
            self.memset("pool", mB[:], 1.0, (ctk,))
            for h in range(2):
                sel(mB[:, h * 256:h * 256 + 128], [[-1, 128]], 1, 0)
                sel(mB[:, h * 256 + 128:h * 256 + 256], [[1, 128]], -1, 0)
        for h in range(2):
            sel(self.maskB2[1][:, h * 256:h * 256 + 128], [[0, 128]], 1, -64)
            sel(self.maskB2[2][:, h * 256 + 128:h * 256 + 256], [[0, 128]], -1, 63)

        with ExitStack() as es:
            A = self.A
            stage = A(es, "stage", [128, 6 * D], F32)
            stk = Tk()
            self.memset("dve", stage[:], 0.0, (stk,))
            cT = A(es, "cT", [128, 8, ns], F32)
            scT = A(es, "scT", [128, 8, ns], BF16)
            screp = A(es, "screp", [128, 8, ns, 128], BF16)
            bmT = A(es, "bmT", [128, 48], F32)
            wm = A(es, "wm", [128, 8, 6 * D], BF16)
            wst = [A(es, "wst", [128, 8, 512], F32) for _ in range(4)]
            wst_tk = [Tk() for _ in range(4)]
            wm_tk = [Tk() for _ in range(48)]
            grow = A(es, "grow", [128, 512], F32)
            brow = A(es, "brow", [1, 512], F32)
            g_tk, b_tk, sc_tk, bm_tk = Tk(), Tk(), Tk(), Tk()
            for l in range(self.depth):
                self.load_T(es, stage, stk, self.vecs[:, l, 0, :].unsqueeze(2), W["norm1_g"][l:l + 1, :], 1, D)
                self.load_T(es, stage, stk, self.vecs[:, l, 1, :].unsqueeze(2), W["norm2_g"][l:l + 1, :], 1, D)
                for wi, nm in enumerate(("conv_c_b", "ln_c_g", "ln_c_b")):
                    self.load_T(es, stage, stk, self.cvec[:, l, wi, :].unsqueeze(2), W[nm][l:l + 1, :], 1, 384)
                self.load_T(es, stage, stk, self.wc[:, l, :, :], W["conv_c_w"][l], 31, 384)
                self.load_T(es, stage, stk, self.wf[:, l, :, :], W["conv_f_w"][l], 3, FF)
                self.load_T(es, stage, stk, self.bf[:, l, :].unsqueeze(2), W["conv_f_b"][l:l + 1, :], 1, FF)
            self.load_T(es, stage, stk, cT[:], self.c_d, ns, D)
            self.act(scT[:], cT[:], AF.Silu, (ctk,), (sc_tk,))
            self.cp("dve", screp[:], scT[:].unsqueeze(3).to_broadcast([128, 8, ns, 128]), (sc_tk,), (sc_tk,))
            for l in range(self.depth):
                self.load_T(es, stage, stk, bmT[:].unsqueeze(2), W["b_mod"][l:l + 1, :], 1, 6 * D)
                for f4 in range(12):
                    b = f4 % 4
                    self.dma(wst[b][:], W["w_mod"][l][:, f4 * 512:(f4 + 1) * 512].rearrange("(k p) c -> p k c", p=128),
                             (), (wst_tk[b],))
                    self.cp("act" if f4 % 2 else "dve", wm[:, :, f4 * 512:(f4 + 1) * 512], wst[b][:], (wst_tk[b],),
                            tuple(wm_tk[4 * f4:4 * f4 + 4]))
                ps, _, pt = self.ps()
                for fb in range(48):
                    for k in range(8):
                        self.mm(ps[:, fb * ns:(fb + 1) * ns], wm[:, k, fb * 128:(fb + 1) * 128], scT[:, k, :],
                                k == 0, k == 7, (wm_tk[fb], sc_tk), (pt,))
                self.tt("dve", self.modT[:, l, :, :], ps[:, 0:48 * ns].rearrange("p (f s) -> p f s", s=ns),
                        bmT[:].unsqueeze(2).to_broadcast([128, 48, ns]), ALU.add, (pt, ctk), (ctk,))
                for s in range(ns):
                    for gi, cb in ((0, 2 * D), (1, 5 * D)):
                        for hh in range(2):
                            ps, _, pt = self.ps()
                            c0 = cb + hh * 512
                            for k in range(8):
                                self.mm(ps[:, :], screp[:, k, s, :], wm[:, k, c0:c0 + 512], k == 0, k == 7,
                                        tuple(wm_tk[c0 // 128:c0 // 128 + 4]) + (sc_tk,), (pt,))
                            self.dma(brow[:], W["b_mod"][l:l + 1, c0:c0 + 512], (), (b_tk,))
                            self.tt("dve", grow[0:1, :], ps[0:1, :], brow[:], ALU.add, (pt, b_tk), (g_tk,))
                            row = (l * ns + s) * 2 + gi
                            self.dma(self.gs_d[row:row + 1, hh * 512:(hh + 1) * 512], grow[0:1, :], (g_tk,), ())
        self.s.barrier()

    def layer(self, l, xin, xout):
        ctk, abk = self.const_tk, self.ab_tk
        m = self.modT
        for s in range(self.nseq):
            ab = self.ab[:, s]
            self.ts("dve", ab[:, 0, :], m[:, l, 8:16, s], 1.0, None, ALU.add, None, (ctk,), (abk,))
            self.tt("dve", ab[:, 0, :], ab[:, 0, :], self.vecs[:, l, 0, :], ALU.mult, (abk, ctk), (abk,))
            self.cp("dve", ab[:, 1, :], m[:, l, 0:8, s], (ctk,), (abk,))
            self.ts("dve", ab[:, 2, :], m[:, l, 32:40, s], 1.0, None, ALU.add, None, (ctk,), (abk,))
            self.tt("dve", ab[:, 2, :], ab[:, 2, :], self.vecs[:, l, 1, :], ALU.mult, (abk, ctk), (abk,))
            self.cp("dve", ab[:, 3, :], m[:, l, 24:32, s], (ctk,), (abk,))
        self.phase1(l, xin)
        self.s.barrier()
        if self.debug == 1:
            return
        W = self.W
        with ExitStack() as esX:
            A = self.A
            sh = {}
            sh["wd"] = A(esX, "wd", [128, NFB, D], BF16)
            sh["wd_tk"] = [Tk() for _ in range(NFB)]
            with ExitStack() as esY:
                wo = A(esY, "wo", [128, 8, D], BF16)
                wo_tk = [Tk() for _ in range(8)]
                wst = [A(esY, "wst", [128, 512], F32) for _ in range(4)]
                wst_tk = [Tk() for _ in range(4)]
                diag = A(esY, "diag", [128, 93, 128], BF16)
                dg_tk = Tk()
                sh.update(wo=wo, wo_tk=wo_tk, diag=diag, dg_tk=dg_tk)

                def bg_weights(pieces):
                    def dma_(j):
                        src, dst, tk = pieces[j]
                        self.dma(wst[j % 4][:], src, (), (wst_tk[j % 4],))

                    def slot(j):
                        src, dst, tk = pieces[j]
                        self.cp("pool", dst, wst[j % 4][:], (wst_tk[j % 4],), (tk,))
                        if j + 4 < len(pieces):
                            dma_(j + 4)
                    for j in range(min(4, len(pieces))):
                        dma_(j)
                    return [lambda j=j: slot(j) for j in range(len(pieces))]

                pcs = [(W["w_out"][l][k * 128:(k + 1) * 128, hh * 512:(hh + 1) * 512], wo[:, k, hh * 512:(hh + 1) * 512], wo_tk[k])
                       for k in range(8) for hh in range(2)]
                self.bg = bg_weights(pcs)
                for cb in range(3):
                    for k in range(31):
                        self.bg.append(lambda cb=cb, k=k: self.ts("pool", diag[:, cb * 31 + k, :], self.identb[:], self.wc[:, l, cb, k:k + 1], None,
                                                                ALU.mult, None, (ctk,), (dg_tk,)))
                self.phase2a(l)
                self.bg_flush()
                self.s.barrier()
                if self.debug == 2:
                    return
                self.bg_flush()
                pcs = [(W["w_down"][l][fb * 128:(fb + 1) * 128, hh * 512:(hh + 1) * 512], sh["wd"][:, fb, hh * 512:(hh + 1) * 512], sh["wd_tk"][fb])
                       for fb in range(NFB) for hh in range(2)]
                self.bg = bg_weights(pcs)
                self.phase2c(l, xin, sh)
                self.bg_flush()
                self.s.barrier()
                if self.debug == 3:
                    return
            self.phase3(l, xout, sh)
            self.s.barrier()

    def rms_tile(self, xt, xt_tk, xn, xn_tk, junk, junk_tk, st, st_tk, P=128):
        self.act(junk[0:P, :], xt[0:P, :], AF.Square, (xt_tk,), (junk_tk, st_tk), scale=1.0 / 32.0, accum_out=st[0:P, 0:1])
        self.act(st[0:P, 1:2], st[0:P, 0:1], AF.Ln, (st_tk, self.const_tk), (st_tk,), bias=self.epsT[0:P, :])
        self.act(st[0:P, 2:3], st[0:P, 1:2], AF.Exp, (st_tk,), (st_tk,), scale=-0.5)
        self.act(xn[0:P, :], xt[0:P, :], AF.Copy, (xt_tk, st_tk), (xn_tk,), scale=st[0:P, 2:3])

    def phase1(self, l, xin):
        W = self.W
        ctk, abk = self.const_tk, self.ab_tk
        NCB = 4
        with ExitStack() as es:
            A = self.A
            wi = A(es, "wi", [128, 8, INW], BF16)
            wi_tk = [Tk() for _ in range(19)]
            wst = [A(es, "wst", [128, 4, 128], F32) for _ in range(4)]
            wst_tk = [Tk() for _ in range(4)]
            cct = A(es, "cct", [128, 32, 64], F32)
            sst = A(es, "sst", [128, 32, 64], F32)
            gt = A(es, "gt", [128, 18, 64], F32)
            gsm = A(es, "gsm", [128, 4, 64], F32)
            tab_tk = Tk()
            xt = [A(es, "xt", [128, D], F32) for _ in range(2)]
            xt_tk = [Tk(), Tk()]
            xn4 = [A(es, "xn4", [128, 4, D], F32)] * 2
            xn_tk = [[Tk() for _ in range(4)]] * 2
            junk = A(es, "junk", [128, D], BF16)
            junk_tk = Tk()
            st = [A(es, "st", [128, 4], F32) for _ in range(2)]
            st_tk = [Tk(), Tk()]
            hT = [A(es, "hT", [128, 8, 512], BF16) for _ in range(2)]
            hT_tk = [[Tk() for _ in range(8)] for _ in range(2)]
            qs = [A(es, "qs", [128, 18, 64], F32) for _ in range(NCB)]
            sq = [A(es, "sq", [128, 18, 64], F32) for _ in range(NCB)]
            msb = [A(es, "msb", [128, 18, 64], F32) for _ in range(NCB)]
            qf = [A(es, "qf", [128, 1152], BF16) for _ in range(NCB)]
            ssq = [A(es, "ssq", [128, 3, 18], F32) for _ in range(NCB)]
            qs_tk = [Tk() for _ in range(NCB)]
            sq_tk = [Tk() for _ in range(NCB)]
            ms_tk = [Tk() for _ in range(NCB)]
            qf_tk = [Tk() for _ in range(NCB)]
            ssq_tk = [Tk() for _ in range(NCB)]
            qkst = [A(es, "qkst", [128, 9, 512], BF16)] * 2
            qkst_tk = [Tk()] * 2
            vst = [A(es, "vst", [128, 8, 128], BF16) for _ in range(2)]
            vst_tk = [Tk(), Tk()]
            ust = [A(es, "ust", [128, 3, 512], BF16)] * 2
            ust_tk = [Tk()] * 2
            sg = [A(es, "sg", [128, 512], F32) for _ in range(2)]
            sg_tk = [Tk(), Tk()]

            groups = ((0, 512), (512, 1024), (1024, 1152), (1152, 1664))
            chunks = [(s, c) for s in range(self.nseq) for c in range(8)]
            cnt = {"x": 0, "ch": 0}

            def stNd(i, t4):
                s, c = chunks[i]
                c0 = c * 512
                b = t4 % 2
                self.dma(xt[b][:], xin[s][c0 + t4 * 128:c0 + (t4 + 1) * 128, :], (), (xt_tk[b],))

            def stNc(i, t4):
                X4, X4t = xn4[i % 2], xn_tk[i % 2]
                b = t4 % 2
                self.rms_tile(xt[b], xt_tk[b], X4[:, t4, :], X4t[t4], junk, junk_tk, st[b], st_tk[b])

            def stN(i):
                for t4 in range(4):
                    stNd(i, t4)
                    stNc(i, t4)

            def stT(i):
                s, c = chunks[i]
                ab = self.ab[:, s]
                X4, X4t = xn4[i % 2], xn_tk[i % 2]
                H, Ht = hT[i % 2], hT_tk[i % 2]
                for k in range(8):
                    ps, _, pt = self.ps()
                    for t4 in range(4):
                        self.tr(ps[:, t4 * 128:(t4 + 1) * 128], X4[:, t4, k * 128:(k + 1) * 128], self.ident[:], (X4t[t4], ctk), (pt,))
                    if k % 2 == 0:
                        self.ts("dve", H[:, k, :], ps[:, :], ab[:, 0, k:k + 1], ab[:, 1, k:k + 1], ALU.mult, ALU.add, (pt, abk), (Ht[k],))
                    else:
                        self.act(H[:, k, :], ps[:, :], AF.Identity, (pt, abk), (Ht[k],), scale=ab[:, 0, k:k + 1], bias=ab[:, 1, k:k + 1])

            def stA(i, t4):
                s, c = chunks[i]
                tile = c * 4 + t4
                gtile = i * 4 + t4
                H, Ht = hT[i % 2], hT_tk[i % 2]
                cb = cnt["ch"] % NCB
                cnt["ch"] += 1
                pss = []
                for (g0, g1) in groups:
                    ps, _, pt = self.ps()
                    pss.append((ps, pt))
                    for k in range(8):
                        self.mm(ps[:, 0:g1 - g0], H[:, k, t4 * 128:(t4 + 1) * 128], wi[:, k, g0:g1], k == 0, k == 7,
                                (Ht[k],) + tuple(wi_tk[g0 // 128:(g1 + 127) // 128]), (pt,))
                Q_, SQ, MS, QF, SS = qs[cb], sq[cb], msb[cb], qf[cb], ssq[cb]
                qsf = Q_[:].rearrange("p h d -> p (h d)")
                sqf = SQ[:].rearrange("p h d -> p (h d)")
                gtf = gt[:].rearrange("p h d -> p (h d)")
                for gi, (g0, g1) in enumerate(groups[:3]):
                    self.act(sqf[:, g0:g1], pss[gi][0][:, 0:g1 - g0], AF.Square, (pss[gi][1],), (sq_tk[cb],))
                    self.tt("dve", qsf[:, g0:g1], pss[gi][0][:, 0:g1 - g0], gtf[:, g0:g1], ALU.mult, (pss[gi][1], tab_tk), (qs_tk[cb],))
                vb = gtile % 2
                self.cp("act", vst[vb][:, :, 0:64], pss[3][0][:, :].rearrange("p (h d) -> p h d", d=64), (pss[3][1],), (vst_tk[vb],))
                self.dma(self.vs_d[s, tile * 128:(tile + 1) * 128, :, :], vst[vb][:], (vst_tk[vb],), ())
                self.s.op("dve", lambda e: e.tensor_reduce(out=SS[:, 0, :], in_=SQ[:], axis=AX.X, op=ALU.add), (sq_tk[cb],), (ssq_tk[cb],))
                self.act(SS[:, 1, :], SS[:, 0, :], AF.Ln, (ssq_tk[cb], ctk), (ssq_tk[cb],), scale=1.0 / 64, bias=self.epsT[:, :])
                self.act(SS[:, 2, :], SS[:, 1, :], AF.Exp, (ssq_tk[cb],), (ssq_tk[cb],), scale=-0.5)
                return cb

            def stA2(i, t4, cb):
                s, c = chunks[i]
                tile = c * 4 + t4
                Q_, SQ, MS, QF, SS = qs[cb], sq[cb], msb[cb], qf[cb], ssq[cb]
                self.tt("pool", SQ[:], Q_[:], cct[:, tile:tile + 1, :].to_broadcast([128, 18, 64]), ALU.mult,
                        (qs_tk[cb], tab_tk, ssq_tk[cb]), (sq_tk[cb],))
                self.tt("pool", MS[:], Q_[:], sst[:, tile:tile + 1, :].to_broadcast([128, 18, 64]), ALU.mult, (qs_tk[cb], tab_tk), (ms_tk[cb],))
                self.tt("pool", Q_[:, :, 0:32], SQ[:, :, 0:32], MS[:, :, 32:64], ALU.subtract, (sq_tk[cb], ms_tk[cb]), (qs_tk[cb],))
                self.tt("pool", Q_[:, :, 32:64], SQ[:, :, 32:64], MS[:, :, 0:32], ALU.add, (sq_tk[cb], ms_tk[cb]), (qs_tk[cb],))
                self.tt("dve", QF[:].rearrange("p (h d) -> p h d", d=64), Q_[:],
                        SS[:, 2, :].unsqueeze(2).to_broadcast([128, 18, 64]), ALU.mult, (qs_tk[cb], ssq_tk[cb]), (qf_tk[cb],))

            def stB(i, t4, cb):
                s, c = chunks[i]
                c0 = c * 512
                QF = qf[cb]
                KS, KSt = qkst[i % 2], qkst_tk[i % 2]
                ps, psb, pt = self.ps()
                ps2, psb2, pt2 = self.ps()
                for j in range(8):
                    self.tr(psb[:, j * 128:(j + 1) * 128], QF[:, j * 128:(j + 1) * 128], self.identb[:], (qf_tk[cb], ctk), (pt,))
                self.tr(psb2[:, 0:128], QF[:, 1024:1152], self.identb[:], (qf_tk[cb], ctk), (pt2,))
                self.cp("act", KS[:, 0:8, t4 * 128:(t4 + 1) * 128], psb[:, :].rearrange("p (i t) -> p i t", t=128), (pt,), (KSt,))
                self.cp("dve", KS[:, 8, t4 * 128:(t4 + 1) * 128], psb2[:, 0:128], (pt2,), (KSt,))
                if t4 == 3:
                    self.dma(self.qk_d[s, :, c0:c0 + 512].rearrange("(i p) t -> p i t", p=128), KS[:], (KSt,), ())

            def stG(i):
                s, c = chunks[i]
                c0 = c * 512
                H, Ht = hT[i % 2], hT_tk[i % 2]
                ub = i % 2
                for cb in range(3):
                    sb = (i * 3 + cb) % 2
                    ps, _, pt = self.ps()
                    g0 = (16 + cb) * 128
                    for k in range(8):
                        self.mm(ps[:, :], wi[:, k, g0:g0 + 128], H[:, k, :], k == 0, k == 7, (Ht[k], wi_tk[16 + cb]), (pt,))
                    ps2, _, pt2 = self.ps()
                    g0 = (13 + cb) * 128
                    for k in range(8):
                        self.mm(ps2[:, :], wi[:, k, g0:g0 + 128], H[:, k, :], k == 0, k == 7, (Ht[k], wi_tk[13 + cb]), (pt2,))
                    self.act(sg[sb][:], ps[:, :], AF.Exp, (pt,), (sg_tk[sb],), scale=-1.0)
                    self.act(sg[sb][:], sg[sb][:], AF.Ln, (sg_tk[sb],), (sg_tk[sb],), bias=1.0)
                    self.act(sg[sb][:], sg[sb][:], AF.Exp, (sg_tk[sb],), (sg_tk[sb],), scale=-1.0)
                    self.tt("dve", ust[ub][:, cb, :], ps2[:, :], sg[sb][:], ALU.mult, (pt2, sg_tk[sb]), (ust_tk[ub],))
                self.dma(self.u_d[s, :, c0:c0 + 512].rearrange("(i p) t -> p i t", p=128), ust[ub][:], (ust_tk[ub],), ())

            n = len(chunks)
            tiles = [(i, t4) for i in range(n) for t4 in range(4)]
            cbs = {}
            stN(0)
            self.dma(cct[:], self.cc_d.rearrange("(t p) c -> p t c", p=128), (), (tab_tk,))
            self.dma(sst[:], self.ss_d.rearrange("(t p) c -> p t c", p=128), (), (tab_tk,))
            for gi, nm in enumerate(("qn_a", "kn_a", "qn_b", "kn_b")):
                self.dma(gsm[:, gi, :], W[nm][l:l + 1, :].to_broadcast([128, 64]), (), (tab_tk,))
            for (h0, h1, gi) in ((0, 4, 0), (4, 6, 1), (6, 12, 2), (12, 18, 3)):
                self.cp("pool", gt[:, h0:h1, :], gsm[:, gi:gi + 1, :].to_broadcast([128, h1 - h0, 64]), (tab_tk,), (tab_tk,))
            for v in vst:
                self.memset("pool", v[:], 1.0, (vst_tk[0], vst_tk[1]))
            for blk in range(19):
                sc = SRC_COL[blk]
                for kh in range(2):
                    b = (blk * 2 + kh) % 4
                    self.dma(wst[b][:], W["w_in"][l][kh * 512:(kh + 1) * 512, sc:sc + 128].rearrange("(k p) c -> p k c", p=128),
                             (), (wst_tk[b],))
                    self.cp("act", wi[:, 4 * kh:4 * kh + 4, blk * 128:(blk + 1) * 128], wst[b][:], (wst_tk[b],), (wi_tk[blk],))

            stT(0)
            if n > 1:
                stNd(1, 0)
                stNd(1, 1)
            for u, (i, t4) in enumerate(tiles):
                cbs[u] = stA(i, t4)
                if i + 1 < n:
                    if t4 == 0:
                        stNc(i + 1, 0)
                        stNd(i + 1, 2)
                    elif t4 == 1:
                        stNc(i + 1, 1)
                        stNd(i + 1, 3)
                    elif t4 == 2:
                        stNc(i + 1, 2)
                        stNc(i + 1, 3)
                    else:
                        stT(i + 1)
                        if i + 2 < n:
                            stNd(i + 2, 0)
                            stNd(i + 2, 1)
                if u >= 1:
                    stA2(tiles[u - 1][0], tiles[u - 1][1], cbs[u - 1])
                if u >= 3:
                    stB(tiles[u - 3][0], tiles[u - 3][1], cbs[u - 3])
                if t4 == 3:
                    stG(i)
            nt = len(tiles)
            stA2(tiles[nt - 1][0], tiles[nt - 1][1], cbs[nt - 1])
            for u in range(max(0, nt - 3), nt):
                stB(tiles[u][0], tiles[u][1], cbs[u])

    def phase2a(self, l):
        W = self.W
        ctk = self.const_tk
        with ExitStack() as es:
            A = self.A
            kbuf = [A(es, "kbuf", [128, PADK + S + PADK], BF16) for _ in range(2)]
            qbuf = [A(es, "qbuf", [128, S], BF16) for _ in range(2)]
            kq_tk = [Tk(), Tk()]
            kqp_tk = [[Tk() for _ in range(24)] for _ in range(2)]
            accs = [A(es, "acc", [128, 2048], F32) for _ in range(4)]
            accs_tk = [Tk() for _ in range(4)]
            lnd = A(es, "lnd", [128, 2048], F32)
            rcp = lnd
            fin_tk = Tk()
            mst = [A(es, "mst", [128, 2048], BF16) for _ in range(2)]
            mst_tk = [Tk(), Tk()]
            NP = 8
            pT = [A(es, "pT", [128, 512], BF16) for _ in range(NP)]
            pT_tk = [Tk() for _ in range(NP)]
            NV = 12
            vt = [A(es, "vt", [128, 2, 128], BF16) for _ in range(NV)]
            vt_tk = [Tk() for _ in range(NV)]
            sk = A(es, "sk", [128, 4], F32)
            sk_tk = Tk()
            for b in range(2):
                self.memset("dve", kbuf[b][:, 0:PADK], 0.0, (kq_tk[b],))
                self.memset("dve", kbuf[b][:, PADK + S:], 0.0, (kq_tk[b],))
            for v in range(NV):
                self.memset("dve", vt[v][:], 0.0, (vt_tk[v],))
            self.dma(sk[:], W["sink_a"][l:l + 1, :].to_broadcast([128, 4]), (), (sk_tk,))
            self.act(sk[:], sk[:], AF.Exp, (sk_tk,), (sk_tk,))
            pcount = 0
            vcount = 0
            pend = []
            finq = []
            pgs0 = []
            for p in range(2):
                pgs0.append(dict(q0=128 * p, k=[(256 + 64 * p, 0), (256 + 64 * p, 64)], vh=p, nvh=1, a=True, o0=128 * p, hq=2 * p))
            for p in range(3):
                pgs0.append(dict(q0=384 + 128 * p, k=[(768 + 128 * p, 0), (768 + 128 * p + 64, 64)], vh=2 + 2 * p, nvh=2,
                                 a=False, o0=256 + 128 * p, hq=0))
            pgs = [dict(pg, s=s) for s in range(self.nseq) for pg in pgs0]

            def load_kq_pieces(pi):
                pg_ = pgs[pi]
                kb_ = pi % 2
                qk = self.qk_d[pg_["s"]]
                pcs = []
                for c8 in range(8):
                    cs = slice(c8 * 512, (c8 + 1) * 512)
                    pcs.append(lambda cs=cs, c8=c8: self.dma(qbuf[kb_][:, cs], qk[pg_["q0"]:pg_["q0"] + 128, cs], (), (kqp_tk[kb_][3 * c8],)))
                    for ki, (r0, dp0) in enumerate(pg_["k"]):
                        pcs.append(lambda cs=cs, r0=r0, dp0=dp0, c8=c8, ki=ki: self.dma(
                            kbuf[kb_][dp0:dp0 + 64, PADK + c8 * 512:PADK + (c8 + 1) * 512], qk[r0:r0 + 64, cs], (),
                            (kqp_tk[kb_][3 * c8 + 1 + ki],)))
                return pcs

            kqq = load_kq_pieces(0)
            fcount = 0
            for pi, pg in enumerate(pgs):
                kb = pi % 2
                K, Q, kqt = kbuf[kb], qbuf[kb], (kq_tk[kb],) + tuple(kqp_tk[kb])
                vs_d = self.vs_d[pg["s"]]
                while kqq:
                    kqq.pop(0)()
                if pi + 1 < len(pgs):
                    kqq = load_kq_pieces(pi + 1)
                pats = [(1, 3)] if pg["a"] else [(1, 2), (4, 2), (16, 2)]
                for ST in range(2):
                    t0 = ST * 2048
                    par = (pi * 2 + ST) % 2
                    acc = accs[2 * par:2 * par + 2]
                    acc_tk = accs_tk[2 * par:2 * par + 2]
                    for pidx, (d, nk) in enumerate(pats):
                        Ls = S // d
                        nqb = 2048 // (128 * d)
                        nblk = Ls // 128
                        cache = {}
                        pv = [None, None]
                        for b in range(16):
                            r, jj = divmod(b, nqb)
                            j = ST * nqb + jj
                            s0 = 128 * j
                            g, pb = divmod(b, 4)
                            if pb == 0:
                                pv = [self.psx(0, 4), self.psx(0, 4)]
                            if pg["a"]:
                                ms = [m for m in (j - 1, j, j + 1) if 0 <= m < nblk]
                                if j == 0:
                                    mask = self.maskA[:, 128:384]
                                elif j == nblk - 1:
                                    mask = self.maskA[:, 0:256]
                                else:
                                    mask = self.maskA[:, :]
                                koff = 0
                            else:
                                ms = [j, j + 1]
                                mask = None
                                koff = 64
                            nkk = len(ms)
                            slots = []
                            for m in ms:
                                key = (r, m)
                                if key not in cache:
                                    sl = vcount % NV
                                    vcount += 1
                                    ks = 128 * m - koff
                                    lo, hi = max(ks, 0), min(ks + 128, Ls)
                                    src = vs_d[:, pg["vh"]:pg["vh"] + pg["nvh"], :]
                                    tok0 = r + d * lo
                                    nrow = hi - lo
                                    if d == 1:
                                        srcv = src[tok0:tok0 + nrow]
                                    else:
                                        srcv = src[tok0:tok0 + d * (nrow - 1) + 1:d]
                                    self.dma(vt[sl][lo - ks:hi - ks, 0:pg["nvh"], :], srcv, (), (vt_tk[sl],))
                                    cache[key] = sl
                                slots.append(cache[key])
                            self.bg_step(1)
                            heads_iter = [(0,), (1,)] if pg["a"] else [(0, 1)]
                            if not pg["a"]:
                                mask = self.maskB2[1 if j == 0 else (2 if j == nblk - 1 else 0)][:, :]
                            for hs_ in heads_iter:
                                pi_ = pcount % NP
                                pcount += 1
                                mask_eng = "pool" if pcount % 3 == 0 else "dve"

                                def stA(hs_=hs_, r=r, s0=s0, ms=ms, nkk=nkk, mask=mask, koff=koff, pi_=pi_, d=d, mask_eng=mask_eng,
                                        K=K, Q=Q, kqt=kqt):
                                    W_ = nkk * 128
                                    qc0 = r + d * s0
                                    tot = len(hs_) * W_
                                    P, Pt = pT[pi_], pT_tk[pi_]
                                    if len(hs_) == 1:
                                        ps, _, pt = self.psx(4, 8)
                                        banks = [(ps, pt)]
                                        pin = ps[:, 0:tot]
                                        pout = P[:, 0:tot]
                                    else:
                                        c2 = self.psc.get("st2", 0)
                                        self.psc["st2"] = c2 + 1
                                        b0 = 4 + 2 * (c2 % 2)
                                        banks = [(self.PS[b0], self.PST[b0]), (self.PS[b0 + 1], self.PST[b0 + 1])]
                                        pin = self.psall[:, b0 * 512:(b0 + 2) * 512].rearrange("p (b c) -> p b c", c=512)[:, :, 0:W_]
                                        pout = P[:, 0:tot].rearrange("p (b c) -> p b c", c=W_)
                                    for ci, m in enumerate(ms):
                                        for idx, hi_ in enumerate(hs_):
                                            ps, pt = banks[idx]
                                            rows = slice(64 * hi_, 64 * hi_ + 64)
                                            qsl = Q[rows, qc0:qc0 + d * 127 + 1:d] if d > 1 else Q[rows, qc0:qc0 + 128]
                                            kc0 = PADK + r + d * (128 * m - koff)
                                            ksl = K[rows, kc0:kc0 + d * 127 + 1:d] if d > 1 else K[rows, kc0:kc0 + 128]
                                            self.mm(ps[:, ci * 128:(ci + 1) * 128], ksl, qsl, True, True, kqt, (pt,))
                                    pts = tuple(b_[1] for b_ in banks)
                                    self.act(pout, pin, AF.Exp, pts, (Pt,), scale=0.125)
                                    self.tt(mask_eng, P[:, 0:tot], P[:, 0:tot], mask, ALU.mult, (Pt, ctk), (Pt,))

                                def stB(hs_=hs_, ms=ms, nkk=nkk, slots=slots, pi_=pi_, pv=pv, pb=pb, g=g, d=d, pidx=pidx, a=pg["a"],
                                        acc=acc, acc_tk=acc_tk):
                                    P, Pt = pT[pi_], pT_tk[pi_]
                                    W_ = nkk * 128
                                    for idx, hi_ in enumerate(hs_):
                                        pvp, _, pvt = pv[hi_]
                                        for ci, m in enumerate(ms):
                                            sl = slots[ci]
                                            hsel = 0 if a else hi_
                                            self.mm(pvp[:, pb * 128:(pb + 1) * 128], vt[sl][:, hsel, :],
                                                    P[:, idx * W_ + ci * 128:idx * W_ + (ci + 1) * 128],
                                                    ci == 0, ci == nkk - 1, (vt_tk[sl], Pt), (pvt,))
                                    if pb == 3:
                                        for hi_ in hs_:
                                            pvp, _, pvt = pv[hi_]
                                            a_ = acc[hi_]
                                            if d == 1:
                                                dst = a_[:, 512 * g:512 * g + 512]
                                                srcp = pvp[:, :]
                                            elif d == 4:
                                                dst = a_[:, :].rearrange("p (x r) -> p r x", r=4)[:, g, :]
                                                srcp = pvp[:, :]
                                            else:
                                                dst = a_[:, :].rearrange("p (i r) -> p r i", r=16)[:, 4 * g:4 * g + 4, :]
                                                srcp = pvp[:, :].rearrange("p (r i) -> p r i", i=128)
                                            if pidx == 0:
                                                self.cp("dve" if hi_ == 0 else "act", dst, srcp, (pvt,), (acc_tk[hi_],))
                                            else:
                                                self.tt("dve", dst, srcp, dst, ALU.add, (pvt, acc_tk[hi_]), (acc_tk[hi_],))

                                stA()
                                pend.append(stB)
                                while len(pend) > 4:
                                    pend.pop(0)()
                                if finq:
                                    finq.pop(0)()
                                if kqq:
                                    kqq.pop(0)()
                    mb = fcount % 2
                    fcount += 1

                    def mkfin(pg=pg, acc=acc, acc_tk=acc_tk, mb=mb, t0=t0):
                        pieces = []
                        for hi_ in range(2):
                            a_ = acc[hi_]
                            for q4 in range(4):
                                cs = slice(512 * q4, 512 * q4 + 512)

                                def p_ln(a_=a_, hi_=hi_, cs=cs):
                                    if pg["a"]:
                                        hq = pg["hq"] + hi_
                                        self.act(lnd[64:128, cs], a_[64:128, cs], AF.Ln, (acc_tk[hi_], sk_tk), (fin_tk,), bias=sk[64:128, hq:hq + 1])
                                    else:
                                        self.act(lnd[64:128, cs], a_[64:128, cs], AF.Ln, (acc_tk[hi_],), (fin_tk,))

                                def p_exp(cs=cs):
                                    self.act(rcp[0:64, cs], lnd[64:128, cs], AF.Exp, (fin_tk,), (fin_tk,), scale=-1.0)

                                def p_mul(a_=a_, hi_=hi_, cs=cs):
                                    self.tt("dve", mst[mb][64 * hi_:64 * hi_ + 64, cs], a_[0:64, cs], rcp[0:64, cs], ALU.mult,
                                            (acc_tk[hi_], fin_tk), (mst_tk[mb],))
                                pieces += [p_ln, p_exp, p_mul]
                        pieces.append(lambda: self.dma(self.mx_d[pg["s"], pg["o0"]:pg["o0"] + 128, t0:t0 + 2048], mst[mb][:], (mst_tk[mb],), ()))
                        return pieces

                    pend.append(lambda pcs=mkfin(): finq.extend(pcs))
            while pend:
                pend.pop(0)()
            while finq:
                finq.pop(0)()
            while pend:
                pend.pop(0)()

    def phase2c(self, l, xin, sh):
        W = self.W
        ctk = self.const_tk
        with ExitStack() as es:
            A = self.A
            wo, wo_tk, diag, dg_tk = sh["wo"], sh["wo_tk"], sh["diag"], sh["dg_tk"]
            g1b = [A(es, "g1b", [128, D], F32) for _ in range(2)]
            g_tk = [Tk(), Tk()]
            ub = [A(es, "ub", [128, 3, 542], BF16) for _ in range(3)]
            ub_tk = [Tk() for _ in range(3)]
            mix = [A(es, "mix", [128, 8, 512], BF16) for _ in range(4)]
            mixa_tk = [Tk() for _ in range(4)]
            mixc_tk = [Tk() for _ in range(4)]
            ysb = [A(es, "ysb", [128, 3, 512], F32) for _ in range(2)]
            ysb_tk = [Tk(), Tk()]
            ytok = [A(es, "ytok", [128, 384], F32) for _ in range(2)]
            ytok_tk = [Tk(), Tk()]
            yn4 = A(es, "yn4", [128, 4, 384], F32)
            yn_tk = [Tk() for _ in range(4)]
            bst = [A(es, "bst", [128, 12], F32) for _ in range(2)]
            bst_tk = [Tk(), Tk()]
            xt = [A(es, "xt", [128, D], F32) for _ in range(4)]
            xt_tk = [Tk() for _ in range(4)]
            tmpo = [A(es, "tmpo", [128, 512], F32) for _ in range(2)]
            tmpo_tk = [Tk(), Tk()]
            junk = A(es, "junk", [128, 384], BF16)
            junk_tk = Tk()
            for s in range(self.nseq):
                row = (l * self.nseq + s) * 2
                if s < 2:
                    self.dma(g1b[s % 2][:], self.gs_d[row:row + 1, :].to_broadcast([128, D]), (), (g_tk[s % 2],))
            for b in range(3):
                self.memset("pool", ub[b][:], 0.0, (ub_tk[b],))
            chunks = [(s, c) for s in range(self.nseq) for c in range(8)]
            cnt = {"x": 0, "t": 0}

            def stL(i):
                s, c = chunks[i]
                c0 = c * 512
                b = i % 3
                mb = i % 4
                lo, hi = max(c0 - 15, 0), min(c0 + 527, S)
                if c == 7:
                    self.memset("pool", ub[b][:, :, 527:542], 0.0, (ub_tk[b],))
                if c == 0:
                    self.memset("pool", ub[b][:, :, 0:15], 0.0, (ub_tk[b],))
                self.dma(ub[b][:, :, lo - (c0 - 15):hi - (c0 - 15)], self.u_d[s, :, lo:hi].rearrange("(i p) t -> p i t", p=128),
                         (), (ub_tk[b],))
                self.dma(mix[mb][:, 0:5, :], self.mx_d[s, :, c0:c0 + 512].rearrange("(i p) t -> p i t", p=128), (), (mixa_tk[mb],))

            def st1(i):
                b = i % 3
                yb = i % 2
                for cb in range(3):
                    ps, _, pt = self.ps()
                    for k in range(31):
                        self.mm(ps[:, :], diag[:, cb * 31 + k, :], ub[b][:, cb, k:k + 512], k == 0, k == 30, (dg_tk, ub_tk[b]), (pt,))
                    self.act(ysb[yb][:, cb, :], ps[:, :], AF.Identity, (pt, ctk), (ysb_tk[yb],), bias=self.cvec[:, l, 0, cb:cb + 1])

            def st2(i):
                b = i % 2
                for t4 in range(4):
                    tb = cnt["t"] % 2
                    cnt["t"] += 1
                    ps, _, pt = self.ps()
                    for cb in range(3):
                        self.tr(ps[:, cb * 128:(cb + 1) * 128], ysb[b][:, cb, t4 * 128:(t4 + 1) * 128], self.ident[:], (ysb_tk[b], ctk), (pt,))
                    B = bst[tb]
                    Y = ytok[tb]
                    self.act(Y[:], ps[:, 0:384], AF.Copy, (pt,), (ytok_tk[tb], bst_tk[tb]), accum_out=B[:, 0:1])
                    self.ts("dve", B[:, 1:2], B[:, 0:1], -1.0 / 384.0, None, ALU.mult, None, (bst_tk[tb],), (bst_tk[tb],))
                    self.ts("dve", Y[:], Y[:], B[:, 1:2], None, ALU.add, None, (ytok_tk[tb], bst_tk[tb]), (ytok_tk[tb],))
                    self.act(junk[:], Y[:], AF.Square, (ytok_tk[tb],), (junk_tk, bst_tk[tb]), scale=1.0 / math.sqrt(384.0), accum_out=B[:, 2:3])
                    self.ts("pool", B[:, 3:4], B[:, 2:3], EPS, None, ALU.add, None, (bst_tk[tb],), (bst_tk[tb],))
                    self.tt("pool", B[:, 4:5], B[:, 3:4], self.mhalf[:, :], ALU.pow, (bst_tk[tb], ctk), (bst_tk[tb],))
                    self.act(yn4[:, t4, :], Y[:], AF.Copy, (ytok_tk[tb], bst_tk[tb]), (yn_tk[t4],), scale=B[:, 4:5])

            def st3(i):
                mb = i % 4
                for cb in range(3):
                    ps, _, pt = self.ps()
                    for t4 in range(4):
                        self.tr(ps[:, t4 * 128:(t4 + 1) * 128], yn4[:, t4, cb * 128:(cb + 1) * 128], self.ident[:], (yn_tk[t4], ctk), (pt,))
                    self.act(mix[mb][:, 5 + cb, :], ps[:, :], AF.Silu, (pt, ctk), (mixc_tk[mb],),
                             scale=self.cvec[:, l, 1, cb:cb + 1], bias=self.cvec[:, l, 2, cb:cb + 1])

            def stXl(i):
                s, c = chunks[i]
                for t4 in range(4):
                    r0 = c * 512 + t4 * 128
                    self.dma(xt[t4][:], xin[s][r0:r0 + 128, :], (), (xt_tk[t4],))

            def st4(i):
                s, c = chunks[i]
                c0 = c * 512
                mb = i % 4
                if c == 0 and s >= 2:
                    row = (l * self.nseq + s) * 2
                    self.dma(g1b[s % 2][:], self.gs_d[row:row + 1, :].to_broadcast([128, D]), (), (g_tk[s % 2],))
                G, Gt = g1b[s % 2], g_tk[s % 2]
                for t4 in range(4):
                    xb_ = t4
                    r0 = c0 + t4 * 128
                    for hh in range(2):
                        ps, _, pt = self.ps()
                        for k in range(8):
                            self.mm(ps[:, :], mix[mb][:, k, t4 * 128:(t4 + 1) * 128], wo[:, k, hh * 512:(hh + 1) * 512], k == 0, k == 7,
                                    (mixa_tk[mb], mixc_tk[mb], wo_tk[k]), (pt,))
                        self.tt("dve", tmpo[hh][:], ps[:, :], G[:, hh * 512:(hh + 1) * 512], ALU.mult, (pt, Gt), (tmpo_tk[hh],))
                        self.tt("pool", xt[xb_][:, hh * 512:(hh + 1) * 512], tmpo[hh][:], xt[xb_][:, hh * 512:(hh + 1) * 512], ALU.add,
                                (tmpo_tk[hh], xt_tk[xb_]), (xt_tk[xb_],))
                    self.dma(self.xa_d[s, r0:r0 + 128, :], xt[xb_][:], (xt_tk[xb_],), ())

            n = len(chunks)
            stL(0)
            if n > 1:
                stL(1)
            st1(0)
            for i in range(n):
                if i + 2 < n:
                    stL(i + 2)
                self.bg_step(2)
                if i + 1 < n:
                    st1(i + 1)
                self.bg_step(2)
                if i >= 1:
                    stXl(i - 1)
                st2(i)
                if i >= 1:
                    st4(i - 1)
                st3(i)
            stXl(n - 1)
            st4(n - 1)

    def phase3(self, l, xout, sh):
        W = self.W
        ctk, abk = self.const_tk, self.ab_tk
        CH = 256
        NB = 4
        with ExitStack() as es:
            A = self.A
            wu = A(es, "wu", [128, 8, 2 * FF], BF16)
            wu_tk = [Tk() for _ in range(2 * NFB)]
            wd, wd_tk = sh["wd"], sh["wd_tk"]
            g2b = A(es, "g2b", [128, D], F32)
            g_tk = Tk()
            xt = [A(es, "xt", [128, D], F32) for _ in range(3)]
            xt_tk = [Tk() for _ in range(3)]
            junk = A(es, "junk", [128, D], BF16)
            junk_tk = Tk()
            st = [A(es, "st", [128, 4], F32) for _ in range(3)]
            st_tk = [Tk() for _ in range(3)]
            h2 = [A(es, "h2", [128, 8, CH + 2], BF16) for _ in range(2)]
            h2_tk = [Tk(), Tk()]
            hal = A(es, "hal", [128, 16], F32)
            hal_tk = Tk()
            t0b = [A(es, "t0b", [128, CH], F32) for _ in range(NB)]
            t0_tk = [Tk() for _ in range(NB)]
            aT = A(es, "aT", [128, NFB, CH], BF16)
            aT_tk = [Tk() for _ in range(NFB)]
            xr = [A(es, "xr", [128, D], F32) for _ in range(2)]
            xr_tk = [Tk(), Tk()]
            tmpo = [A(es, "tmpo", [128, 512], F32) for _ in range(2)]
            tmpo_tk = [Tk(), Tk()]
            wst = list(xr) + [A(es, "wst", [128, D], F32) for _ in range(2)]
            wst_tk = list(xr_tk) + [Tk(), Tk()]
            row = (l * self.nseq) * 2 + 1
            self.dma(g2b[:], self.gs_d[row:row + 1, :].to_broadcast([128, D]), (), (g_tk,))
            nch = S // CH
            chunks = [(s, c) for s in range(self.nseq) for c in range(nch)]
            cnt = {"b": 0, "r": 0}

            def stPNd(i):
                s, c = chunks[i]
                c0 = c * CH
                xa = self.xa_d[s]
                for ti in range(2):
                    r0 = c0 + ti * 128
                    self.dma(xt[ti][:], xa[r0:r0 + 128, :], (), (xt_tk[ti],))
                has_l, has_r = c > 0, c < nch - 1
                if has_l and has_r:
                    self.dma(xt[2][0:2, :], xa[c0 - 1:c0 + CH + 1:CH + 1, :], (), (xt_tk[2],))
                elif has_r:
                    self.dma(xt[2][0:2, :], xa[c0 + CH - 1:c0 + CH + 1, :], (), (xt_tk[2],))
                else:
                    self.dma(xt[2][0:2, :], xa[c0 - 1:c0 + 1, :], (), (xt_tk[2],))

            def stPNc_ops(i):
                ops = []
                for ti in range(3):
                    P = 128 if ti < 2 else 2
                    X, Xt, ST, STt = xt[ti], xt_tk[ti], st[ti], st_tk[ti]
                    ops.append(lambda X=X, Xt=Xt, ST=ST, STt=STt, P=P: self.act(junk[0:P, :], X[0:P, :], AF.Square, (Xt,), (junk_tk, STt),
                                                                               scale=1.0 / 32.0, accum_out=ST[0:P, 0:1]))
                    ops.append(lambda ST=ST, STt=STt, P=P: self.ts("pool", ST[0:P, 1:2], ST[0:P, 0:1], EPS, None, ALU.add, None, (STt,), (STt,)))
                    ops.append(lambda ST=ST, STt=STt, P=P: self.tt("pool", ST[0:P, 2:3], ST[0:P, 1:2], self.mhalf[0:P, :], ALU.pow, (STt, ctk), (STt,)))
                    ops.append(lambda X=X, Xt=Xt, ST=ST, STt=STt, P=P: self.act(X[0:P, :], X[0:P, :], AF.Copy, (Xt, STt), (Xt,), scale=ST[0:P, 2:3]))
                return ops

            def stPNc(i):
                for o in stPNc_ops(i):
                    o()

            def stPT(i):
                s, c = chunks[i]
                ab = self.ab[:, s]
                H, Ht = h2[i % 2], h2_tk[i % 2]
                for ti, colo in ((0, 1), (1, 129)):
                    for half in range(2):
                        ps, _, pt = self.psx(6, 8)
                        for kk in range(4):
                            k = half * 4 + kk
                            self.tr(ps[:, kk * 128:(kk + 1) * 128], xt[ti][:, k * 128:(k + 1) * 128], self.ident[:], (xt_tk[ti], ctk), (pt,))
                        for kk in range(4):
                            k = half * 4 + kk
                            self.act(H[:, k, colo:colo + 128], ps[:, kk * 128:(kk + 1) * 128], AF.Identity, (pt, abk), (Ht,),
                                     scale=ab[:, 2, k:k + 1], bias=ab[:, 3, k:k + 1])
                ps, _, pt = self.psx(6, 8)
                for k in range(8):
                    self.tr(ps[:, k * 2:k * 2 + 2], xt[2][0:2, k * 128:(k + 1) * 128], self.ident[0:2, 0:2], (xt_tk[2], ctk), (pt,))
                hv3 = hal[:, 0:16].rearrange("p (k t) -> p k t", t=2)
                self.tt("dve", hv3, ps[:, 0:16].rearrange("p (k t) -> p k t", t=2),
                        ab[:, 2, :].unsqueeze(2).to_broadcast([128, 8, 2]), ALU.mult, (pt, abk), (hal_tk,))
                self.tt("dve", H[:, :, 0:CH + 2:CH + 1], hv3, ab[:, 3, :].unsqueeze(2).to_broadcast([128, 8, 2]), ALU.add, (hal_tk, abk), (Ht,))
                if c == 0:
                    self.memset("dve", H[:, :, 0:1], 0.0, (Ht,))
                if c == nch - 1:
                    self.memset("dve", H[:, :, CH + 1:CH + 2], 0.0, (Ht,))

            def stA(i, fb):
                H, Ht = h2[i % 2], h2_tk[i % 2]
                bb = cnt["b"] % NB
                cnt["b"] += 1
                psg, _, ptg = self.psx(0, 6)
                for k in range(8):
                    self.mm(psg[:, 0:CH + 2], wu[:, k, fb * 128:(fb + 1) * 128], H[:, k, :], k == 0, k == 7, (Ht, wu_tk[fb]), (ptg,))
                psu, _, ptu = self.psx(0, 6)
                for k in range(8):
                    self.mm(psu[:, 0:CH], wu[:, k, FF + fb * 128:FF + (fb + 1) * 128], H[:, k, 1:CH + 1], k == 0, k == 7,
                            (Ht, wu_tk[NFB + fb]), (ptu,))
                T0, T0t = t0b[bb], t0_tk[bb]
                self.act(T0[:], psg[:, 0:CH], AF.Identity, (ptg, ctk), (T0t,), scale=self.wf[:, l, fb, 0:1], bias=self.bf[:, l, fb:fb + 1])
                self.stt(T0[:], psg[:, 1:CH + 1], self.wf[:, l, fb, 1:2], T0[:], ALU.mult, ALU.add, (ptg, T0t, ctk), (T0t,))
                self.stt(T0[:], psg[:, 2:CH + 2], self.wf[:, l, fb, 2:3], T0[:], ALU.mult, ALU.add, (ptg, T0t, ctk), (T0t,))
                return (bb, psu, ptu)

            def stB(i, fb, info):
                bb, psu, ptu = info
                self.act(t0b[bb][:], t0b[bb][:], AF.Silu, (t0_tk[bb],), (t0_tk[bb],))
                self.tt("dve", aT[:, fb, :], psu[:, 0:CH], t0b[bb][:], ALU.mult, (ptu, t0_tk[bb]), (aT_tk[fb],))

            def stDl(i):
                s, c = chunks[i]
                for t2 in range(CH // 128):
                    r0 = c * CH + t2 * 128
                    self.dma(xr[t2][:], self.xa_d[s, r0:r0 + 128, :], (), (xr_tk[t2],))

            def stD(i):
                s, c = chunks[i]
                c0 = c * CH
                if c == 0 and s >= 1:
                    row = (l * self.nseq + s) * 2 + 1
                    self.dma(g2b[:], self.gs_d[row:row + 1, :].to_broadcast([128, D]), (), (g_tk,))
                for t2 in range(CH // 128):
                    rb = t2
                    r0 = c0 + t2 * 128
                    for hh in range(2):
                        ps, _, pt = self.psx(6, 8)
                        for fb in range(NFB):
                            self.mm(ps[:, :], aT[:, fb, t2 * 128:(t2 + 1) * 128], wd[:, fb, hh * 512:(hh + 1) * 512], fb == 0, fb == NFB - 1,
                                    (aT_tk[fb], wd_tk[fb]), (pt,))
                        self.tt("dve", tmpo[hh][:], ps[:, :], g2b[:, hh * 512:(hh + 1) * 512], ALU.mult, (pt, g_tk), (tmpo_tk[hh],))
                        self.tt("pool", xr[rb][:, hh * 512:(hh + 1) * 512], tmpo[hh][:], xr[rb][:, hh * 512:(hh + 1) * 512], ALU.add,
                                (tmpo_tk[hh], xr_tk[rb]), (xr_tk[rb],))
                    self.dma(xout[s][r0:r0 + 128, :], xr[rb][:], (xr_tk[rb],), ())

            n = len(chunks)
            pend = []
            stPNd(0)
            stPNc(0)
            stPT(0)
            pc = 0
            for fp in range(NFB // 2):
                for part in range(2):
                    for kh in range(2):
                        b = pc % 4
                        pc += 1
                        col = part * FF + fp * 256
                        self.dma(wst[b][:].rearrange("p (k c) -> p k c", c=256),
                                 W["w_up"][l][kh * 512:(kh + 1) * 512, col:col + 256].rearrange("(k p) c -> p k c", p=128), (), (wst_tk[b],))
                        self.cp("act" if pc % 2 else "pool", wu[:, 4 * kh:4 * kh + 4, col:col + 256],
                                wst[b][:].rearrange("p (k c) -> p k c", c=256), (wst_tk[b],),
                                (wu_tk[part * NFB + 2 * fp], wu_tk[part * NFB + 2 * fp + 1]))
            for i in range(n):
                for fb in range(NFB):
                    info = stA(i, fb)
                    pend.append(lambda i=i, fb=fb, info=info: stB(i, fb, info))
                    if fb == NFB - 1:
                        pend.append(lambda i=i: stD(i))
                    while len(pend) > 2:
                        pend.pop(0)()
                    if fb == 12:
                        stDl(i)
                    if i + 1 < n:
                        if fb == 1:
                            stPNd(i + 1)
                            pnops = stPNc_ops(i + 1)
                        if 5 <= fb < 17:
                            pnops[fb - 5]()
                        if fb == NFB - 1:
                            stPT(i + 1)
            while pend:
                pend.pop(0)()


def sch_emit_engine(sch, eng, h, sems, dsems):
    if not getattr(sch, "_prepared", False):
        for e in sch.ENGS:
            c = 0
            for ent in sch.q[e]:
                if ent[2] is True:
                    c += 1
                ent[3] = c
        sch._prepared = True
    seen = {}
    for fn, deps, sig, _ in sch.q[eng]:
        for k, v in deps.items():
            if isinstance(k, tuple):
                sem, tgt = dsems[k[1]], v
            else:
                ent = sch.q[k][v]
                sem, tgt = sems[k], ent[3]
            if seen.get(k, 0) >= tgt:
                continue
            seen[k] = tgt
            h.wait_ge(sem, tgt)
        ins = fn(h)
        if sig is True:
            ins.then_inc(sems[eng], 1)
        elif isinstance(sig, tuple):
            ins.then_inc(dsems[sig[1]], 16)
    for k, v in sch.pending[eng].items():
        if isinstance(k, tuple):
            sem, tgt = dsems[k[1]], v
        else:
            sem, tgt = sems[k], sch.q[k][v][3]
        if seen.get(k, 0) >= tgt:
            continue
        h.wait_ge(sem, tgt)


def rope_tables():
    half = 32
    inv = (1.0 / (np.float32(10000.0) ** (np.arange(half, dtype=np.float32) / np.float32(half)))).astype(np.float32)
    ang = (np.arange(S, dtype=np.float32)[:, None] * inv[None, :]).astype(np.float32)
    c = np.cos(ang).astype(np.float32)
    s_ = np.sin(ang).astype(np.float32)
    return np.concatenate([c, c], axis=1), np.concatenate([s_, s_], axis=1)


_CACHE = {}


def kernel(**inputs):
    f = lambda a: np.ascontiguousarray(np.asarray(a), dtype=np.float32)
    xp, xs = f(inputs["x_prompt"]), f(inputs["x_sample"])
    cp, cs = f(inputs["c_prompt"]), f(inputs["c_sample"])
    wnames = ["norm1_g", "norm2_g", "w_mod", "b_mod", "w_in", "qn_a", "kn_a", "sink_a", "qn_b", "kn_b", "conv_c_w",
              "conv_c_b", "ln_c_g", "ln_c_b", "w_out", "w_up", "conv_f_w", "conv_f_b", "w_down"]
    wts = {n: f(inputs[n]) for n in wnames}
    cc, ss = rope_tables()
    if "nc" not in _CACHE:
        _CACHE["nc"] = Prog().build()
    nc = _CACHE["nc"]
    in_maps = []
    for i in range(8):
        m = dict(wts)
        m["x"] = np.ascontiguousarray(np.stack([xp[i], xs[2 * i], xs[2 * i + 1]]))
        m["c"] = np.ascontiguousarray(np.stack([cp[i], cs[2 * i], cs[2 * i + 1]]))
        m["ropec"] = cc
        m["ropes"] = ss
        in_maps.append(m)
    res = run_bass_kernel_spmd(nc, in_maps, core_ids=list(range(8)))
    yp = np.empty_like(xp)
    ys = np.empty_like(xs)
    for i in range(8):
        y = np.asarray(res.results[i]["y"], dtype=np.float32)
        yp[i] = y[0]
        ys[2 * i] = y[1]
        ys[2 * i + 1] = y[2]
    return (yp, ys)
```

```python
import math
from contextlib import ExitStack
import numpy as np
import concourse.bass as bass
import concourse.mybir as mybir
from concourse.bass_utils import run_bass_kernel_spmd

F32 = mybir.dt.float32
BF16 = mybir.dt.bfloat16
ALU = mybir.AluOpType
AF = mybir.ActivationFunctionType
AX = mybir.AxisListType

S = 4096
D = 1024
DEPTH = 2
NSEQ = 3
INW = 2432
FF = 2816
NFB = 22
EPS = 1e-6
PADK = 1024
SRC_COL = [0, 128, 256, 512, 640, 768, 896, 1024, 1152, 384, 1280, 1408, 1536,
           1664, 1792, 1920, 2048, 2176, 2304]


class Tk:
    __slots__ = ("w", "r", "excl")

    def __init__(self, excl=False):
        self.w = {}
        self.r = {}
        self.excl = excl


class Sched:
    ENGS = ("pe", "act", "dve", "pool", "sp")
    NDSEM = 24

    def __init__(self):
        self.q = {e: [] for e in self.ENGS}
        self.dcount = [0] * self.NDSEM
        self.dn = 0
        self.pending = {e: {} for e in self.ENGS}

    def _deps(self, eng, rd, wr):
        deps = self.pending[eng]
        self.pending[eng] = {}

        def need(k, v):
            if deps.get(k, -1) < v:
                deps[k] = v

        for t in rd:
            for k, v in t.w.items():
                if k == eng and eng in ("pe", "sp"):
                    continue
                need(k, v)
            if t.excl:
                for k, v in t.r.items():
                    if k != eng:
                        need(k, v)
        for t in wr:
            for k, v in t.w.items():
                if k != eng:
                    need(k, v)
            for k, v in t.r.items():
                if k != eng:
                    need(k, v)
        for k, v in deps.items():
            if not isinstance(k, tuple):
                self.q[k][v][2] = True
        return deps

    def op(self, eng, fn, rd=(), wr=()):
        deps = self._deps(eng, rd, wr)
        idx = len(self.q[eng])
        self.q[eng].append([fn, deps, False, 0])
        for t in rd:
            t.r[eng] = idx
        for t in wr:
            t.w = {eng: idx}
            t.r = {}

    def dma(self, fn, rd=(), wr=(), q="sp"):
        i = self.dn % self.NDSEM
        self.dn += 1
        deps = self._deps(q, rd, wr)
        key = ("d", i)
        prev = self.dcount[i] * 16
        if prev > 0 and deps.get(key, -1) < prev:
            deps[key] = prev
        self.dcount[i] += 1
        val = self.dcount[i] * 16
        self.q[q].append([fn, deps, key, 0])
        for t in rd:
            t.r[key] = val
        for t in wr:
            t.w = {key: val}
            t.r = {}

    def barrier(self):
        last = {e: len(self.q[e]) - 1 for e in ("pe", "act", "dve", "pool")}
        for e in self.ENGS:
            p = self.pending[e]
            for k, v in last.items():
                if k != e and v >= 0 and p.get(k, -1) < v:
                    p[k] = v
            for i in range(self.NDSEM):
                if self.dcount[i] > 0:
                    p[("d", i)] = max(p.get(("d", i), 0), self.dcount[i] * 16)

    def finalize(self):
        for e in self.ENGS:
            for k, v in self.pending[e].items():
                if not isinstance(k, tuple):
                    self.q[k][v][2] = True


class Prog:
    def __init__(self, nseq=NSEQ, depth=DEPTH, debug=False):
        self.nseq = nseq
        self.depth = depth
        self.debug = debug
        self.nc = bass.Bass("TRN2", target_bir_lowering=False)
        self.s = Sched()
        self.uid = 0
        self.psi = 0
        self.psc = {}
        self.bg = []

    def name(self, n):
        self.uid += 1
        return "%s_%d" % (n, self.uid)

    def A(self, es, n, shape, dt):
        return es.enter_context(self.nc.sbuf_tensor(self.name(n), shape, dt))

    def ps(self):
        i = self.psi % 8
        self.psi += 1
        return self.PS[i], self.PSB[i], self.PST[i]

    def bg_step(self, n=1):
        for _ in range(n):
            if self.bg:
                self.bg.pop(0)()

    def bg_flush(self):
        while self.bg:
            self.bg.pop(0)()

    def psx(self, lo, hi):
        c = self.psc.get((lo, hi), 0)
        self.psc[(lo, hi)] = c + 1
        i = lo + c % (hi - lo)
        return self.PS[i], self.PSB[i], self.PST[i]

    def act(self, out, in_, func, rd, wr, **kw):
        self.s.op("act", lambda e: e.activation(out=out, in_=in_, func=func, **kw), rd, wr)

    def tt(self, eng, out, in0, in1, op, rd, wr):
        self.s.op(eng, lambda e: e.tensor_tensor(out=out, in0=in0, in1=in1, op=op), rd, wr)

    def ts(self, eng, out, in0, s1, s2, op0, op1, rd, wr):
        if s2 is None:
            self.s.op(eng, lambda e: e.tensor_scalar(out=out, in0=in0, scalar1=s1, scalar2=None, op0=op0), rd, wr)
        else:
            self.s.op(eng, lambda e: e.tensor_scalar(out=out, in0=in0, scalar1=s1, scalar2=s2, op0=op0, op1=op1), rd, wr)

    def stt(self, out, in0, scalar, in1, op0, op1, rd, wr):
        self.s.op("dve", lambda e: e.scalar_tensor_tensor(out=out, in0=in0, scalar=scalar, in1=in1, op0=op0, op1=op1), rd, wr)

    def cp(self, eng, out, in_, rd, wr):
        if eng == "act":
            self.s.op("act", lambda e: e.activation(out=out, in_=in_, func=AF.Copy), rd, wr)
        else:
            self.s.op(eng, lambda e: e.tensor_copy(out=out, in_=in_), rd, wr)

    def memset(self, eng, ap, val, wr):
        self.s.op(eng, lambda e: e.memset(ap, val), (), wr)

    def mm(self, out, lhsT, rhs, start, stop, rd, wr):
        self.s.op("pe", lambda e: e.matmul(out, lhsT=lhsT, rhs=rhs, start=start, stop=stop), rd, wr)

    def tr(self, out, in_, ident, rd, wr):
        self.s.op("pe", lambda e: e.transpose(out=out, in_=in_, identity=ident), rd, wr)

    def dma(self, out, in_, rd, wr, q="sp"):
        self.s.dma(lambda e: e.dma_start(out=out, in_=in_), rd, wr, q=q)

    def build(self):
        nc = self.nc
        ns, dp = self.nseq, self.depth
        dt = nc.dram_tensor
        self.x_d = dt("x", [ns, S, D], F32, kind="ExternalInput").ap()
        self.c_d = dt("c", [ns, D], F32, kind="ExternalInput").ap()
        self.cc_d = dt("ropec", [S, 64], F32, kind="ExternalInput").ap()
        self.ss_d = dt("ropes", [S, 64], F32, kind="ExternalInput").ap()
        W = {}
        for nm, shp in (("norm1_g", [DEPTH, D]), ("norm2_g", [DEPTH, D]), ("w_mod", [DEPTH, D, 6 * D]),
                        ("b_mod", [DEPTH, 6 * D]), ("w_in", [DEPTH, D, INW]), ("qn_a", [DEPTH, 64]),
                        ("kn_a", [DEPTH, 64]), ("sink_a", [DEPTH, 4]), ("qn_b", [DEPTH, 64]),
                        ("kn_b", [DEPTH, 64]), ("conv_c_w", [DEPTH, 31, 384]), ("conv_c_b", [DEPTH, 384]),
                        ("ln_c_g", [DEPTH, 384]), ("ln_c_b", [DEPTH, 384]), ("w_out", [DEPTH, D, D]),
                        ("w_up", [DEPTH, D, 2 * FF]), ("conv_f_w", [DEPTH, 3, FF]), ("conv_f_b", [DEPTH, FF]),
                        ("w_down", [DEPTH, FF, D])):
            W[nm] = dt(nm, shp, F32, kind="ExternalInput").ap()
        self.W = W
        self.y_d = dt("y", [ns, S, D], F32, kind="ExternalOutput").ap()
        kd = "ExternalOutput" if self.debug else "Internal"
        self.xa_d = dt("xa", [ns, S, D], F32, kind=kd).ap()
        self.xb_d = dt("xb", [ns, S, D], F32, kind="Internal").ap()
        self.qk_d = dt("qkT", [ns, 1152, S], BF16, kind=kd).ap()
        self.vs_d = dt("vsc", [ns, S, 8, 128], BF16, kind=kd).ap()
        self.u_d = dt("uT", [ns, 384, S], BF16, kind=kd).ap()
        self.mx_d = dt("mixA", [ns, 640, S], BF16, kind=kd).ap()
        self.gs_d = dt("gsc", [DEPTH * ns * 2, D], F32, kind="Internal").ap()

        with ExitStack() as top:
            self.PS, self.PSB, self.PST = [], [], []
            self.psall = top.enter_context(nc.psum_tensor(self.name("psall"), [128, 8 * 512], F32))
            for i in range(8):
                p = self.psall[:, i * 512:(i + 1) * 512]
                self.PS.append(p)
                self.PSB.append(p.bitcast(BF16))
                self.PST.append(Tk(excl=True))
            sems = {e: top.enter_context(nc.semaphore(self.name("s" + e))) for e in ("pe", "act", "dve", "pool")}
            dsems = [top.enter_context(nc.semaphore(self.name("d"))) for _ in range(Sched.NDSEM)]
            block = top.enter_context(nc.Block())
            self.persist(top)
            self.prologue()
            for l in range(dp):
                xin = [self.x_d[s] if l == 0 else self.xb_d[s] for s in range(ns)]
                xout = [self.y_d[s] if l == dp - 1 else self.xb_d[s] for s in range(ns)]
                self.layer(l, xin, xout)
            self.s.barrier()
            self.s.finalize()
            sch = self.s

            def mk(eng_name):
                def f(h):
                    sch_emit_engine(sch, eng_name, h, sems, dsems)
                return f

            for bname, en in (("tensor", "pe"), ("scalar", "act"), ("vector", "dve"), ("gpsimd", "pool"), ("sync", "sp")):
                getattr(block, bname)(mk(en))
        return nc

    def persist(self, es):
        A = self.A
        self.ident = A(es, "ident", [128, 128], F32)
        self.identb = A(es, "identb", [128, 128], BF16)
        self.epsT = A(es, "eps", [128, 1], F32)
        self.mhalf = A(es, "mhalf", [128, 1], F32)
        self.maskA = A(es, "maskA", [128, 384], BF16)
        self.maskB2 = [A(es, "maskB2", [128, 512], BF16) for _ in range(3)]
        self.vecs = A(es, "vecs", [128, DEPTH, 2, 8], F32)
        self.cvec = A(es, "cvec", [128, DEPTH, 3, 3], F32)
        self.wc = A(es, "wc", [128, DEPTH, 3, 31], F32)
        self.wf = A(es, "wf", [128, DEPTH, NFB, 3], F32)
        self.bf = A(es, "bf", [128, DEPTH, NFB], F32)
        self.modT = A(es, "modT", [128, DEPTH, 48, self.nseq], F32)
        self.ab = A(es, "ab", [128, self.nseq, 4, 8], F32)
        self.const_tk = Tk()
        self.ab_tk = Tk()

    def load_T(self, es_stage, stage, stage_tk, dst, src, R, C):
        nb = C // 128
        Rp = R + (R % 2)
        self.dma(stage[0:R, 0:C], src, (), (stage_tk,))
        ps, _, pt = self.ps()
        for j in range(nb):
            self.tr(ps[:, j * Rp:(j + 1) * Rp], stage[0:Rp, j * 128:(j + 1) * 128], self.ident[0:Rp, 0:Rp],
                    (stage_tk, self.const_tk), (pt,))
        self.cp("dve", dst, ps[:, 0:nb * Rp].rearrange("p (j r) -> p j r", r=Rp)[:, :, 0:R], (pt,), (self.const_tk,))

    def prologue(self):
        nc, W, ns = self.nc, self.W, self.nseq
        ctk = self.const_tk
        self.memset("pool", self.ident[:], 1.0, (ctk,))
        self.s.op("pool", lambda e: e.affine_select(out=self.ident[:], in_=self.ident[:], pattern=[[-1, 128]],
                                                    compare_op=ALU.is_equal, fill=0.0, base=0, channel_multiplier=1),
                  (ctk,), (ctk,))
        self.cp("pool", self.identb[:], self.ident[:], (ctk,), (ctk,))
        self.memset("pool", self.epsT[:], EPS, (ctk,))
        self.memset("pool", self.mhalf[:], -0.5, (ctk,))
        mA = self.maskA
        self.memset("pool", mA[:], 1.0, (ctk,))

        def sel(ap, pattern, cm, base):
            self.s.op("pool", lambda e: e.affine_select(out=ap, in_=ap, pattern=pattern, compare_op=ALU.is_ge,
                                                        fill=0.0, base=base, channel_multiplier=cm), (ctk,), (ctk,))
        sel(mA[:, 0:128], [[-1, 128]], 1, 0)
        sel(mA[:, 256:384], [[1, 128]], -1, 0)
        for v in range(3):
            mB = self.maskB2[v]
            self.memset("pool", mB[:], 1.0, (ctk,))
            for h in range(2):
                sel(mB[:, h * 256:h * 256 + 128], [[-1, 128]], 1, 0)
                sel(mB[:, h * 256 + 128:h * 256 + 256], [[1, 128]], -1, 0)
        for h in range(2):
            sel(self.maskB2[1][:, h * 256:h * 256 + 128], [[0, 128]], 1, -64)
            sel(self.maskB2[2][:, h * 256 + 128:h * 256 + 256], [[0, 128]], -1, 63)

        with ExitStack() as es:
            A = self.A
            stage = A(es, "stage", [128, 6 * D], F32)
            stk = Tk()
            self.memset("dve", stage[:], 0.0, (stk,))
            cT = A(es, "cT", [128, 8, ns], F32)
            scT = A(es, "scT", [128, 8, ns], BF16)
            screp = A(es, "screp", [128, 8, ns, 128], BF16)
            bmT = A(es, "bmT", [128, 48], F32)
            wm = A(es, "wm", [128, 8, 6 * D], BF16)
            wst = [A(es, "wst", [128, 8, 512], F32) for _ in range(4)]
            wst_tk = [Tk() for _ in range(4)]
            wm_tk = [Tk() for _ in range(48)]
            grow = A(es, "grow", [128, 512], F32)
            brow = A(es, "brow", [1, 512], F32)
            g_tk, b_tk, sc_tk, bm_tk = Tk(), Tk(), Tk(), Tk()
            for l in range(self.depth):
                self.load_T(es, stage, stk, self.vecs[:, l, 0, :].unsqueeze(2), W["norm1_g"][l:l + 1, :], 1, D)
                self.load_T(es, stage, stk, self.vecs[:, l, 1, :].unsqueeze(2), W["norm2_g"][l:l + 1, :], 1, D)
                for wi, nm in enumerate(("conv_c_b", "ln_c_g", "ln_c_b")):
                    self.load_T(es, stage, stk, self.cvec[:, l, wi, :].unsqueeze(2), W[nm][l:l + 1, :], 1, 384)
                self.load_T(es, stage, stk, self.wc[:, l, :, :], W["conv_c_w"][l], 31, 384)
                self.load_T(es, stage, stk, self.wf[:, l, :, :], W["conv_f_w"][l], 3, FF)
                self.load_T(es, stage, stk, self.bf[:, l, :].unsqueeze(2), W["conv_f_b"][l:l + 1, :], 1, FF)
            self.load_T(es, stage, stk, cT[:], self.c_d, ns, D)
            self.act(scT[:], cT[:], AF.Silu, (ctk,), (sc_tk,))
            self.cp("dve", screp[:], scT[:].unsqueeze(3).to_broadcast([128, 8, ns, 128]), (sc_tk,), (sc_tk,))
            for l in range(self.depth):
                self.load_T(es, stage, stk, bmT[:].unsqueeze(2), W["b_mod"][l:l + 1, :], 1, 6 * D)
                for f4 in range(12):
                    b = f4 % 4
                    self.dma(wst[b][:], W["w_mod"][l][:, f4 * 512:(f4 + 1) * 512].rearrange("(k p) c -> p k c", p=128),
                             (), (wst_tk[b],))
                    self.cp("act" if f4 % 2 else "dve", wm[:, :, f4 * 512:(f4 + 1) * 512], wst[b][:], (wst_tk[b],),
                            tuple(wm_tk[4 * f4:4 * f4 + 4]))
                ps, _, pt = self.ps()
                for fb in range(48):
                    for k in range(8):
                        self.mm(ps[:, fb * ns:(fb + 1) * ns], wm[:, k, fb * 128:(fb + 1) * 128], scT[:, k, :],
                                k == 0, k == 7, (wm_tk[fb], sc_tk), (pt,))
                self.tt("dve", self.modT[:, l, :, :], ps[:, 0:48 * ns].rearrange("p (f s) -> p f s", s=ns),
                        bmT[:].unsqueeze(2).to_broadcast([128, 48, ns]), ALU.add, (pt, ctk), (ctk,))
                for s in range(ns):
                    for gi, cb in ((0, 2 * D), (1, 5 * D)):
                        for hh in range(2):
                            ps, _, pt = self.ps()
                            c0 = cb + hh * 512
                            for k in range(8):
                                self.mm(ps[:, :], screp[:, k, s, :], wm[:, k, c0:c0 + 512], k == 0, k == 7,
                                        tuple(wm_tk[c0 // 128:c0 // 128 + 4]) + (sc_tk,), (pt,))
                            self.dma(brow[:], W["b_mod"][l:l + 1, c0:c0 + 512], (), (b_tk,))
                            self.tt("dve", grow[0:1, :], ps[0:1, :], brow[:], ALU.add, (pt, b_tk), (g_tk,))
                            row = (l * ns + s) * 2 + gi
                            self.dma(self.gs_d[row:row + 1, hh * 512:(hh + 1) * 512], grow[0:1, :], (g_tk,), ())
        self.s.barrier()

    def layer(self, l, xin, xout):
        ctk, abk = self.const_tk, self.ab_tk
        m = self.modT
        for s in range(self.nseq):
            ab = self.ab[:, s]
            self.ts("dve", ab[:, 0, :], m[:, l, 8:16, s], 1.0, None, ALU.add, None, (ctk,), (abk,))
            self.tt("dve", ab[:, 0, :], ab[:, 0, :], self.vecs[:, l, 0, :], ALU.mult, (abk, ctk), (abk,))
            self.cp("dve", ab[:, 1, :], m[:, l, 0:8, s], (ctk,), (abk,))
            self.ts("dve", ab[:, 2, :], m[:, l, 32:40, s], 1.0, None, ALU.add, None, (ctk,), (abk,))
            self.tt("dve", ab[:, 2, :], ab[:, 2, :], self.vecs[:, l, 1, :], ALU.mult, (abk, ctk), (abk,))
            self.cp("dve", ab[:, 3, :], m[:, l, 24:32, s], (ctk,), (abk,))
        self.phase1(l, xin)
        self.s.barrier()
        if self.debug == 1:
            return
        W = self.W
        with ExitStack() as esX:
            A = self.A
            sh = {}
            sh["wd"] = A(esX, "wd", [128, NFB, D], BF16)
            sh["wd_tk"] = [Tk() for _ in range(NFB)]
            with ExitStack() as esY:
                wo = A(esY, "wo", [128, 8, D], BF16)
                wo_tk = [Tk() for _ in range(8)]
                wst = [A(esY, "wst", [128, 512], F32) for _ in range(4)]
                wst_tk = [Tk() for _ in range(4)]
                diag = A(esY, "diag", [128, 93, 128], BF16)
                dg_tk = Tk()
                sh.update(wo=wo, wo_tk=wo_tk, diag=diag, dg_tk=dg_tk)

                def bg_weights(pieces):
                    def dma_(j):
                        src, dst, tk = pieces[j]
                        self.dma(wst[j % 4][:], src, (), (wst_tk[j % 4],))

                    def slot(j):
                        src, dst, tk = pieces[j]
                        self.cp("pool", dst, wst[j % 4][:], (wst_tk[j % 4],), (tk,))
                        if j + 4 < len(pieces):
                            dma_(j + 4)
                    for j in range(min(4, len(pieces))):
                        dma_(j)
                    return [lambda j=j: slot(j) for j in range(len(pieces))]

                pcs = [(W["w_out"][l][k * 128:(k + 1) * 128, hh * 512:(hh + 1) * 512], wo[:, k, hh * 512:(hh + 1) * 512], wo_tk[k])
                       for k in range(8) for hh in range(2)]
                self.bg = bg_weights(pcs)
                for cb in range(3):
                    for k in range(31):
                        self.bg.append(lambda cb=cb, k=k: self.ts("pool", diag[:, cb * 31 + k, :], self.identb[:], self.wc[:, l, cb, k:k + 1], None,
                                                                ALU.mult, None, (ctk,), (dg_tk,)))
                self.phase2a(l)
                self.bg_flush()
                self.s.barrier()
                if self.debug == 2:
                    return
                self.bg_flush()
                pcs = [(W["w_down"][l][fb * 128:(fb + 1) * 128, hh * 512:(hh + 1) * 512], sh["wd"][:, fb, hh * 512:(hh + 1) * 512], sh["wd_tk"][fb])
                       for fb in range(NFB) for hh in range(2)]
                self.bg = bg_weights(pcs)
                self.phase2c(l, xin, sh)
                self.bg_flush()
                self.s.barrier()
                if self.debug == 3:
                    return
            self.phase3(l, xout, sh)
            self.s.barrier()

    def rms_tile(self, xt, xt_tk, xn, xn_tk, junk, junk_tk, st, st_tk, P=128):
        self.act(junk[0:P, :], xt[0:P, :], AF.Square, (xt_tk,), (junk_tk, st_tk), scale=1.0 / 32.0, accum_out=st[0:P, 0:1])
        self.act(st[0:P, 1:2], st[0:P, 0:1], AF.Ln, (st_tk, self.const_tk), (st_tk,), bias=self.epsT[0:P, :])
        self.act(st[0:P, 2:3], st[0:P, 1:2], AF.Exp, (st_tk,), (st_tk,), scale=-0.5)
        self.act(xn[0:P, :], xt[0:P, :], AF.Copy, (xt_tk, st_tk), (xn_tk,), scale=st[0:P, 2:3])

    def phase1(self, l, xin):
        W = self.W
        ctk, abk = self.const_tk, self.ab_tk
        NCB = 4
        with ExitStack() as es:
            A = self.A
            wi = A(es, "wi", [128, 8, INW], BF16)
            wi_tk = [Tk() for _ in range(19)]
            wst = [A(es, "wst", [128, 4, 128], F32) for _ in range(4)]
            wst_tk = [Tk() for _ in range(4)]
            cct = A(es, "cct", [128, 32, 64], F32)
            sst = A(es, "sst", [128, 32, 64], F32)
            gt = A(es, "gt", [128, 18, 64], F32)
            gsm = A(es, "gsm", [128, 4, 64], F32)
            tab_tk = Tk()
            xt = [A(es, "xt", [128, D], F32) for _ in range(2)]
            xt_tk = [Tk(), Tk()]
            xn4 = [A(es, "xn4", [128, 4, D], F32)] * 2
            xn_tk = [[Tk() for _ in range(4)]] * 2
            junk = A(es, "junk", [128, D], BF16)
            junk_tk = Tk()
            st = [A(es, "st", [128, 4], F32) for _ in range(2)]
            st_tk = [Tk(), Tk()]
            hT = [A(es, "hT", [128, 8, 512], BF16) for _ in range(2)]
            hT_tk = [[Tk() for _ in range(8)] for _ in range(2)]
            qs = [A(es, "qs", [128, 18, 64], F32) for _ in range(NCB)]
            sq = [A(es, "sq", [128, 18, 64], F32) for _ in range(NCB)]
            msb = [A(es, "msb", [128, 18, 64], F32) for _ in range(NCB)]
            qf = [A(es, "qf", [128, 1152], BF16) for _ in range(NCB)]
            ssq = [A(es, "ssq", [128, 3, 18], F32) for _ in range(NCB)]
            qs_tk = [Tk() for _ in range(NCB)]
            sq_tk = [Tk() for _ in range(NCB)]
            ms_tk = [Tk() for _ in range(NCB)]
            qf_tk = [Tk() for _ in range(NCB)]
            ssq_tk = [Tk() for _ in range(NCB)]
            qkst = [A(es, "qkst", [128, 9, 512], BF16)] * 2
            qkst_tk = [Tk()] * 2
            vst = [A(es, "vst", [128, 8, 128], BF16) for _ in range(2)]
            vst_tk = [Tk(), Tk()]
            ust = [A(es, "ust", [128, 3, 512], BF16)] * 2
            ust_tk = [Tk()] * 2
            sg = [A(es, "sg", [128, 512], F32) for _ in range(2)]
            sg_tk = [Tk(), Tk()]

            groups = ((0, 512), (512, 1024), (1024, 1152), (1152, 1664))
            chunks = [(s, c) for s in range(self.nseq) for c in range(8)]
            cnt = {"x": 0, "ch": 0}

            def stNd(i, t4):
                s, c = chunks[i]
                c0 = c * 512
                b = t4 % 2
                self.dma(xt[b][:], xin[s][c0 + t4 * 128:c0 + (t4 + 1) * 128, :], (), (xt_tk[b],))

            def stNc(i, t4):
                X4, X4t = xn4[i % 2], xn_tk[i % 2]
                b = t4 % 2
                self.rms_tile(xt[b], xt_tk[b], X4[:, t4, :], X4t[t4], junk, junk_tk, st[b], st_tk[b])

            def stN(i):
                for t4 in range(4):
                    stNd(i, t4)
                    stNc(i, t4)

            def stT(i):
                s, c = chunks[i]
                ab = self.ab[:, s]
                X4, X4t = xn4[i % 2], xn_tk[i % 2]
                H, Ht = hT[i % 2], hT_tk[i % 2]
                for k in range(8):
                    ps, _, pt = self.ps()
                    for t4 in range(4):
                        self.tr(ps[:, t4 * 128:(t4 + 1) * 128], X4[:, t4, k * 128:(k + 1) * 128], self.ident[:], (X4t[t4], ctk), (pt,))
                    if k % 2 == 0:
                        self.ts("dve", H[:, k, :], ps[:, :], ab[:, 0, k:k + 1], ab[:, 1, k:k + 1], ALU.mult, ALU.add, (pt, abk), (Ht[k],))
                    else:
                        self.act(H[:, k, :], ps[:, :], AF.Identity, (pt, abk), (Ht[k],), scale=ab[:, 0, k:k + 1], bias=ab[:, 1, k:k + 1])

            def stA(i, t4):
                s, c = chunks[i]
                tile = c * 4 + t4
                gtile = i * 4 + t4
                H, Ht = hT[i % 2], hT_tk[i % 2]
                cb = cnt["ch"] % NCB
                cnt["ch"] += 1
                pss = []
                for (g0, g1) in groups:
                    ps, _, pt = self.ps()
                    pss.append((ps, pt))
                    for k in range(8):
                        self.mm(ps[:, 0:g1 - g0], H[:, k, t4 * 128:(t4 + 1) * 128], wi[:, k, g0:g1], k == 0, k == 7,
                                (Ht[k],) + tuple(wi_tk[g0 // 128:(g1 + 127) // 128]), (pt,))
                Q_, SQ, MS, QF, SS = qs[cb], sq[cb], msb[cb], qf[cb], ssq[cb]
                qsf = Q_[:].rearrange("p h d -> p (h d)")
                sqf = SQ[:].rearrange("p h d -> p (h d)")
                gtf = gt[:].rearrange("p h d -> p (h d)")
                for gi, (g0, g1) in enumerate(groups[:3]):
                    self.act(sqf[:, g0:g1], pss[gi][0][:, 0:g1 - g0], AF.Square, (pss[gi][1],), (sq_tk[cb],))
                    self.tt("dve", qsf[:, g0:g1], pss[gi][0][:, 0:g1 - g0], gtf[:, g0:g1], ALU.mult, (pss[gi][1], tab_tk), (qs_tk[cb],))
                vb = gtile % 2
                self.cp("act", vst[vb][:, :, 0:64], pss[3][0][:, :].rearrange("p (h d) -> p h d", d=64), (pss[3][1],), (vst_tk[vb],))
                self.dma(self.vs_d[s, tile * 128:(tile + 1) * 128, :, :], vst[vb][:], (vst_tk[vb],), ())
                self.s.op("dve", lambda e: e.tensor_reduce(out=SS[:, 0, :], in_=SQ[:], axis=AX.X, op=ALU.add), (sq_tk[cb],), (ssq_tk[cb],))
                self.act(SS[:, 1, :], SS[:, 0, :], AF.Ln, (ssq_tk[cb], ctk), (ssq_tk[cb],), scale=1.0 / 64, bias=self.epsT[:, :])
                self.act(SS[:, 2, :], SS[:, 1, :], AF.Exp, (ssq_tk[cb],), (ssq_tk[cb],), scale=-0.5)
                return cb

            def stA2(i, t4, cb):
                s, c = chunks[i]
                tile = c * 4 + t4
                Q_, SQ, MS, QF, SS = qs[cb], sq[cb], msb[cb], qf[cb], ssq[cb]
                self.tt("pool", SQ[:], Q_[:], cct[:, tile:tile + 1, :].to_broadcast([128, 18, 64]), ALU.mult,
                        (qs_tk[cb], tab_tk, ssq_tk[cb]), (sq_tk[cb],))
                self.tt("pool", MS[:], Q_[:], sst[:, tile:tile + 1, :].to_broadcast([128, 18, 64]), ALU.mult, (qs_tk[cb], tab_tk), (ms_tk[cb],))
                self.tt("pool", Q_[:, :, 0:32], SQ[:, :, 0:32], MS[:, :, 32:64], ALU.subtract, (sq_tk[cb], ms_tk[cb]), (qs_tk[cb],))
                self.tt("pool", Q_[:, :, 32:64], SQ[:, :, 32:64], MS[:, :, 0:32], ALU.add, (sq_tk[cb], ms_tk[cb]), (qs_tk[cb],))
                self.tt("dve", QF[:].rearrange("p (h d) -> p h d", d=64), Q_[:],
                        SS[:, 2, :].unsqueeze(2).to_broadcast([128, 18, 64]), ALU.mult, (qs_tk[cb], ssq_tk[cb]), (qf_tk[cb],))

            def stB(i, t4, cb):
                s, c = chunks[i]
                c0 = c * 512
                QF = qf[cb]
                KS, KSt = qkst[i % 2], qkst_tk[i % 2]
                ps, psb, pt = self.ps()
                ps2, psb2, pt2 = self.ps()
                for j in range(8):
                    self.tr(psb[:, j * 128:(j + 1) * 128], QF[:, j * 128:(j + 1) * 128], self.identb[:], (qf_tk[cb], ctk), (pt,))
                self.tr(psb2[:, 0:128], QF[:, 1024:1152], self.identb[:], (qf_tk[cb], ctk), (pt2,))
                self.cp("act", KS[:, 0:8, t4 * 128:(t4 + 1) * 128], psb[:, :].rearrange("p (i t) -> p i t", t=128), (pt,), (KSt,))
                self.cp("dve", KS[:, 8, t4 * 128:(t4 + 1) * 128], psb2[:, 0:128], (pt2,), (KSt,))
                if t4 == 3:
                    self.dma(self.qk_d[s, :, c0:c0 + 512].rearrange("(i p) t -> p i t", p=128), KS[:], (KSt,), ())

            def stG(i):
                s, c = chunks[i]
                c0 = c * 512
                H, Ht = hT[i % 2], hT_tk[i % 2]
                ub = i % 2
                for cb in range(3):
                    sb = (i * 3 + cb) % 2
                    ps, _, pt = self.ps()
                    g0 = (16 + cb) * 128
                    for k in range(8):
                        self.mm(ps[:, :], wi[:, k, g0:g0 + 128], H[:, k, :], k == 0, k == 7, (Ht[k], wi_tk[16 + cb]), (pt,))
                    ps2, _, pt2 = self.ps()
                    g0 = (13 + cb) * 128
                    for k in range(8):
                        self.mm(ps2[:, :], wi[:, k, g0:g0 + 128], H[:, k, :], k == 0, k == 7, (Ht[k], wi_tk[13 + cb]), (pt2,))
                    self.act(sg[sb][:], ps[:, :], AF.Exp, (pt,), (sg_tk[sb],), scale=-1.0)
                    self.act(sg[sb][:], sg[sb][:], AF.Ln, (sg_tk[sb],), (sg_tk[sb],), bias=1.0)
                    self.act(sg[sb][:], sg[sb][:], AF.Exp, (sg_tk[sb],), (sg_tk[sb],), scale=-1.0)
                    self.tt("dve", ust[ub][:, cb, :], ps2[:, :], sg[sb][:], ALU.mult, (pt2, sg_tk[sb]), (ust_tk[ub],))
                self.dma(self.u_d[s, :, c0:c0 + 512].rearrange("(i p) t -> p i t", p=128), ust[ub][:], (ust_tk[ub],), ())

            n = len(chunks)
            tiles = [(i, t4) for i in range(n) for t4 in range(4)]
            cbs = {}
            stN(0)
            self.dma(cct[:], self.cc_d.rearrange("(t p) c -> p t c", p=128), (), (tab_tk,))
            self.dma(sst[:], self.ss_d.rearrange("(t p) c -> p t c", p=128), (), (tab_tk,))
            for gi, nm in enumerate(("qn_a", "kn_a", "qn_b", "kn_b")):
                self.dma(gsm[:, gi, :], W[nm][l:l + 1, :].to_broadcast([128, 64]), (), (tab_tk,))
            for (h0, h1, gi) in ((0, 4, 0), (4, 6, 1), (6, 12, 2), (12, 18, 3)):
                self.cp("pool", gt[:, h0:h1, :], gsm[:, gi:gi + 1, :].to_broadcast([128, h1 - h0, 64]), (tab_tk,), (tab_tk,))
            for v in vst:
                self.memset("pool", v[:], 1.0, (vst_tk[0], vst_tk[1]))
            for blk in range(19):
                sc = SRC_COL[blk]
                for kh in range(2):
                    b = (blk * 2 + kh) % 4
                    self.dma(wst[b][:], W["w_in"][l][kh * 512:(kh + 1) * 512, sc:sc + 128].rearrange("(k p) c -> p k c", p=128),
                             (), (wst_tk[b],))
                    self.cp("act", wi[:, 4 * kh:4 * kh + 4, blk * 128:(blk + 1) * 128], wst[b][:], (wst_tk[b],), (wi_tk[blk],))

            stT(0)
            if n > 1:
                stNd(1, 0)
                stNd(1, 1)
            for u, (i, t4) in enumerate(tiles):
                cbs[u] = stA(i, t4)
                if i + 1 < n:
                    if t4 == 0:
                        stNc(i + 1, 0)
                        stNd(i + 1, 2)
                    elif t4 == 1:
                        stNc(i + 1, 1)
                        stNd(i + 1, 3)
                    elif t4 == 2:
                        stNc(i + 1, 2)
                        stNc(i + 1, 3)
                    else:
                        stT(i + 1)
                        if i + 2 < n:
                            stNd(i + 2, 0)
                            stNd(i + 2, 1)
                if u >= 1:
                    stA2(tiles[u - 1][0], tiles[u - 1][1], cbs[u - 1])
                if u >= 3:
                    stB(tiles[u - 3][0], tiles[u - 3][1], cbs[u - 3])
                if t4 == 3:
                    stG(i)
            nt = len(tiles)
            stA2(tiles[nt - 1][0], tiles[nt - 1][1], cbs[nt - 1])
            for u in range(max(0, nt - 3), nt):
                stB(tiles[u][0], tiles[u][1], cbs[u])

    def phase2a(self, l):
        W = self.W
        ctk = self.const_tk
        with ExitStack() as es:
            A = self.A
            kbuf = [A(es, "kbuf", [128, PADK + S + PADK], BF16) for _ in range(2)]
            qbuf = [A(es, "qbuf", [128, S], BF16) for _ in range(2)]
            kq_tk = [Tk(), Tk()]
            kqp_tk = [[Tk() for _ in range(24)] for _ in range(2)]
            accs = [A(es, "acc", [128, 2048], F32) for _ in range(4)]
            accs_tk = [Tk() for _ in range(4)]
            lnd = A(es, "lnd", [128, 2048], F32)
            rcp = lnd
            fin_tk = Tk()
            mst = [A(es, "mst", [128, 2048], BF16) for _ in range(2)]
            mst_tk = [Tk(), Tk()]
            NP = 8
            pT = [A(es, "pT", [128, 512], BF16) for _ in range(NP)]
            pT_tk = [Tk() for _ in range(NP)]
            NV = 14
            vt = [A(es, "vt", [128, 2, 128], BF16) for _ in range(NV)]
            vt_tk = [Tk() for _ in range(NV)]
            sk = A(es, "sk", [128, 4], F32)
            sk_tk = Tk()
            for b in range(2):
                self.memset("dve", kbuf[b][:, 0:PADK], 0.0, (kq_tk[b],))
                self.memset("dve", kbuf[b][:, PADK + S:], 0.0, (kq_tk[b],))
            for v in range(NV):
                self.memset("dve", vt[v][:], 0.0, (vt_tk[v],))
            self.dma(sk[:], W["sink_a"][l:l + 1, :].to_broadcast([128, 4]), (), (sk_tk,))
            self.act(sk[:], sk[:], AF.Exp, (sk_tk,), (sk_tk,))
            pcount = 0
            vcount = 0
            pend = []
            finq = []
            pgs0 = []
            for p in range(2):
                pgs0.append(dict(q0=128 * p, k=[(256 + 64 * p, 0), (256 + 64 * p, 64)], vh=p, nvh=1, a=True, o0=128 * p, hq=2 * p))
            for p in range(3):
                pgs0.append(dict(q0=384 + 128 * p, k=[(768 + 128 * p, 0), (768 + 128 * p + 64, 64)], vh=2 + 2 * p, nvh=2,
                                 a=False, o0=256 + 128 * p, hq=0))
            pgs = [dict(pg, s=s) for s in range(self.nseq) for pg in pgs0]

            def load_kq_pieces(pi):
                pg_ = pgs[pi]
                kb_ = pi % 2
                qk = self.qk_d[pg_["s"]]
                pcs = []
                for c8 in range(8):
                    cs = slice(c8 * 512, (c8 + 1) * 512)
                    pcs.append(lambda cs=cs, c8=c8: self.dma(qbuf[kb_][:, cs], qk[pg_["q0"]:pg_["q0"] + 128, cs], (), (kqp_tk[kb_][3 * c8],)))
                    for ki, (r0, dp0) in enumerate(pg_["k"]):
                        pcs.append(lambda cs=cs, r0=r0, dp0=dp0, c8=c8, ki=ki: self.dma(
                            kbuf[kb_][dp0:dp0 + 64, PADK + c8 * 512:PADK + (c8 + 1) * 512], qk[r0:r0 + 64, cs], (),
                            (kqp_tk[kb_][3 * c8 + 1 + ki],)))
                return pcs

            kqq = load_kq_pieces(0)
            fcount = 0
            for pi, pg in enumerate(pgs):
                kb = pi % 2
                K, Q, kqt = kbuf[kb], qbuf[kb], (kq_tk[kb],) + tuple(kqp_tk[kb])
                vs_d = self.vs_d[pg["s"]]
                while kqq:
                    kqq.pop(0)()
                if pi + 1 < len(pgs):
                    kqq = load_kq_pieces(pi + 1)
                pats = [(1, 3)] if pg["a"] else [(1, 2), (4, 2), (16, 2)]
                for ST in range(2):
                    t0 = ST * 2048
                    par = (pi * 2 + ST) % 2
                    acc = accs[2 * par:2 * par + 2]
                    acc_tk = accs_tk[2 * par:2 * par + 2]
                    for pidx, (d, nk) in enumerate(pats):
                        Ls = S // d
                        nqb = 2048 // (128 * d)
                        nblk = Ls // 128
                        cache = {}
                        pv = [None, None]
                        for b in range(16):
                            r, jj = divmod(b, nqb)
                            j = ST * nqb + jj
                            s0 = 128 * j
                            g, pb = divmod(b, 4)
                            if pb == 0:
                                pv = [self.psx(0, 4), self.psx(0, 4)]
                            if pg["a"]:
                                ms = [m for m in (j - 1, j, j + 1) if 0 <= m < nblk]
                                if j == 0:
                                    mask = self.maskA[:, 128:384]
                                elif j == nblk - 1:
                                    mask = self.maskA[:, 0:256]
                                else:
                                    mask = self.maskA[:, :]
                                koff = 0
                            else:
                                ms = [j, j + 1]
                                mask = None
                                koff = 64
                            nkk = len(ms)
                            slots = []
                            for m in ms:
                                key = (r, m)
                                if key not in cache:
                                    sl = vcount % NV
                                    vcount += 1
                                    ks = 128 * m - koff
                                    lo, hi = max(ks, 0), min(ks + 128, Ls)
                                    src = vs_d[:, pg["vh"]:pg["vh"] + pg["nvh"], :]
                                    tok0 = r + d * lo
                                    nrow = hi - lo
                                    if d == 1:
                                        srcv = src[tok0:tok0 + nrow]
                                    else:
                                        srcv = src[tok0:tok0 + d * (nrow - 1) + 1:d]
                                    self.dma(vt[sl][lo - ks:hi - ks, 0:pg["nvh"], :], srcv, (), (vt_tk[sl],))
                                    cache[key] = sl
                                slots.append(cache[key])
                            self.bg_step(1)
                            heads_iter = [(0,), (1,)] if pg["a"] else [(0, 1)]
                            if not pg["a"]:
                                mask = self.maskB2[1 if j == 0 else (2 if j == nblk - 1 else 0)][:, :]
                            for hs_ in heads_iter:
                                pi_ = pcount % NP
                                pcount += 1
                                mask_eng = "pool" if pcount % 3 == 0 else "dve"

                                def stA(hs_=hs_, r=r, s0=s0, ms=ms, nkk=nkk, mask=mask, koff=koff, pi_=pi_, d=d, mask_eng=mask_eng,
                                        K=K, Q=Q, kqt=kqt):
                                    W_ = nkk * 128
                                    qc0 = r + d * s0
                                    tot = len(hs_) * W_
                                    P, Pt = pT[pi_], pT_tk[pi_]
                                    if len(hs_) == 1:
                                        ps, _, pt = self.psx(4, 8)
                                        banks = [(ps, pt)]
                                        pin = ps[:, 0:tot]
                                        pout = P[:, 0:tot]
                                    else:
                                        c2 = self.psc.get("st2", 0)
                                        self.psc["st2"] = c2 + 1
                                        b0 = 4 + 2 * (c2 % 2)
                                        banks = [(self.PS[b0], self.PST[b0]), (self.PS[b0 + 1], self.PST[b0 + 1])]
                                        pin = self.psall[:, b0 * 512:(b0 + 2) * 512].rearrange("p (b c) -> p b c", c=512)[:, :, 0:W_]
                                        pout = P[:, 0:tot].rearrange("p (b c) -> p b c", c=W_)
                                    for ci, m in enumerate(ms):
                                        for idx, hi_ in enumerate(hs_):
                                            ps, pt = banks[idx]
                                            rows = slice(64 * hi_, 64 * hi_ + 64)
                                            qsl = Q[rows, qc0:qc0 + d * 127 + 1:d] if d > 1 else Q[rows, qc0:qc0 + 128]
                                            kc0 = PADK + r + d * (128 * m - koff)
                                            ksl = K[rows, kc0:kc0 + d * 127 + 1:d] if d > 1 else K[rows, kc0:kc0 + 128]
                                            self.mm(ps[:, ci * 128:(ci + 1) * 128], ksl, qsl, True, True, kqt, (pt,))
                                    pts = tuple(b_[1] for b_ in banks)
                                    self.act(pout, pin, AF.Exp, pts, (Pt,), scale=0.125)
                                    self.tt(mask_eng, P[:, 0:tot], P[:, 0:tot], mask, ALU.mult, (Pt, ctk), (Pt,))

                                def stB(hs_=hs_, ms=ms, nkk=nkk, slots=slots, pi_=pi_, pv=pv, pb=pb, g=g, d=d, pidx=pidx, a=pg["a"],
                                        acc=acc, acc_tk=acc_tk):
                                    P, Pt = pT[pi_], pT_tk[pi_]
                                    W_ = nkk * 128
                                    for idx, hi_ in enumerate(hs_):
                                        pvp, _, pvt = pv[hi_]
                                        for ci, m in enumerate(ms):
                                            sl = slots[ci]
                                            hsel = 0 if a else hi_
                                            self.mm(pvp[:, pb * 128:(pb + 1) * 128], vt[sl][:, hsel, :],
                                                    P[:, idx * W_ + ci * 128:idx * W_ + (ci + 1) * 128],
                                                    ci == 0, ci == nkk - 1, (vt_tk[sl], Pt), (pvt,))
                                    if pb == 3:
                                        for hi_ in hs_:
                                            pvp, _, pvt = pv[hi_]
                                            a_ = acc[hi_]
                                            if d == 1:
                                                dst = a_[:, 512 * g:512 * g + 512]
                                                srcp = pvp[:, :]
                                            elif d == 4:
                                                dst = a_[:, :].rearrange("p (x r) -> p r x", r=4)[:, g, :]
                                                srcp = pvp[:, :]
                                            else:
                                                dst = a_[:, :].rearrange("p (i r) -> p r i", r=16)[:, 4 * g:4 * g + 4, :]
                                                srcp = pvp[:, :].rearrange("p (r i) -> p r i", i=128)
                                            if pidx == 0:
                                                self.cp("dve" if hi_ == 0 else "act", dst, srcp, (pvt,), (acc_tk[hi_],))
                                            else:
                                                self.tt("dve", dst, srcp, dst, ALU.add, (pvt, acc_tk[hi_]), (acc_tk[hi_],))

                                stA()
                                pend.append(stB)
                                while len(pend) > 5:
                                    pend.pop(0)()
                                if finq:
                                    finq.pop(0)()
                                if kqq:
                                    kqq.pop(0)()
                    mb = fcount % 2
                    fcount += 1

                    def mkfin(pg=pg, acc=acc, acc_tk=acc_tk, mb=mb, t0=t0):
                        pieces = []
                        for hi_ in range(2):
                            a_ = acc[hi_]
                            for q4 in range(4):
                                cs = slice(512 * q4, 512 * q4 + 512)

                                def p_ln(a_=a_, hi_=hi_, cs=cs):
                                    if pg["a"]:
                                        hq = pg["hq"] + hi_
                                        self.act(lnd[64:128, cs], a_[64:128, cs], AF.Ln, (acc_tk[hi_], sk_tk), (fin_tk,), bias=sk[64:128, hq:hq + 1])
                                    else:
                                        self.act(lnd[64:128, cs], a_[64:128, cs], AF.Ln, (acc_tk[hi_],), (fin_tk,))

                                def p_exp(cs=cs):
                                    self.act(rcp[0:64, cs], lnd[64:128, cs], AF.Exp, (fin_tk,), (fin_tk,), scale=-1.0)

                                def p_mul(a_=a_, hi_=hi_, cs=cs):
                                    self.tt("dve", mst[mb][64 * hi_:64 * hi_ + 64, cs], a_[0:64, cs], rcp[0:64, cs], ALU.mult,
                                            (acc_tk[hi_], fin_tk), (mst_tk[mb],))
                                pieces += [p_ln, p_exp, p_mul]
                        pieces.append(lambda: self.dma(self.mx_d[pg["s"], pg["o0"]:pg["o0"] + 128, t0:t0 + 2048], mst[mb][:], (mst_tk[mb],), ()))
                        return pieces

                    pend.append(lambda pcs=mkfin(): finq.extend(pcs))
            while pend:
                pend.pop(0)()
            while finq:
                finq.pop(0)()
            while pend:
                pend.pop(0)()

    def phase2c(self, l, xin, sh):
        W = self.W
        ctk = self.const_tk
        with ExitStack() as es:
            A = self.A
            wo, wo_tk, diag, dg_tk = sh["wo"], sh["wo_tk"], sh["diag"], sh["dg_tk"]
            g1b = [A(es, "g1b", [128, D], F32) for _ in range(2)]
            g_tk = [Tk(), Tk()]
            ub = [A(es, "ub", [128, 3, 542], BF16) for _ in range(3)]
            ub_tk = [Tk() for _ in range(3)]
            mix = [A(es, "mix", [128, 8, 512], BF16) for _ in range(4)]
            mixa_tk = [Tk() for _ in range(4)]
            mixc_tk = [Tk() for _ in range(4)]
            ysb = [A(es, "ysb", [128, 3, 512], F32) for _ in range(2)]
            ysb_tk = [Tk(), Tk()]
            ytok = [A(es, "ytok", [128, 384], F32) for _ in range(2)]
            ytok_tk = [Tk(), Tk()]
            yn4 = A(es, "yn4", [128, 4, 384], F32)
            yn_tk = [Tk() for _ in range(4)]
            bst = [A(es, "bst", [128, 12], F32) for _ in range(2)]
            bst_tk = [Tk(), Tk()]
            xt = [A(es, "xt", [128, D], F32) for _ in range(4)]
            xt_tk = [Tk() for _ in range(4)]
            tmpo = [A(es, "tmpo", [128, 512], F32) for _ in range(2)]
            tmpo_tk = [Tk(), Tk()]
            junk = A(es, "junk", [128, 384], BF16)
            junk_tk = Tk()
            for s in range(self.nseq):
                row = (l * self.nseq + s) * 2
                if s < 2:
                    self.dma(g1b[s % 2][:], self.gs_d[row:row + 1, :].to_broadcast([128, D]), (), (g_tk[s % 2],))
            for b in range(3):
                self.memset("pool", ub[b][:], 0.0, (ub_tk[b],))
            chunks = [(s, c) for s in range(self.nseq) for c in range(8)]
            cnt = {"x": 0, "t": 0}

            def stL(i):
                s, c = chunks[i]
                c0 = c * 512
                b = i % 3
                mb = i % 4
                lo, hi = max(c0 - 15, 0), min(c0 + 527, S)
                if c == 7:
                    self.memset("pool", ub[b][:, :, 527:542], 0.0, (ub_tk[b],))
                if c == 0:
                    self.memset("pool", ub[b][:, :, 0:15], 0.0, (ub_tk[b],))
                self.dma(ub[b][:, :, lo - (c0 - 15):hi - (c0 - 15)], self.u_d[s, :, lo:hi].rearrange("(i p) t -> p i t", p=128),
                         (), (ub_tk[b],))
                self.dma(mix[mb][:, 0:5, :], self.mx_d[s, :, c0:c0 + 512].rearrange("(i p) t -> p i t", p=128), (), (mixa_tk[mb],))

            def st1(i):
                b = i % 3
                yb = i % 2
                for cb in range(3):
                    ps, _, pt = self.ps()
                    for k in range(31):
                        self.mm(ps[:, :], diag[:, cb * 31 + k, :], ub[b][:, cb, k:k + 512], k == 0, k == 30, (dg_tk, ub_tk[b]), (pt,))
                    self.act(ysb[yb][:, cb, :], ps[:, :], AF.Identity, (pt, ctk), (ysb_tk[yb],), bias=self.cvec[:, l, 0, cb:cb + 1])

            def st2(i):
                b = i % 2
                for t4 in range(4):
                    tb = cnt["t"] % 2
                    cnt["t"] += 1
                    ps, _, pt = self.ps()
                    for cb in range(3):
                        self.tr(ps[:, cb * 128:(cb + 1) * 128], ysb[b][:, cb, t4 * 128:(t4 + 1) * 128], self.ident[:], (ysb_tk[b], ctk), (pt,))
                    B = bst[tb]
                    Y = ytok[tb]
                    self.act(Y[:], ps[:, 0:384], AF.Copy, (pt,), (ytok_tk[tb], bst_tk[tb]), accum_out=B[:, 0:1])
                    self.ts("dve", B[:, 1:2], B[:, 0:1], -1.0 / 384.0, None, ALU.mult, None, (bst_tk[tb],), (bst_tk[tb],))
                    self.ts("dve", Y[:], Y[:], B[:, 1:2], None, ALU.add, None, (ytok_tk[tb], bst_tk[tb]), (ytok_tk[tb],))
                    self.act(junk[:], Y[:], AF.Square, (ytok_tk[tb],), (junk_tk, bst_tk[tb]), scale=1.0 / math.sqrt(384.0), accum_out=B[:, 2:3])
                    self.ts("pool", B[:, 3:4], B[:, 2:3], EPS, None, ALU.add, None, (bst_tk[tb],), (bst_tk[tb],))
                    self.tt("pool", B[:, 4:5], B[:, 3:4], self.mhalf[:, :], ALU.pow, (bst_tk[tb], ctk), (bst_tk[tb],))
                    self.act(yn4[:, t4, :], Y[:], AF.Copy, (ytok_tk[tb], bst_tk[tb]), (yn_tk[t4],), scale=B[:, 4:5])

            def st3(i):
                mb = i % 4
                for cb in range(3):
                    ps, _, pt = self.ps()
                    for t4 in range(4):
                        self.tr(ps[:, t4 * 128:(t4 + 1) * 128], yn4[:, t4, cb * 128:(cb + 1) * 128], self.ident[:], (yn_tk[t4], ctk), (pt,))
                    self.act(mix[mb][:, 5 + cb, :], ps[:, :], AF.Silu, (pt, ctk), (mixc_tk[mb],),
                             scale=self.cvec[:, l, 1, cb:cb + 1], bias=self.cvec[:, l, 2, cb:cb + 1])

            def stXl(i):
                s, c = chunks[i]
                for t4 in range(4):
                    r0 = c * 512 + t4 * 128
                    self.dma(xt[t4][:], xin[s][r0:r0 + 128, :], (), (xt_tk[t4],))

            def st4(i):
                s, c = chunks[i]
                c0 = c * 512
                mb = i % 4
                if c == 0 and s >= 2:
                    row = (l * self.nseq + s) * 2
                    self.dma(g1b[s % 2][:], self.gs_d[row:row + 1, :].to_broadcast([128, D]), (), (g_tk[s % 2],))
                G, Gt = g1b[s % 2], g_tk[s % 2]
                for t4 in range(4):
                    xb_ = t4
                    r0 = c0 + t4 * 128
                    for hh in range(2):
                        ps, _, pt = self.ps()
                        for k in range(8):
                            self.mm(ps[:, :], mix[mb][:, k, t4 * 128:(t4 + 1) * 128], wo[:, k, hh * 512:(hh + 1) * 512], k == 0, k == 7,
                                    (mixa_tk[mb], mixc_tk[mb], wo_tk[k]), (pt,))
                        self.tt("dve", tmpo[hh][:], ps[:, :], G[:, hh * 512:(hh + 1) * 512], ALU.mult, (pt, Gt), (tmpo_tk[hh],))
                        self.tt("pool", xt[xb_][:, hh * 512:(hh + 1) * 512], tmpo[hh][:], xt[xb_][:, hh * 512:(hh + 1) * 512], ALU.add,
                                (tmpo_tk[hh], xt_tk[xb_]), (xt_tk[xb_],))
                    self.dma(self.xa_d[s, r0:r0 + 128, :], xt[xb_][:], (xt_tk[xb_],), ())

            n = len(chunks)
            stL(0)
            if n > 1:
                stL(1)
            st1(0)
            for i in range(n):
                if i + 2 < n:
                    stL(i + 2)
                self.bg_step(2)
                if i + 1 < n:
                    st1(i + 1)
                self.bg_step(2)
                if i >= 1:
                    stXl(i - 1)
                st2(i)
                if i >= 1:
                    st4(i - 1)
                st3(i)
            stXl(n - 1)
            st4(n - 1)

    def phase3(self, l, xout, sh):
        W = self.W
        ctk, abk = self.const_tk, self.ab_tk
        CH = 256
        NB = 4
        with ExitStack() as es:
            A = self.A
            wu = A(es, "wu", [128, 8, 2 * FF], BF16)
            wu_tk = [Tk() for _ in range(2 * NFB)]
            wd, wd_tk = sh["wd"], sh["wd_tk"]
            g2b = A(es, "g2b", [128, D], F32)
            g_tk = Tk()
            xt = [A(es, "xt", [128, D], F32) for _ in range(3)]
            xt_tk = [Tk() for _ in range(3)]
            junk = A(es, "junk", [128, D], BF16)
            junk_tk = Tk()
            st = [A(es, "st", [128, 4], F32) for _ in range(3)]
            st_tk = [Tk() for _ in range(3)]
            h2 = [A(es, "h2", [128, 8, CH + 2], BF16) for _ in range(2)]
            h2_tk = [Tk(), Tk()]
            hal = A(es, "hal", [128, 16], F32)
            hal_tk = Tk()
            t0b = [A(es, "t0b", [128, CH], F32) for _ in range(NB)]
            t0_tk = [Tk() for _ in range(NB)]
            aT = A(es, "aT", [128, NFB, CH], BF16)
            aT_tk = [Tk() for _ in range(NFB)]
            xr = [A(es, "xr", [128, D], F32) for _ in range(2)]
            xr_tk = [Tk(), Tk()]
            tmpo = [A(es, "tmpo", [128, 512], F32) for _ in range(2)]
            tmpo_tk = [Tk(), Tk()]
            wst = list(xr) + [A(es, "wst", [128, D], F32) for _ in range(2)]
            wst_tk = list(xr_tk) + [Tk(), Tk()]
            row = (l * self.nseq) * 2 + 1
            self.dma(g2b[:], self.gs_d[row:row + 1, :].to_broadcast([128, D]), (), (g_tk,))
            nch = S // CH
            chunks = [(s, c) for s in range(self.nseq) for c in range(nch)]
            cnt = {"b": 0, "r": 0}

            def stPNd(i):
                s, c = chunks[i]
                c0 = c * CH
                xa = self.xa_d[s]
                for ti in range(2):
                    r0 = c0 + ti * 128
                    self.dma(xt[ti][:], xa[r0:r0 + 128, :], (), (xt_tk[ti],))
                has_l, has_r = c > 0, c < nch - 1
                if has_l and has_r:
                    self.dma(xt[2][0:2, :], xa[c0 - 1:c0 + CH + 1:CH + 1, :], (), (xt_tk[2],))
                elif has_r:
                    self.dma(xt[2][0:2, :], xa[c0 + CH - 1:c0 + CH + 1, :], (), (xt_tk[2],))
                else:
                    self.dma(xt[2][0:2, :], xa[c0 - 1:c0 + 1, :], (), (xt_tk[2],))

            def stPNc_ops(i):
                ops = []
                for ti in range(3):
                    P = 128 if ti < 2 else 2
                    X, Xt, ST, STt = xt[ti], xt_tk[ti], st[ti], st_tk[ti]
                    ops.append(lambda X=X, Xt=Xt, ST=ST, STt=STt, P=P: self.act(junk[0:P, :], X[0:P, :], AF.Square, (Xt,), (junk_tk, STt),
                                                                               scale=1.0 / 32.0, accum_out=ST[0:P, 0:1]))
                    ops.append(lambda ST=ST, STt=STt, P=P: self.ts("pool", ST[0:P, 1:2], ST[0:P, 0:1], EPS, None, ALU.add, None, (STt,), (STt,)))
                    ops.append(lambda ST=ST, STt=STt, P=P: self.tt("pool", ST[0:P, 2:3], ST[0:P, 1:2], self.mhalf[0:P, :], ALU.pow, (STt, ctk), (STt,)))
                    ops.append(lambda X=X, Xt=Xt, ST=ST, STt=STt, P=P: self.act(X[0:P, :], X[0:P, :], AF.Copy, (Xt, STt), (Xt,), scale=ST[0:P, 2:3]))
                return ops

            def stPNc(i):
                for o in stPNc_ops(i):
                    o()

            def stPT(i):
                s, c = chunks[i]
                ab = self.ab[:, s]
                H, Ht = h2[i % 2], h2_tk[i % 2]
                for ti, colo in ((0, 1), (1, 129)):
                    for half in range(2):
                        ps, _, pt = self.psx(6, 8)
                        for kk in range(4):
                            k = half * 4 + kk
                            self.tr(ps[:, kk * 128:(kk + 1) * 128], xt[ti][:, k * 128:(k + 1) * 128], self.ident[:], (xt_tk[ti], ctk), (pt,))
                        for kk in range(4):
                            k = half * 4 + kk
                            self.act(H[:, k, colo:colo + 128], ps[:, kk * 128:(kk + 1) * 128], AF.Identity, (pt, abk), (Ht,),
                                     scale=ab[:, 2, k:k + 1], bias=ab[:, 3, k:k + 1])
                ps, _, pt = self.psx(6, 8)
                for k in range(8):
                    self.tr(ps[:, k * 2:k * 2 + 2], xt[2][0:2, k * 128:(k + 1) * 128], self.ident[0:2, 0:2], (xt_tk[2], ctk), (pt,))
                hv3 = hal[:, 0:16].rearrange("p (k t) -> p k t", t=2)
                self.tt("dve", hv3, ps[:, 0:16].rearrange("p (k t) -> p k t", t=2),
                        ab[:, 2, :].unsqueeze(2).to_broadcast([128, 8, 2]), ALU.mult, (pt, abk), (hal_tk,))
                self.tt("dve", H[:, :, 0:CH + 2:CH + 1], hv3, ab[:, 3, :].unsqueeze(2).to_broadcast([128, 8, 2]), ALU.add, (hal_tk, abk), (Ht,))
                if c == 0:
                    self.memset("dve", H[:, :, 0:1], 0.0, (Ht,))
                if c == nch - 1:
                    self.memset("dve", H[:, :, CH + 1:CH + 2], 0.0, (Ht,))

            def stA(i, fb):
                H, Ht = h2[i % 2], h2_tk[i % 2]
                bb = cnt["b"] % NB
                cnt["b"] += 1
                psg, _, ptg = self.psx(0, 6)
                for k in range(8):
                    self.mm(psg[:, 0:CH + 2], wu[:, k, fb * 128:(fb + 1) * 128], H[:, k, :], k == 0, k == 7, (Ht, wu_tk[fb]), (ptg,))
                psu, _, ptu = self.psx(0, 6)
                for k in range(8):
                    self.mm(psu[:, 0:CH], wu[:, k, FF + fb * 128:FF + (fb + 1) * 128], H[:, k, 1:CH + 1], k == 0, k == 7,
                            (Ht, wu_tk[NFB + fb]), (ptu,))
                T0, T0t = t0b[bb], t0_tk[bb]
                self.act(T0[:], psg[:, 0:CH], AF.Identity, (ptg, ctk), (T0t,), scale=self.wf[:, l, fb, 0:1], bias=self.bf[:, l, fb:fb + 1])
                self.stt(T0[:], psg[:, 1:CH + 1], self.wf[:, l, fb, 1:2], T0[:], ALU.mult, ALU.add, (ptg, T0t, ctk), (T0t,))
                self.stt(T0[:], psg[:, 2:CH + 2], self.wf[:, l, fb, 2:3], T0[:], ALU.mult, ALU.add, (ptg, T0t, ctk), (T0t,))
                return (bb, psu, ptu)

            def stB(i, fb, info):
                bb, psu, ptu = info
                self.act(t0b[bb][:], t0b[bb][:], AF.Silu, (t0_tk[bb],), (t0_tk[bb],))
                self.tt("dve", aT[:, fb, :], psu[:, 0:CH], t0b[bb][:], ALU.mult, (ptu, t0_tk[bb]), (aT_tk[fb],))

            def stDl(i):
                s, c = chunks[i]
                for t2 in range(CH // 128):
                    r0 = c * CH + t2 * 128
                    self.dma(xr[t2][:], self.xa_d[s, r0:r0 + 128, :], (), (xr_tk[t2],))

            def stD(i):
                s, c = chunks[i]
                c0 = c * CH
                if c == 0 and s >= 1:
                    row = (l * self.nseq + s) * 2 + 1
                    self.dma(g2b[:], self.gs_d[row:row + 1, :].to_broadcast([128, D]), (), (g_tk,))
                for t2 in range(CH // 128):
                    rb = t2
                    r0 = c0 + t2 * 128
                    for hh in range(2):
                        ps, _, pt = self.psx(6, 8)
                        for fb in range(NFB):
                            self.mm(ps[:, :], aT[:, fb, t2 * 128:(t2 + 1) * 128], wd[:, fb, hh * 512:(hh + 1) * 512], fb == 0, fb == NFB - 1,
                                    (aT_tk[fb], wd_tk[fb]), (pt,))
                        self.tt("dve", tmpo[hh][:], ps[:, :], g2b[:, hh * 512:(hh + 1) * 512], ALU.mult, (pt, g_tk), (tmpo_tk[hh],))
                        self.tt("pool", xr[rb][:, hh * 512:(hh + 1) * 512], tmpo[hh][:], xr[rb][:, hh * 512:(hh + 1) * 512], ALU.add,
                                (tmpo_tk[hh], xr_tk[rb]), (xr_tk[rb],))
                    self.dma(xout[s][r0:r0 + 128, :], xr[rb][:], (xr_tk[rb],), ())

            n = len(chunks)
            pend = []
            stPNd(0)
            stPNc(0)
            stPT(0)
            pc = 0
            for fp in range(NFB // 2):
                for part in range(2):
                    for kh in range(2):
                        b = pc % 4
                        pc += 1
                        col = part * FF + fp * 256
                        self.dma(wst[b][:].rearrange("p (k c) -> p k c", c=256),
                                 W["w_up"][l][kh * 512:(kh + 1) * 512, col:col + 256].rearrange("(k p) c -> p k c", p=128), (), (wst_tk[b],))
                        self.cp("act" if pc % 2 else "pool", wu[:, 4 * kh:4 * kh + 4, col:col + 256],
                                wst[b][:].rearrange("p (k c) -> p k c", c=256), (wst_tk[b],),
                                (wu_tk[part * NFB + 2 * fp], wu_tk[part * NFB + 2 * fp + 1]))
            for i in range(n):
                for fb in range(NFB):
                    info = stA(i, fb)
                    pend.append(lambda i=i, fb=fb, info=info: stB(i, fb, info))
                    if fb == NFB - 1:
                        pend.append(lambda i=i: stD(i))
                    while len(pend) > 2:
                        pend.pop(0)()
                    if fb == 12:
                        stDl(i)
                    if i + 1 < n:
                        if fb == 1:
                            stPNd(i + 1)
                            pnops = stPNc_ops(i + 1)
                        if 5 <= fb < 17:
                            pnops[fb - 5]()
                        if fb == NFB - 1:
                            stPT(i + 1)
            while pend:
                pend.pop(0)()


def sch_emit_engine(sch, eng, h, sems, dsems):
    if not getattr(sch, "_prepared", False):
        for e in sch.ENGS:
            c = 0
            for ent in sch.q[e]:
                if ent[2] is True:
                    c += 1
                ent[3] = c
        sch._prepared = True
    seen = {}
    for fn, deps, sig, _ in sch.q[eng]:
        for k, v in deps.items():
            if isinstance(k, tuple):
                sem, tgt = dsems[k[1]], v
            else:
                ent = sch.q[k][v]
                sem, tgt = sems[k], ent[3]
            if seen.get(k, 0) >= tgt:
                continue
            seen[k] = tgt
            h.wait_ge(sem, tgt)
        ins = fn(h)
        if sig is True:
            ins.then_inc(sems[eng], 1)
        elif isinstance(sig, tuple):
            ins.then_inc(dsems[sig[1]], 16)
    for k, v in sch.pending[eng].items():
        if isinstance(k, tuple):
            sem, tgt = dsems[k[1]], v
        else:
            sem, tgt = sems[k], sch.q[k][v][3]
        if seen.get(k, 0) >= tgt:
            continue
        h.wait_ge(sem, tgt)


def rope_tables():
    half = 32
    inv = (1.0 / (np.float32(10000.0) ** (np.arange(half, dtype=np.float32) / np.float32(half)))).astype(np.float32)
    ang = (np.arange(S, dtype=np.float32)[:, None] * inv[None, :]).astype(np.float32)
    c = np.cos(ang).astype(np.float32)
    s_ = np.sin(ang).astype(np.float32)
    return np.concatenate([c, c], axis=1), np.concatenate([s_, s_], axis=1)


_CACHE = {}


def kernel(**inputs):
    f = lambda a: np.ascontiguousarray(np.asarray(a), dtype=np.float32)
    xp, xs = f(inputs["x_prompt"]), f(inputs["x_sample"])
    cp, cs = f(inputs["c_prompt"]), f(inputs["c_sample"])
    wnames = ["norm1_g", "norm2_g", "w_mod", "b_mod", "w_in", "qn_a", "kn_a", "sink_a", "qn_b", "kn_b", "conv_c_w",
              "conv_c_b", "ln_c_g", "ln_c_b", "w_out", "w_up", "conv_f_w", "conv_f_b", "w_down"]
    wts = {n: f(inputs[n]) for n in wnames}
    cc, ss = rope_tables()
    if "nc" not in _CACHE:
        _CACHE["nc"] = Prog().build()
    nc = _CACHE["nc"]
    in_maps = []
    for i in range(8):
        m = dict(wts)
        m["x"] = np.ascontiguousarray(np.stack([xp[i], xs[2 * i], xs[2 * i + 1]]))
        m["c"] = np.ascontiguousarray(np.stack([cp[i], cs[2 * i], cs[2 * i + 1]]))
        m["ropec"] = cc
        m["ropes"] = ss
        in_maps.append(m)
    res = run_bass_kernel_spmd(nc, in_maps, core_ids=list(range(8)))
    yp = np.empty_like(xp)
    ys = np.empty_like(xs)
    for i in range(8):
        y = np.asarray(res.results[i]["y"], dtype=np.float32)
        yp[i] = y[0]
        ys[2 * i] = y[1]
        ys[2 * i + 1] = y[2]
    return (yp, ys)
```

```python
import math
from contextlib import ExitStack
import numpy as np
import concourse.bass as bass
import concourse.mybir as mybir
from concourse.bass_utils import run_bass_kernel_spmd

F32 = mybir.dt.float32
BF16 = mybir.dt.bfloat16
ALU = mybir.AluOpType
AF = mybir.ActivationFunctionType
AX = mybir.AxisListType

S = 4096
D = 1024
DEPTH = 2
NSEQ = 3
INW = 2432
FF = 2816
NFB = 22
EPS = 1e-6
PADK = 1024
SRC_COL = [0, 128, 256, 512, 640, 768, 896, 1024, 1152, 384, 1280, 1408, 1536,
           1664, 1792, 1920, 2048, 2176, 2304]


class Tk:
    __slots__ = ("w", "r", "excl")

    def __init__(self, excl=False):
        self.w = {}
        self.r = {}
        self.excl = excl


class Sched:
    ENGS = ("pe", "act", "dve", "pool", "sp")
    NDSEM = 24

    def __init__(self):
        self.q = {e: [] for e in self.ENGS}
        self.dcount = [0] * self.NDSEM
        self.dn = 0
        self.pending = {e: {} for e in self.ENGS}

    def _deps(self, eng, rd, wr):
        deps = self.pending[eng]
        self.pending[eng] = {}

        def need(k, v):
            if deps.get(k, -1) < v:
                deps[k] = v

        for t in rd:
            for k, v in t.w.items():
                if k == eng and eng in ("pe", "sp"):
                    continue
                need(k, v)
            if t.excl:
                for k, v in t.r.items():
                    if k != eng:
                        need(k, v)
        for t in wr:
            for k, v in t.w.items():
                if k != eng:
                    need(k, v)
            for k, v in t.r.items():
                if k != eng:
                    need(k, v)
        for k, v in deps.items():
            if not isinstance(k, tuple):
                self.q[k][v][2] = True
        return deps

    def op(self, eng, fn, rd=(), wr=()):
        deps = self._deps(eng, rd, wr)
        idx = len(self.q[eng])
        self.q[eng].append([fn, deps, False, 0])
        for t in rd:
            t.r[eng] = idx
        for t in wr:
            t.w = {eng: idx}
            t.r = {}

    def dma(self, fn, rd=(), wr=(), q="sp"):
        i = self.dn % self.NDSEM
        self.dn += 1
        deps = self._deps(q, rd, wr)
        key = ("d", i)
        prev = self.dcount[i] * 16
        if prev > 0 and deps.get(key, -1) < prev:
            deps[key] = prev
        self.dcount[i] += 1
        val = self.dcount[i] * 16
        self.q[q].append([fn, deps, key, 0])
        for t in rd:
            t.r[key] = val
        for t in wr:
            t.w = {key: val}
            t.r = {}

    def barrier(self):
        last = {e: len(self.q[e]) - 1 for e in ("pe", "act", "dve", "pool")}
        for e in self.ENGS:
            p = self.pending[e]
            for k, v in last.items():
                if k != e and v >= 0 and p.get(k, -1) < v:
                    p[k] = v
            for i in range(self.NDSEM):
                if self.dcount[i] > 0:
                    p[("d", i)] = max(p.get(("d", i), 0), self.dcount[i] * 16)

    def finalize(self):
        for e in self.ENGS:
            for k, v in self.pending[e].items():
                if not isinstance(k, tuple):
                    self.q[k][v][2] = True


class Prog:
    def __init__(self, nseq=NSEQ, depth=DEPTH, debug=False):
        self.nseq = nseq
        self.depth = depth
        self.debug = debug
        self.nc = bass.Bass("TRN2", target_bir_lowering=False)
        self.s = Sched()
        self.uid = 0
        self.psi = 0
        self.psc = {}
        self.bg = []

    def name(self, n):
        self.uid += 1
        return "%s_%d" % (n, self.uid)

    def A(self, es, n, shape, dt):
        return es.enter_context(self.nc.sbuf_tensor(self.name(n), shape, dt))

    def ps(self):
        i = self.psi % 8
        self.psi += 1
        return self.PS[i], self.PSB[i], self.PST[i]

    def bg_step(self, n=1):
        for _ in range(n):
            if self.bg:
                self.bg.pop(0)()

    def bg_flush(self):
        while self.bg:
            self.bg.pop(0)()

    def psx(self, lo, hi):
        c = self.psc.get((lo, hi), 0)
        self.psc[(lo, hi)] = c + 1
        i = lo + c % (hi - lo)
        return self.PS[i], self.PSB[i], self.PST[i]

    def act(self, out, in_, func, rd, wr, **kw):
        self.s.op("act", lambda e: e.activation(out=out, in_=in_, func=func, **kw), rd, wr)

    def tt(self, eng, out, in0, in1, op, rd, wr):
        self.s.op(eng, lambda e: e.tensor_tensor(out=out, in0=in0, in1=in1, op=op), rd, wr)

    def ts(self, eng, out, in0, s1, s2, op0, op1, rd, wr):
        if s2 is None:
            self.s.op(eng, lambda e: e.tensor_scalar(out=out, in0=in0, scalar1=s1, scalar2=None, op0=op0), rd, wr)
        else:
            self.s.op(eng, lambda e: e.tensor_scalar(out=out, in0=in0, scalar1=s1, scalar2=s2, op0=op0, op1=op1), rd, wr)

    def stt(self, out, in0, scalar, in1, op0, op1, rd, wr):
        self.s.op("dve", lambda e: e.scalar_tensor_tensor(out=out, in0=in0, scalar=scalar, in1=in1, op0=op0, op1=op1), rd, wr)

    def cp(self, eng, out, in_, rd, wr):
        if eng == "act":
            self.s.op("act", lambda e: e.activation(out=out, in_=in_, func=AF.Copy), rd, wr)
        else:
            self.s.op(eng, lambda e: e.tensor_copy(out=out, in_=in_), rd, wr)

    def memset(self, eng, ap, val, wr):
        self.s.op(eng, lambda e: e.memset(ap, val), (), wr)

    def mm(self, out, lhsT, rhs, start, stop, rd, wr):
        self.s.op("pe", lambda e: e.matmul(out, lhsT=lhsT, rhs=rhs, start=start, stop=stop), rd, wr)

    def tr(self, out, in_, ident, rd, wr):
        self.s.op("pe", lambda e: e.transpose(out=out, in_=in_, identity=ident), rd, wr)

    def dma(self, out, in_, rd, wr, q="sp"):
        self.s.dma(lambda e: e.dma_start(out=out, in_=in_), rd, wr, q=q)

    def build(self):
        nc = self.nc
        ns, dp = self.nseq, self.depth
        dt = nc.dram_tensor
        self.x_d = dt("x", [ns, S, D], F32, kind="ExternalInput").ap()
        self.c_d = dt("c", [ns, D], F32, kind="ExternalInput").ap()
        self.cc_d = dt("ropec", [S, 64], F32, kind="ExternalInput").ap()
        self.ss_d = dt("ropes", [S, 64], F32, kind="ExternalInput").ap()
        W = {}
        for nm, shp in (("norm1_g", [DEPTH, D]), ("norm2_g", [DEPTH, D]), ("w_mod", [DEPTH, D, 6 * D]),
                        ("b_mod", [DEPTH, 6 * D]), ("w_in", [DEPTH, D, INW]), ("qn_a", [DEPTH, 64]),
                        ("kn_a", [DEPTH, 64]), ("sink_a", [DEPTH, 4]), ("qn_b", [DEPTH, 64]),
                        ("kn_b", [DEPTH, 64]), ("conv_c_w", [DEPTH, 31, 384]), ("conv_c_b", [DEPTH, 384]),
                        ("ln_c_g", [DEPTH, 384]), ("ln_c_b", [DEPTH, 384]), ("w_out", [DEPTH, D, D]),
                        ("w_up", [DEPTH, D, 2 * FF]), ("conv_f_w", [DEPTH, 3, FF]), ("conv_f_b", [DEPTH, FF]),
                        ("w_down", [DEPTH, FF, D])):
            W[nm] = dt(nm, shp, F32, kind="ExternalInput").ap()
        self.W = W
        self.y_d = dt("y", [ns, S, D], F32, kind="ExternalOutput").ap()
        kd = "ExternalOutput" if self.debug else "Internal"
        self.xa_d = dt("xa", [ns, S, D], F32, kind=kd).ap()
        self.xb_d = dt("xb", [ns, S, D], F32, kind="Internal").ap()
        self.qk_d = dt("qkT", [ns, 1152, S], BF16, kind=kd).ap()
        self.vs_d = dt("vsc", [ns, S, 8, 128], BF16, kind=kd).ap()
        self.u_d = dt("uT", [ns, 384, S], BF16, kind=kd).ap()
        self.mx_d = dt("mixA", [ns, 640, S], BF16, kind=kd).ap()
        self.gs_d = dt("gsc", [DEPTH * ns * 2, D], F32, kind="Internal").ap()

        with ExitStack() as top:
            self.PS, self.PSB, self.PST = [], [], []
            self.psall = top.enter_context(nc.psum_tensor(self.name("psall"), [128, 8 * 512], F32))
            for i in range(8):
                p = self.psall[:, i * 512:(i + 1) * 512]
                self.PS.append(p)
                self.PSB.append(p.bitcast(BF16))
                self.PST.append(Tk(excl=True))
            sems = {e: top.enter_context(nc.semaphore(self.name("s" + e))) for e in ("pe", "act", "dve", "pool")}
            dsems = [top.enter_context(nc.semaphore(self.name("d"))) for _ in range(Sched.NDSEM)]
            block = top.enter_context(nc.Block())
            self.persist(top)
            self.prologue()
            for l in range(dp):
                xin = [self.x_d[s] if l == 0 else self.xb_d[s] for s in range(ns)]
                xout = [self.y_d[s] if l == dp - 1 else self.xb_d[s] for s in range(ns)]
                self.layer(l, xin, xout)
            self.s.barrier()
            self.s.finalize()
            sch = self.s

            def mk(eng_name):
                def f(h):
                    sch_emit_engine(sch, eng_name, h, sems, dsems)
                return f

            for bname, en in (("tensor", "pe"), ("scalar", "act"), ("vector", "dve"), ("gpsimd", "pool"), ("sync", "sp")):
                getattr(block, bname)(mk(en))
        return nc

    def persist(self, es):
        A = self.A
        self.ident = A(es, "ident", [128, 128], F32)
        self.identb = A(es, "identb", [128, 128], BF16)
        self.epsT = A(es, "eps", [128, 1], F32)
        self.mhalf = A(es, "mhalf", [128, 1], F32)
        self.maskA = A(es, "maskA", [128, 384], BF16)
        self.maskB2 = [A(es, "maskB2", [128, 512], BF16) for _ in range(3)]
        self.vecs = A(es, "vecs", [128, DEPTH, 2, 8], F32)
        self.cvec = A(es, "cvec", [128, DEPTH, 3, 3], F32)
        self.wc = A(es, "wc", [128, DEPTH, 3, 31], F32)
        self.wf = A(es, "wf", [128, DEPTH, NFB, 3], F32)
        self.bf = A(es, "bf", [128, DEPTH, NFB], F32)
        self.modT = A(es, "modT", [128, DEPTH, 48, self.nseq], F32)
        self.ab = A(es, "ab", [128, self.nseq, 4, 8], F32)
        self.const_tk = Tk()
        self.ab_tk = Tk()

    def load_T(self, es_stage, stage, stage_tk, dst, src, R, C):
        nb = C // 128
        Rp = R + (R % 2)
        self.dma(stage[0:R, 0:C], src, (), (stage_tk,))
        ps, _, pt = self.ps()
        for j in range(nb):
            self.tr(ps[:, j * Rp:(j + 1) * Rp], stage[0:Rp, j * 128:(j + 1) * 128], self.ident[0:Rp, 0:Rp],
                    (stage_tk, self.const_tk), (pt,))
        self.cp("dve", dst, ps[:, 0:nb * Rp].rearrange("p (j r) -> p j r", r=Rp)[:, :, 0:R], (pt,), (self.const_tk,))

    def prologue(self):
        nc, W, ns = self.nc, self.W, self.nseq
        ctk = self.const_tk
        self.memset("pool", self.ident[:], 1.0, (ctk,))
        self.s.op("pool", lambda e: e.affine_select(out=self.ident[:], in_=self.ident[:], pattern=[[-1, 128]],
                                                    compare_op=ALU.is_equal, fill=0.0, base=0, channel_multiplier=1),
                  (ctk,), (ctk,))
        self.cp("pool", self.identb[:], self.ident[:], (ctk,), (ctk,))
        self.memset("pool", self.epsT[:], EPS, (ctk,))
        self.memset("pool", self.mhalf[:], -0.5, (ctk,))
        mA = self.maskA
        self.memset("pool", mA[:], 1.0, (ctk,))

        def sel(ap, pattern, cm, base):
            self.s.op("pool", lambda e: e.affine_select(out=ap, in_=ap, pattern=pattern, compare_op=ALU.is_ge,
                                                        fill=0.0, base=base, channel_multiplier=cm), (ctk,), (ctk,))
        sel(mA[:, 0:128], [[-1, 128]], 1, 0)
        sel(mA[:, 256:384], [[1, 128]], -1, 0)
        for v in range(3):
            mB = self.maskB2[v]
            self.memset("pool", mB[:], 1.0, (ctk,))
            for h in range(2):
                sel(mB[:, h * 256:h * 256 + 128], [[-1, 128]], 1, 0)
                sel(mB[:, h * 256 + 128:h * 256 + 256], [[1, 128]], -1, 0)
        for h in range(2):
            sel(self.maskB2[1][:, h * 256:h * 256 + 128], [[0, 128]], 1, -64)
            sel(self.maskB2[2][:, h * 256 + 128:h * 256 + 256], [[0, 128]], -1, 63)

        with ExitStack() as es:
            A = self.A
            stage = A(es, "stage", [128, 6 * D], F32)
            stk = Tk()
            self.memset("dve", stage[:], 0.0, (stk,))
            cT = A(es, "cT", [128, 8, ns], F32)
            scT = A(es, "scT", [128, 8, ns], BF16)
            screp = A(es, "screp", [128, 8, ns, 128], BF16)
            bmT = A(es, "bmT", [128, 48], F32)
            wm = A(es, "wm", [128, 8, 6 * D], BF16)
            wst = [A(es, "wst", [128, 8, 512], F32) for _ in range(4)]
            wst_tk = [Tk() for _ in range(4)]
            wm_tk = [Tk() for _ in range(48)]
            grow = A(es, "grow", [128, 512], F32)
            brow = A(es, "brow", [1, 512], F32)
            g_tk, b_tk, sc_tk, bm_tk = Tk(), Tk(), Tk(), Tk()
            for l in range(self.depth):
                self.load_T(es, stage, stk, self.vecs[:, l, 0, :].unsqueeze(2), W["norm1_g"][l:l + 1, :], 1, D)
                self.load_T(es, stage, stk, self.vecs[:, l, 1, :].unsqueeze(2), W["norm2_g"][l:l + 1, :], 1, D)
                for wi, nm in enumerate(("conv_c_b", "ln_c_g", "ln_c_b")):
                    self.load_T(es, stage, stk, self.cvec[:, l, wi, :].unsqueeze(2), W[nm][l:l + 1, :], 1, 384)
                self.load_T(es, stage, stk, self.wc[:, l, :, :], W["conv_c_w"][l], 31, 384)
                self.load_T(es, stage, stk, self.wf[:, l, :, :], W["conv_f_w"][l], 3, FF)
                self.load_T(es, stage, stk, self.bf[:, l, :].unsqueeze(2), W["conv_f_b"][l:l + 1, :], 1, FF)
            self.load_T(es, stage, stk, cT[:], self.c_d, ns, D)
            self.act(scT[:], cT[:], AF.Silu, (ctk,), (sc_tk,))
            self.cp("dve", screp[:], scT[:].unsqueeze(3).to_broadcast([128, 8, ns, 128]), (sc_tk,), (sc_tk,))
            for l in range(self.depth):
                self.load_T(es, stage, stk, bmT[:].unsqueeze(2), W["b_mod"][l:l + 1, :], 1, 6 * D)
                for f4 in range(12):
                    b = f4 % 4
                    self.dma(wst[b][:], W["w_mod"][l][:, f4 * 512:(f4 + 1) * 512].rearrange("(k p) c -> p k c", p=128),
                             (), (wst_tk[b],))
                    self.cp("act" if f4 % 2 else "dve", wm[:, :, f4 * 512:(f4 + 1) * 512], wst[b][:], (wst_tk[b],),
                            tuple(wm_tk[4 * f4:4 * f4 + 4]))
                ps, _, pt = self.ps()
                for fb in range(48):
                    for k in range(8):
                        self.mm(ps[:, fb * ns:(fb + 1) * ns], wm[:, k, fb * 128:(fb + 1) * 128], scT[:, k, :],
                                k == 0, k == 7, (wm_tk[fb], sc_tk), (pt,))
                self.tt("dve", self.modT[:, l, :, :], ps[:, 0:48 * ns].rearrange("p (f s) -> p f s", s=ns),
                        bmT[:].unsqueeze(2).to_broadcast([128, 48, ns]), ALU.add, (pt, ctk), (ctk,))
                for s in range(ns):
                    for gi, cb in ((0, 2 * D), (1, 5 * D)):
                        for hh in range(2):
                            ps, _, pt = self.ps()
                            c0 = cb + hh * 512
                            for k in range(8):
                                self.mm(ps[:, :], screp[:, k, s, :], wm[:, k, c0:c0 + 512], k == 0, k == 7,
                                        tuple(wm_tk[c0 // 128:c0 // 128 + 4]) + (sc_tk,), (pt,))
                            self.dma(brow[:], W["b_mod"][l:l + 1, c0:c0 + 512], (), (b_tk,))
                            self.tt("dve", grow[0:1, :], ps[0:1, :], brow[:], ALU.add, (pt, b_tk), (g_tk,))
                            row = (l * ns + s) * 2 + gi
                            self.dma(self.gs_d[row:row + 1, hh * 512:(hh + 1) * 512], grow[0:1, :], (g_tk,), ())
        self.s.barrier()

    def layer(self, l, xin, xout):
        ctk, abk = self.const_tk, self.ab_tk
        m = self.modT
        for s in range(self.nseq):
            ab = self.ab[:, s]
            self.ts("dve", ab[:, 0, :], m[:, l, 8:16, s], 1.0, None, ALU.add, None, (ctk,), (abk,))
            self.tt("dve", ab[:, 0, :], ab[:, 0, :], self.vecs[:, l, 0, :], ALU.mult, (abk, ctk), (abk,))
            self.cp("dve", ab[:, 1, :], m[:, l, 0:8, s], (ctk,), (abk,))
            self.ts("dve", ab[:, 2, :], m[:, l, 32:40, s], 1.0, None, ALU.add, None, (ctk,), (abk,))
            self.tt("dve", ab[:, 2, :], ab[:, 2, :], self.vecs[:, l, 1, :], ALU.mult, (abk, ctk), (abk,))
            self.cp("dve", ab[:, 3, :], m[:, l, 24:32, s], (ctk,), (abk,))
        self.phase1(l, xin)
        self.s.barrier()
        if self.debug == 1:
            return
        W = self.W
        with ExitStack() as esX:
            A = self.A
            sh = {}
            sh["wd"] = A(esX, "wd", [128, NFB, D], BF16)
            sh["wd_tk"] = [Tk() for _ in range(NFB)]
            with ExitStack() as esY:
                wo = A(esY, "wo", [128, 8, D], BF16)
                wo_tk = [Tk() for _ in range(8)]
                wst = [A(esY, "wst", [128, 512], F32) for _ in range(4)]
                wst_tk = [Tk() for _ in range(4)]
                diag = A(esY, "diag", [128, 93, 128], BF16)
                dg_tk = Tk()
                sh.update(wo=wo, wo_tk=wo_tk, diag=diag, dg_tk=dg_tk)

                def bg_weights(pieces):
                    def dma_(j):
                        src, dst, tk = pieces[j]
                        self.dma(wst[j % 4][:], src, (), (wst_tk[j % 4],))

                    def slot(j):
                        src, dst, tk = pieces[j]
                        self.cp("pool", dst, wst[j % 4][:], (wst_tk[j % 4],), (tk,))
                        if j + 4 < len(pieces):
                            dma_(j + 4)
                    for j in range(min(4, len(pieces))):
                        dma_(j)
                    return [lambda j=j: slot(j) for j in range(len(pieces))]

                pcs = [(W["w_out"][l][k * 128:(k + 1) * 128, hh * 512:(hh + 1) * 512], wo[:, k, hh * 512:(hh + 1) * 512], wo_tk[k])
                       for k in range(8) for hh in range(2)]
                self.bg = bg_weights(pcs)
                for cb in range(3):
                    for k in range(31):
                        self.bg.append(lambda cb=cb, k=k: self.ts("pool", diag[:, cb * 31 + k, :], self.identb[:], self.wc[:, l, cb, k:k + 1], None,
                                                                ALU.mult, None, (ctk,), (dg_tk,)))
                self.phase2a(l)
                self.bg_flush()
                self.s.barrier()
                if self.debug == 2:
                    return
                self.bg_flush()
                pcs = [(W["w_down"][l][fb * 128:(fb + 1) * 128, hh * 512:(hh + 1) * 512], sh["wd"][:, fb, hh * 512:(hh + 1) * 512], sh["wd_tk"][fb])
                       for fb in range(NFB) for hh in range(2)]
                self.bg = bg_weights(pcs)
                self.phase2c(l, xin, sh)
                self.bg_flush()
                self.s.barrier()
                if self.debug == 3:
                    return
            self.phase3(l, xout, sh)
            self.s.barrier()

    def rms_tile(self, xt, xt_tk, xn, xn_tk, junk, junk_tk, st, st_tk, P=128):
        self.act(junk[0:P, :], xt[0:P, :], AF.Square, (xt_tk,), (junk_tk, st_tk), scale=1.0 / 32.0, accum_out=st[0:P, 0:1])
        self.act(st[0:P, 1:2], st[0:P, 0:1], AF.Ln, (st_tk, self.const_tk), (st_tk,), bias=self.epsT[0:P, :])
        self.act(st[0:P, 2:3], st[0:P, 1:2], AF.Exp, (st_tk,), (st_tk,), scale=-0.5)
        self.act(xn[0:P, :], xt[0:P, :], AF.Copy, (xt_tk, st_tk), (xn_tk,), scale=st[0:P, 2:3])

    def phase1(self, l, xin):
        W = self.W
        ctk, abk = self.const_tk, self.ab_tk
        NCB = 4
        with ExitStack() as es:
            A = self.A
            wi = A(es, "wi", [128, 8, INW], BF16)
            wi_tk = [Tk() for _ in range(19)]
            wst = [A(es, "wst", [128, 4, 128], F32) for _ in range(4)]
            wst_tk = [Tk() for _ in range(4)]
            cct = A(es, "cct", [128, 32, 64], F32)
            sst = A(es, "sst", [128, 32, 64], F32)
            gt = A(es, "gt", [128, 18, 64], F32)
            gsm = A(es, "gsm", [128, 4, 64], F32)
            tab_tk = Tk()
            xt = [A(es, "xt", [128, D], F32) for _ in range(2)]
            xt_tk = [Tk(), Tk()]
            xn4 = [A(es, "xn4", [128, 4, D], F32)] * 2
            xn_tk = [[Tk() for _ in range(4)]] * 2
            junk = A(es, "junk", [128, D], BF16)
            junk_tk = Tk()
            st = [A(es, "st", [128, 4], F32) for _ in range(2)]
            st_tk = [Tk(), Tk()]
            hT = [A(es, "hT", [128, 8, 512], BF16) for _ in range(2)]
            hT_tk = [[Tk() for _ in range(8)] for _ in range(2)]
            qs = [A(es, "qs", [128, 18, 64], F32) for _ in range(NCB)]
            sq = [A(es, "sq", [128, 18, 64], F32) for _ in range(NCB)]
            msb = [A(es, "msb", [128, 18, 64], F32) for _ in range(NCB)]
            qf = [A(es, "qf", [128, 1152], BF16) for _ in range(NCB)]
            ssq = [A(es, "ssq", [128, 3, 18], F32) for _ in range(NCB)]
            qs_tk = [Tk() for _ in range(NCB)]
            sq_tk = [Tk() for _ in range(NCB)]
            ms_tk = [Tk() for _ in range(NCB)]
            qf_tk = [Tk() for _ in range(NCB)]
            ssq_tk = [Tk() for _ in range(NCB)]
            qkst = [A(es, "qkst", [128, 9, 512], BF16)] * 2
            qkst_tk = [Tk()] * 2
            vst = [A(es, "vst", [128, 8, 128], BF16) for _ in range(2)]
            vst_tk = [Tk(), Tk()]
            ust = [A(es, "ust", [128, 3, 512], BF16)] * 2
            ust_tk = [Tk()] * 2
            sg = [A(es, "sg", [128, 512], F32) for _ in range(2)]
            sg_tk = [Tk(), Tk()]

            groups = ((0, 512), (512, 1024), (1024, 1152), (1152, 1664))
            chunks = [(s, c) for s in range(self.nseq) for c in range(8)]
            cnt = {"x": 0, "ch": 0}

            def stNd(i, t4):
                s, c = chunks[i]
                c0 = c * 512
                b = t4 % 2
                self.dma(xt[b][:], xin[s][c0 + t4 * 128:c0 + (t4 + 1) * 128, :], (), (xt_tk[b],))

            def stNc(i, t4):
                X4, X4t = xn4[i % 2], xn_tk[i % 2]
                b = t4 % 2
                self.rms_tile(xt[b], xt_tk[b], X4[:, t4, :], X4t[t4], junk, junk_tk, st[b], st_tk[b])

            def stN(i):
                for t4 in range(4):
                    stNd(i, t4)
                    stNc(i, t4)

            def stT(i):
                s, c = chunks[i]
                ab = self.ab[:, s]
                X4, X4t = xn4[i % 2], xn_tk[i % 2]
                H, Ht = hT[i % 2], hT_tk[i % 2]
                for k in range(8):
                    ps, _, pt = self.ps()
                    for t4 in range(4):
                        self.tr(ps[:, t4 * 128:(t4 + 1) * 128], X4[:, t4, k * 128:(k + 1) * 128], self.ident[:], (X4t[t4], ctk), (pt,))
                    if k % 2 == 0:
                        self.ts("dve", H[:, k, :], ps[:, :], ab[:, 0, k:k + 1], ab[:, 1, k:k + 1], ALU.mult, ALU.add, (pt, abk), (Ht[k],))
                    else:
                        self.act(H[:, k, :], ps[:, :], AF.Identity, (pt, abk), (Ht[k],), scale=ab[:, 0, k:k + 1], bias=ab[:, 1, k:k + 1])

            def stA(i, t4):
                s, c = chunks[i]
                tile = c * 4 + t4
                gtile = i * 4 + t4
                H, Ht = hT[i % 2], hT_tk[i % 2]
                cb = cnt["ch"] % NCB
                cnt["ch"] += 1
                pss = []
                for (g0, g1) in groups:
                    ps, _, pt = self.ps()
                    pss.append((ps, pt))
                    for k in range(8):
                        self.mm(ps[:, 0:g1 - g0], H[:, k, t4 * 128:(t4 + 1) * 128], wi[:, k, g0:g1], k == 0, k == 7,
                                (Ht[k],) + tuple(wi_tk[g0 // 128:(g1 + 127) // 128]), (pt,))
                Q_, SQ, MS, QF, SS = qs[cb], sq[cb], msb[cb], qf[cb], ssq[cb]
                qsf = Q_[:].rearrange("p h d -> p (h d)")
                sqf = SQ[:].rearrange("p h d -> p (h d)")
                gtf = gt[:].rearrange("p h d -> p (h d)")
                for gi, (g0, g1) in enumerate(groups[:3]):
                    self.act(sqf[:, g0:g1], pss[gi][0][:, 0:g1 - g0], AF.Square, (pss[gi][1],), (sq_tk[cb],))
                    self.tt("dve", qsf[:, g0:g1], pss[gi][0][:, 0:g1 - g0], gtf[:, g0:g1], ALU.mult, (pss[gi][1], tab_tk), (qs_tk[cb],))
                vb = gtile % 2
                self.cp("act", vst[vb][:, :, 0:64], pss[3][0][:, :].rearrange("p (h d) -> p h d", d=64), (pss[3][1],), (vst_tk[vb],))
                self.dma(self.vs_d[s, tile * 128:(tile + 1) * 128, :, :], vst[vb][:], (vst_tk[vb],), ())
                self.s.op("dve", lambda e: e.tensor_reduce(out=SS[:, 0, :], in_=SQ[:], axis=AX.X, op=ALU.add), (sq_tk[cb],), (ssq_tk[cb],))
                self.act(SS[:, 1, :], SS[:, 0, :], AF.Ln, (ssq_tk[cb], ctk), (ssq_tk[cb],), scale=1.0 / 64, bias=self.epsT[:, :])
                self.act(SS[:, 2, :], SS[:, 1, :], AF.Exp, (ssq_tk[cb],), (ssq_tk[cb],), scale=-0.5)
                return cb

            def stA2(i, t4, cb):
                s, c = chunks[i]
                tile = c * 4 + t4
                Q_, SQ, MS, QF, SS = qs[cb], sq[cb], msb[cb], qf[cb], ssq[cb]
                self.tt("pool", SQ[:], Q_[:], cct[:, tile:tile + 1, :].to_broadcast([128, 18, 64]), ALU.mult,
                        (qs_tk[cb], tab_tk, ssq_tk[cb]), (sq_tk[cb],))
                self.tt("pool", MS[:], Q_[:], sst[:, tile:tile + 1, :].to_broadcast([128, 18, 64]), ALU.mult, (qs_tk[cb], tab_tk), (ms_tk[cb],))
                self.tt("pool", Q_[:, :, 0:32], SQ[:, :, 0:32], MS[:, :, 32:64], ALU.subtract, (sq_tk[cb], ms_tk[cb]), (qs_tk[cb],))
                self.tt("pool", Q_[:, :, 32:64], SQ[:, :, 32:64], MS[:, :, 0:32], ALU.add, (sq_tk[cb], ms_tk[cb]), (qs_tk[cb],))
                self.tt("dve", QF[:].rearrange("p (h d) -> p h d", d=64), Q_[:],
                        SS[:, 2, :].unsqueeze(2).to_broadcast([128, 18, 64]), ALU.mult, (qs_tk[cb], ssq_tk[cb]), (qf_tk[cb],))

            def stB(i, t4, cb):
                s, c = chunks[i]
                c0 = c * 512
                QF = qf[cb]
                KS, KSt = qkst[i % 2], qkst_tk[i % 2]
                ps, psb, pt = self.ps()
                ps2, psb2, pt2 = self.ps()
                for j in range(8):
                    self.tr(psb[:, j * 128:(j + 1) * 128], QF[:, j * 128:(j + 1) * 128], self.identb[:], (qf_tk[cb], ctk), (pt,))
                self.tr(psb2[:, 0:128], QF[:, 1024:1152], self.identb[:], (qf_tk[cb], ctk), (pt2,))
                self.cp("act", KS[:, 0:8, t4 * 128:(t4 + 1) * 128], psb[:, :].rearrange("p (i t) -> p i t", t=128), (pt,), (KSt,))
                self.cp("dve", KS[:, 8, t4 * 128:(t4 + 1) * 128], psb2[:, 0:128], (pt2,), (KSt,))
                if t4 == 3:
                    self.dma(self.qk_d[s, :, c0:c0 + 512].rearrange("(i p) t -> p i t", p=128), KS[:], (KSt,), ())

            def stG(i):
                s, c = chunks[i]
                c0 = c * 512
                H, Ht = hT[i % 2], hT_tk[i % 2]
                ub = i % 2
                for cb in range(3):
                    sb = (i * 3 + cb) % 2
                    ps, _, pt = self.ps()
                    g0 = (16 + cb) * 128
                    for k in range(8):
                        self.mm(ps[:, :], wi[:, k, g0:g0 + 128], H[:, k, :], k == 0, k == 7, (Ht[k], wi_tk[16 + cb]), (pt,))
                    ps2, _, pt2 = self.ps()
                    g0 = (13 + cb) * 128
                    for k in range(8):
                        self.mm(ps2[:, :], wi[:, k, g0:g0 + 128], H[:, k, :], k == 0, k == 7, (Ht[k], wi_tk[13 + cb]), (pt2,))
                    self.act(sg[sb][:], ps[:, :], AF.Exp, (pt,), (sg_tk[sb],), scale=-1.0)
                    self.act(sg[sb][:], sg[sb][:], AF.Ln, (sg_tk[sb],), (sg_tk[sb],), bias=1.0)
                    self.act(sg[sb][:], sg[sb][:], AF.Exp, (sg_tk[sb],), (sg_tk[sb],), scale=-1.0)
                    self.tt("dve", ust[ub][:, cb, :], ps2[:, :], sg[sb][:], ALU.mult, (pt2, sg_tk[sb]), (ust_tk[ub],))
                self.dma(self.u_d[s, :, c0:c0 + 512].rearrange("(i p) t -> p i t", p=128), ust[ub][:], (ust_tk[ub],), ())

            n = len(chunks)
            tiles = [(i, t4) for i in range(n) for t4 in range(4)]
            cbs = {}
            stN(0)
            self.dma(cct[:], self.cc_d.rearrange("(t p) c -> p t c", p=128), (), (tab_tk,))
            self.dma(sst[:], self.ss_d.rearrange("(t p) c -> p t c", p=128), (), (tab_tk,))
            for gi, nm in enumerate(("qn_a", "kn_a", "qn_b", "kn_b")):
                self.dma(gsm[:, gi, :], W[nm][l:l + 1, :].to_broadcast([128, 64]), (), (tab_tk,))
            for (h0, h1, gi) in ((0, 4, 0), (4, 6, 1), (6, 12, 2), (12, 18, 3)):
                self.cp("pool", gt[:, h0:h1, :], gsm[:, gi:gi + 1, :].to_broadcast([128, h1 - h0, 64]), (tab_tk,), (tab_tk,))
            for v in vst:
                self.memset("pool", v[:], 1.0, (vst_tk[0], vst_tk[1]))
            for blk in range(19):
                sc = SRC_COL[blk]
                for kh in range(2):
                    b = (blk * 2 + kh) % 4
                    self.dma(wst[b][:], W["w_in"][l][kh * 512:(kh + 1) * 512, sc:sc + 128].rearrange("(k p) c -> p k c", p=128),
                             (), (wst_tk[b],))
                    self.cp("act", wi[:, 4 * kh:4 * kh + 4, blk * 128:(blk + 1) * 128], wst[b][:], (wst_tk[b],), (wi_tk[blk],))

            stT(0)
            if n > 1:
                stNd(1, 0)
                stNd(1, 1)
            for u, (i, t4) in enumerate(tiles):
                cbs[u] = stA(i, t4)
                if i + 1 < n:
                    if t4 == 0:
                        stNc(i + 1, 0)
                        stNd(i + 1, 2)
                    elif t4 == 1:
                        stNc(i + 1, 1)
                        stNd(i + 1, 3)
                    elif t4 == 2:
                        stNc(i + 1, 2)
                        stNc(i + 1, 3)
                    else:
                        stT(i + 1)
                        if i + 2 < n:
                            stNd(i + 2, 0)
                            stNd(i + 2, 1)
                if u >= 1:
                    stA2(tiles[u - 1][0], tiles[u - 1][1], cbs[u - 1])
                if u >= 3:
                    stB(tiles[u - 3][0], tiles[u - 3][1], cbs[u - 3])
                if t4 == 3:
                    stG(i)
            nt = len(tiles)
            stA2(tiles[nt - 1][0], tiles[nt - 1][1], cbs[nt - 1])
            for u in range(max(0, nt - 3), nt):
                stB(tiles[u][0], tiles[u][1], cbs[u])

    def phase2a(self, l):
        W = self.W
        ctk = self.const_tk
        with ExitStack() as es:
            A = self.A
            kbuf = [A(es, "kbuf", [128, PADK + S + PADK], BF16) for _ in range(2)]
            qbuf = [A(es, "qbuf", [128, S], BF16) for _ in range(2)]
            kq_tk = [Tk(), Tk()]
            kqp_tk = [[Tk() for _ in range(24)] for _ in range(2)]
            accs = [A(es, "acc", [128, 2048], F32) for _ in range(4)]
            accs_tk = [Tk() for _ in range(4)]
            lnd = A(es, "lnd", [128, 2048], F32)
            rcp = lnd
            fin_tk = Tk()
            mst = [A(es, "mst", [128, 2048], BF16) for _ in range(2)]
            mst_tk = [Tk(), Tk()]
            NP = 8
            pT = [A(es, "pT", [128, 512], BF16) for _ in range(NP)]
            pT_tk = [Tk() for _ in range(NP)]
            NV = 14
            vt = [A(es, "vt", [128, 2, 128], BF16) for _ in range(NV)]
            vt_tk = [Tk() for _ in range(NV)]
            sk = A(es, "sk", [128, 4], F32)
            sk_tk = Tk()
            for b in range(2):
                self.memset("dve", kbuf[b][:, 0:PADK], 0.0, (kq_tk[b],))
                self.memset("dve", kbuf[b][:, PADK + S:], 0.0, (kq_tk[b],))
            for v in range(NV):
                self.memset("dve", vt[v][:], 0.0, (vt_tk[v],))
            self.dma(sk[:], W["sink_a"][l:l + 1, :].to_broadcast([128, 4]), (), (sk_tk,))
            self.act(sk[:], sk[:], AF.Exp, (sk_tk,), (sk_tk,))
            pcount = 0
            vcount = 0
            pend = []
            finq = []
            pgs0 = []
            for p in range(2):
                pgs0.append(dict(q0=128 * p, k=[(256 + 64 * p, 0), (256 + 64 * p, 64)], vh=p, nvh=1, a=True, o0=128 * p, hq=2 * p))
            for p in range(3):
                pgs0.append(dict(q0=384 + 128 * p, k=[(768 + 128 * p, 0), (768 + 128 * p + 64, 64)], vh=2 + 2 * p, nvh=2,
                                 a=False, o0=256 + 128 * p, hq=0))
            pgs = [dict(pg, s=s) for s in range(self.nseq) for pg in pgs0]

            def load_kq_pieces(pi):
                pg_ = pgs[pi]
                kb_ = pi % 2
                qk = self.qk_d[pg_["s"]]
                pcs = []
                for c8 in range(8):
                    cs = slice(c8 * 512, (c8 + 1) * 512)
                    pcs.append(lambda cs=cs, c8=c8: self.dma(qbuf[kb_][:, cs], qk[pg_["q0"]:pg_["q0"] + 128, cs], (), (kqp_tk[kb_][3 * c8],)))
                    for ki, (r0, dp0) in enumerate(pg_["k"]):
                        pcs.append(lambda cs=cs, r0=r0, dp0=dp0, c8=c8, ki=ki: self.dma(
                            kbuf[kb_][dp0:dp0 + 64, PADK + c8 * 512:PADK + (c8 + 1) * 512], qk[r0:r0 + 64, cs], (),
                            (kqp_tk[kb_][3 * c8 + 1 + ki],)))
                return pcs

            kqq = load_kq_pieces(0)
            fcount = 0
            for pi, pg in enumerate(pgs):
                kb = pi % 2
                K, Q, kqt = kbuf[kb], qbuf[kb], (kq_tk[kb],) + tuple(kqp_tk[kb])
                vs_d = self.vs_d[pg["s"]]
                while kqq:
                    kqq.pop(0)()
                if pi + 1 < len(pgs):
                    kqq = load_kq_pieces(pi + 1)
                pats = [(1, 3)] if pg["a"] else [(1, 2), (4, 2), (16, 2)]
                for ST in range(2):
                    t0 = ST * 2048
                    par = (pi * 2 + ST) % 2
                    acc = accs[2 * par:2 * par + 2]
                    acc_tk = accs_tk[2 * par:2 * par + 2]
                    for pidx, (d, nk) in enumerate(pats):
                        Ls = S // d
                        nqb = 2048 // (128 * d)
                        nblk = Ls // 128
                        cache = {}
                        pv = [None, None]
                        for b in range(16):
                            r, jj = divmod(b, nqb)
                            j = ST * nqb + jj
                            s0 = 128 * j
                            g, pb = divmod(b, 4)
                            if pb == 0:
                                pv = [self.psx(0, 4), self.psx(0, 4)]
                            if pg["a"]:
                                ms = [m for m in (j - 1, j, j + 1) if 0 <= m < nblk]
                                if j == 0:
                                    mask = self.maskA[:, 128:384]
                                elif j == nblk - 1:
                                    mask = self.maskA[:, 0:256]
                                else:
                                    mask = self.maskA[:, :]
                                koff = 0
                            else:
                                ms = [j, j + 1]
                                mask = None
                                koff = 64
                            nkk = len(ms)
                            slots = []
                            for m in ms:
                                key = (r, m)
                                if key not in cache:
                                    sl = vcount % NV
                                    vcount += 1
                                    ks = 128 * m - koff
                                    lo, hi = max(ks, 0), min(ks + 128, Ls)
                                    src = vs_d[:, pg["vh"]:pg["vh"] + pg["nvh"], :]
                                    tok0 = r + d * lo
                                    nrow = hi - lo
                                    if d == 1:
                                        srcv = src[tok0:tok0 + nrow]
                                    else:
                                        srcv = src[tok0:tok0 + d * (nrow - 1) + 1:d]
                                    self.dma(vt[sl][lo - ks:hi - ks, 0:pg["nvh"], :], srcv, (), (vt_tk[sl],))
                                    cache[key] = sl
                                slots.append(cache[key])
                            self.bg_step(1)
                            heads_iter = [(0,), (1,)] if pg["a"] else [(0, 1)]
                            if not pg["a"]:
                                mask = self.maskB2[1 if j == 0 else (2 if j == nblk - 1 else 0)][:, :]
                            for hs_ in heads_iter:
                                pi_ = pcount % NP
                                pcount += 1
                                mask_eng = "pool" if pcount % 3 == 0 else "dve"

                                def stA(hs_=hs_, r=r, s0=s0, ms=ms, nkk=nkk, mask=mask, koff=koff, pi_=pi_, d=d, mask_eng=mask_eng,
                                        K=K, Q=Q, kqt=kqt):
                                    W_ = nkk * 128
                                    qc0 = r + d * s0
                                    tot = len(hs_) * W_
                                    P, Pt = pT[pi_], pT_tk[pi_]
                                    if len(hs_) == 1:
                                        ps, _, pt = self.psx(4, 8)
                                        banks = [(ps, pt)]
                                        pin = ps[:, 0:tot]
                                        pout = P[:, 0:tot]
                                    else:
                                        c2 = self.psc.get("st2", 0)
                                        self.psc["st2"] = c2 + 1
                                        b0 = 4 + 2 * (c2 % 2)
                                        banks = [(self.PS[b0], self.PST[b0]), (self.PS[b0 + 1], self.PST[b0 + 1])]
                                        pin = self.psall[:, b0 * 512:(b0 + 2) * 512].rearrange("p (b c) -> p b c", c=512)[:, :, 0:W_]
                                        pout = P[:, 0:tot].rearrange("p (b c) -> p b c", c=W_)
                                    for ci, m in enumerate(ms):
                                        for idx, hi_ in enumerate(hs_):
                                            ps, pt = banks[idx]
                                            rows = slice(64 * hi_, 64 * hi_ + 64)
                                            qsl = Q[rows, qc0:qc0 + d * 127 + 1:d] if d > 1 else Q[rows, qc0:qc0 + 128]
                                            kc0 = PADK + r + d * (128 * m - koff)
                                            ksl = K[rows, kc0:kc0 + d * 127 + 1:d] if d > 1 else K[rows, kc0:kc0 + 128]
                                            self.mm(ps[:, ci * 128:(ci + 1) * 128], ksl, qsl, True, True, kqt, (pt,))
                                    pts = tuple(b_[1] for b_ in banks)
                                    self.act(pout, pin, AF.Exp, pts, (Pt,), scale=0.125)
                                    self.tt(mask_eng, P[:, 0:tot], P[:, 0:tot], mask, ALU.mult, (Pt, ctk), (Pt,))

                                def stB(hs_=hs_, ms=ms, nkk=nkk, slots=slots, pi_=pi_, pv=pv, pb=pb, g=g, d=d, pidx=pidx, a=pg["a"],
                                        acc=acc, acc_tk=acc_tk):
                                    P, Pt = pT[pi_], pT_tk[pi_]
                                    W_ = nkk * 128
                                    for idx, hi_ in enumerate(hs_):
                                        pvp, _, pvt = pv[hi_]
                                        for ci, m in enumerate(ms):
                                            sl = slots[ci]
                                            hsel = 0 if a else hi_
                                            self.mm(pvp[:, pb * 128:(pb + 1) * 128], vt[sl][:, hsel, :],
                                                    P[:, idx * W_ + ci * 128:idx * W_ + (ci + 1) * 128],
                                                    ci == 0, ci == nkk - 1, (vt_tk[sl], Pt), (pvt,))
                                    if pb == 3:
                                        for hi_ in hs_:
                                            pvp, _, pvt = pv[hi_]
                                            a_ = acc[hi_]
                                            if d == 1:
                                                dst = a_[:, 512 * g:512 * g + 512]
                                                srcp = pvp[:, :]
                                            elif d == 4:
                                                dst = a_[:, :].rearrange("p (x r) -> p r x", r=4)[:, g, :]
                                                srcp = pvp[:, :]
                                            else:
                                                dst = a_[:, :].rearrange("p (i r) -> p r i", r=16)[:, 4 * g:4 * g + 4, :]
                                                srcp = pvp[:, :].rearrange("p (r i) -> p r i", i=128)
                                            if pidx == 0:
                                                self.cp("dve" if hi_ == 0 else "act", dst, srcp, (pvt,), (acc_tk[hi_],))
                                            else:
                                                self.tt("dve", dst, srcp, dst, ALU.add, (pvt, acc_tk[hi_]), (acc_tk[hi_],))

                                stA()
                                pend.append(stB)
                                while len(pend) > 5:
                                    pend.pop(0)()
                                if finq:
                                    finq.pop(0)()
                                if kqq:
                                    kqq.pop(0)()
                    mb = fcount % 2
                    fcount += 1

                    def mkfin(pg=pg, acc=acc, acc_tk=acc_tk, mb=mb, t0=t0):
                        pieces = []
                        for hi_ in range(2):
                            a_ = acc[hi_]
                            for q4 in range(4):
                                cs = slice(512 * q4, 512 * q4 + 512)

                                def p_ln(a_=a_, hi_=hi_, cs=cs):
                                    if pg["a"]:
                                        hq = pg["hq"] + hi_
                                        self.act(lnd[64:128, cs], a_[64:128, cs], AF.Ln, (acc_tk[hi_], sk_tk), (fin_tk,), bias=sk[64:128, hq:hq + 1])
                                    else:
                                        self.act(lnd[64:128, cs], a_[64:128, cs], AF.Ln, (acc_tk[hi_],), (fin_tk,))

                                def p_exp(cs=cs):
                                    self.act(rcp[0:64, cs], lnd[64:128, cs], AF.Exp, (fin_tk,), (fin_tk,), scale=-1.0)

                                def p_mul(a_=a_, hi_=hi_, cs=cs):
                                    self.tt("dve", mst[mb][64 * hi_:64 * hi_ + 64, cs], a_[0:64, cs], rcp[0:64, cs], ALU.mult,
                                            (acc_tk[hi_], fin_tk), (mst_tk[mb],))
                                pieces += [p_ln, p_exp, p_mul]
                        pieces.append(lambda: self.dma(self.mx_d[pg["s"], pg["o0"]:pg["o0"] + 128, t0:t0 + 2048], mst[mb][:], (mst_tk[mb],), ()))
                        return pieces

                    pend.append(lambda pcs=mkfin(): finq.extend(pcs))
            while pend:
                pend.pop(0)()
            while finq:
                finq.pop(0)()
            while pend:
                pend.pop(0)()

    def phase2c(self, l, xin, sh):
        W = self.W
        ctk = self.const_tk
        with ExitStack() as es:
            A = self.A
            wo, wo_tk, diag, dg_tk = sh["wo"], sh["wo_tk"], sh["diag"], sh["dg_tk"]
            g1b = [A(es, "g1b", [128, D], F32) for _ in range(2)]
            g_tk = [Tk(), Tk()]
            ub = [A(es, "ub", [128, 3, 542], BF16) for _ in range(3)]
            ub_tk = [Tk() for _ in range(3)]
            mix = [A(es, "mix", [128, 8, 512], BF16) for _ in range(4)]
            mixa_tk = [Tk() for _ in range(4)]
            mixc_tk = [Tk() for _ in range(4)]
            ysb = [A(es, "ysb", [128, 3, 512], F32) for _ in range(2)]
            ysb_tk = [Tk(), Tk()]
            ytok = [A(es, "ytok", [128, 384], F32) for _ in range(2)]
            ytok_tk = [Tk(), Tk()]
            yn4 = A(es, "yn4", [128, 4, 384], F32)
            yn_tk = [Tk() for _ in range(4)]
            bst = [A(es, "bst", [128, 12], F32) for _ in range(2)]
            bst_tk = [Tk(), Tk()]
            xt = [A(es, "xt", [128, D], F32) for _ in range(4)]
            xt_tk = [Tk() for _ in range(4)]
            tmpo = [A(es, "tmpo", [128, 512], F32) for _ in range(2)]
            tmpo_tk = [Tk(), Tk()]
            junk = A(es, "junk", [128, 384], BF16)
            junk_tk = Tk()
            for s in range(self.nseq):
                row = (l * self.nseq + s) * 2
                if s < 2:
                    self.dma(g1b[s % 2][:], self.gs_d[row:row + 1, :].to_broadcast([128, D]), (), (g_tk[s % 2],))
            for b in range(3):
                self.memset("pool", ub[b][:], 0.0, (ub_tk[b],))
            chunks = [(s, c) for s in range(self.nseq) for c in range(8)]
            cnt = {"x": 0, "t": 0}

            def stL(i):
                s, c = chunks[i]
                c0 = c * 512
                b = i % 3
                mb = i % 4
                lo, hi = max(c0 - 15, 0), min(c0 + 527, S)
                if c == 7:
                    self.memset("pool", ub[b][:, :, 527:542], 0.0, (ub_tk[b],))
                if c == 0:
                    self.memset("pool", ub[b][:, :, 0:15], 0.0, (ub_tk[b],))
                self.dma(ub[b][:, :, lo - (c0 - 15):hi - (c0 - 15)], self.u_d[s, :, lo:hi].rearrange("(i p) t -> p i t", p=128),
                         (), (ub_tk[b],))
                self.dma(mix[mb][:, 0:5, :], self.mx_d[s, :, c0:c0 + 512].rearrange("(i p) t -> p i t", p=128), (), (mixa_tk[mb],))

            def st1(i):
                b = i % 3
                yb = i % 2
                for cb in range(3):
                    ps, _, pt = self.ps()
                    for k in range(31):
                        self.mm(ps[:, :], diag[:, cb * 31 + k, :], ub[b][:, cb, k:k + 512], k == 0, k == 30, (dg_tk, ub_tk[b]), (pt,))
                    self.act(ysb[yb][:, cb, :], ps[:, :], AF.Identity, (pt, ctk), (ysb_tk[yb],), bias=self.cvec[:, l, 0, cb:cb + 1])

            def st2(i):
                b = i % 2
                for t4 in range(4):
                    tb = cnt["t"] % 2
                    cnt["t"] += 1
                    ps, _, pt = self.ps()
                    for cb in range(3):
                        self.tr(ps[:, cb * 128:(cb + 1) * 128], ysb[b][:, cb, t4 * 128:(t4 + 1) * 128], self.ident[:], (ysb_tk[b], ctk), (pt,))
                    B = bst[tb]
                    Y = ytok[tb]
                    self.act(Y[:], ps[:, 0:384], AF.Copy, (pt,), (ytok_tk[tb], bst_tk[tb]), accum_out=B[:, 0:1])
                    self.ts("dve", B[:, 1:2], B[:, 0:1], -1.0 / 384.0, None, ALU.mult, None, (bst_tk[tb],), (bst_tk[tb],))
                    self.ts("dve", Y[:], Y[:], B[:, 1:2], None, ALU.add, None, (ytok_tk[tb], bst_tk[tb]), (ytok_tk[tb],))
                    self.act(junk[:], Y[:], AF.Square, (ytok_tk[tb],), (junk_tk, bst_tk[tb]), scale=1.0 / math.sqrt(384.0), accum_out=B[:, 2:3])
                    self.ts("pool", B[:, 3:4], B[:, 2:3], EPS, None, ALU.add, None, (bst_tk[tb],), (bst_tk[tb],))
                    self.tt("pool", B[:, 4:5], B[:, 3:4], self.mhalf[:, :], ALU.pow, (bst_tk[tb], ctk), (bst_tk[tb],))
                    self.act(yn4[:, t4, :], Y[:], AF.Copy, (ytok_tk[tb], bst_tk[tb]), (yn_tk[t4],), scale=B[:, 4:5])

            def st3(i):
                mb = i % 4
                for cb in range(3):
                    ps, _, pt = self.ps()
                    for t4 in range(4):
                        self.tr(ps[:, t4 * 128:(t4 + 1) * 128], yn4[:, t4, cb * 128:(cb + 1) * 128], self.ident[:], (yn_tk[t4], ctk), (pt,))
                    self.act(mix[mb][:, 5 + cb, :], ps[:, :], AF.Silu, (pt, ctk), (mixc_tk[mb],),
                             scale=self.cvec[:, l, 1, cb:cb + 1], bias=self.cvec[:, l, 2, cb:cb + 1])

            def stXl(i):
                s, c = chunks[i]
                for t4 in range(4):
                    r0 = c * 512 + t4 * 128
                    self.dma(xt[t4][:], xin[s][r0:r0 + 128, :], (), (xt_tk[t4],))

            def st4(i):
                s, c = chunks[i]
                c0 = c * 512
                mb = i % 4
                if c == 0 and s >= 2:
                    row = (l * self.nseq + s) * 2
                    self.dma(g1b[s % 2][:], self.gs_d[row:row + 1, :].to_broadcast([128, D]), (), (g_tk[s % 2],))
                G, Gt = g1b[s % 2], g_tk[s % 2]
                for t4 in range(4):
                    xb_ = t4
                    r0 = c0 + t4 * 128
                    for hh in range(2):
                        ps, _, pt = self.ps()
                        for k in range(8):
                            self.mm(ps[:, :], mix[mb][:, k, t4 * 128:(t4 + 1) * 128], wo[:, k, hh * 512:(hh + 1) * 512], k == 0, k == 7,
                                    (mixa_tk[mb], mixc_tk[mb], wo_tk[k]), (pt,))
                        self.tt("dve", tmpo[hh][:], ps[:, :], G[:, hh * 512:(hh + 1) * 512], ALU.mult, (pt, Gt), (tmpo_tk[hh],))
                        self.tt("pool", xt[xb_][:, hh * 512:(hh + 1) * 512], tmpo[hh][:], xt[xb_][:, hh * 512:(hh + 1) * 512], ALU.add,
                                (tmpo_tk[hh], xt_tk[xb_]), (xt_tk[xb_],))
                    self.dma(self.xa_d[s, r0:r0 + 128, :], xt[xb_][:], (xt_tk[xb_],), ())

            n = len(chunks)
            stL(0)
            if n > 1:
                stL(1)
            st1(0)
            for i in range(n):
                if i + 2 < n:
                    stL(i + 2)
                self.bg_step(2)
                if i + 1 < n:
                    st1(i + 1)
                self.bg_step(2)
                if i >= 1:
                    stXl(i - 1)
                st2(i)
                if i >= 1:
                    st4(i - 1)
                st3(i)
            stXl(n - 1)
            st4(n - 1)

    def phase3(self, l, xout, sh):
        W = self.W
        ctk, abk = self.const_tk, self.ab_tk
        CH = 256
        NB = 4
        with ExitStack() as es:
            A = self.A
            wu = A(es, "wu", [128, 8, 2 * FF], BF16)
            wu_tk = [Tk() for _ in range(2 * NFB)]
            wd, wd_tk = sh["wd"], sh["wd_tk"]
            g2b = A(es, "g2b", [128, D], F32)
            g_tk = Tk()
            xt = [A(es, "xt", [128, D], F32) for _ in range(3)]
            xt_tk = [Tk() for _ in range(3)]
            junk = A(es, "junk", [128, D], BF16)
            junk_tk = Tk()
            st = [A(es, "st", [128, 4], F32) for _ in range(3)]
            st_tk = [Tk() for _ in range(3)]
            h2 = [A(es, "h2", [128, 8, CH + 2], BF16) for _ in range(2)]
            h2_tk = [Tk(), Tk()]
            hal = A(es, "hal", [128, 16], F32)
            hal_tk = Tk()
            t0b = [A(es, "t0b", [128, CH], F32) for _ in range(NB)]
            t0_tk = [Tk() for _ in range(NB)]
            aT = A(es, "aT", [128, NFB, CH], BF16)
            aT_tk = [Tk() for _ in range(NFB)]
            xr = [A(es, "xr", [128, D], F32) for _ in range(2)]
            xr_tk = [Tk(), Tk()]
            tmpo = [A(es, "tmpo", [128, 512], F32) for _ in range(2)]
            tmpo_tk = [Tk(), Tk()]
            wst = list(xr) + [A(es, "wst", [128, D], F32) for _ in range(3)]
            wst_tk = list(xr_tk) + [Tk(), Tk(), Tk()]
            row = (l * self.nseq) * 2 + 1
            self.dma(g2b[:], self.gs_d[row:row + 1, :].to_broadcast([128, D]), (), (g_tk,))
            nch = S // CH
            chunks = [(s, c) for s in range(self.nseq) for c in range(nch)]
            cnt = {"b": 0, "r": 0}

            def stPNd(i):
                s, c = chunks[i]
                c0 = c * CH
                xa = self.xa_d[s]
                for ti in range(2):
                    r0 = c0 + ti * 128
                    self.dma(xt[ti][:], xa[r0:r0 + 128, :], (), (xt_tk[ti],))
                has_l, has_r = c > 0, c < nch - 1
                if has_l and has_r:
                    self.dma(xt[2][0:2, :], xa[c0 - 1:c0 + CH + 1:CH + 1, :], (), (xt_tk[2],))
                elif has_r:
                    self.dma(xt[2][0:2, :], xa[c0 + CH - 1:c0 + CH + 1, :], (), (xt_tk[2],))
                else:
                    self.dma(xt[2][0:2, :], xa[c0 - 1:c0 + 1, :], (), (xt_tk[2],))

            def stPNc_ops(i):
                ops = []
                for ti in range(3):
                    P = 128 if ti < 2 else 2
                    X, Xt, ST, STt = xt[ti], xt_tk[ti], st[ti], st_tk[ti]
                    ops.append(lambda X=X, Xt=Xt, ST=ST, STt=STt, P=P: self.act(junk[0:P, :], X[0:P, :], AF.Square, (Xt,), (junk_tk, STt),
                                                                               scale=1.0 / 32.0, accum_out=ST[0:P, 0:1]))
                    ops.append(lambda ST=ST, STt=STt, P=P: self.ts("pool", ST[0:P, 1:2], ST[0:P, 0:1], EPS, None, ALU.add, None, (STt,), (STt,)))
                    ops.append(lambda ST=ST, STt=STt, P=P: self.tt("pool", ST[0:P, 2:3], ST[0:P, 1:2], self.mhalf[0:P, :], ALU.pow, (STt, ctk), (STt,)))
                    ops.append(lambda X=X, Xt=Xt, ST=ST, STt=STt, P=P: self.act(X[0:P, :], X[0:P, :], AF.Copy, (Xt, STt), (Xt,), scale=ST[0:P, 2:3]))
                return ops

            def stPNc(i):
                for o in stPNc_ops(i):
                    o()

            def stPT(i):
                s, c = chunks[i]
                ab = self.ab[:, s]
                H, Ht = h2[i % 2], h2_tk[i % 2]
                for ti, colo in ((0, 1), (1, 129)):
                    for half in range(2):
                        ps, _, pt = self.psx(6, 8)
                        for kk in range(4):
                            k = half * 4 + kk
                            self.tr(ps[:, kk * 128:(kk + 1) * 128], xt[ti][:, k * 128:(k + 1) * 128], self.ident[:], (xt_tk[ti], ctk), (pt,))
                        for kk in range(4):
                            k = half * 4 + kk
                            self.act(H[:, k, colo:colo + 128], ps[:, kk * 128:(kk + 1) * 128], AF.Identity, (pt, abk), (Ht,),
                                     scale=ab[:, 2, k:k + 1], bias=ab[:, 3, k:k + 1])
                ps, _, pt = self.psx(6, 8)
                for k in range(8):
                    self.tr(ps[:, k * 2:k * 2 + 2], xt[2][0:2, k * 128:(k + 1) * 128], self.ident[0:2, 0:2], (xt_tk[2], ctk), (pt,))
                hv3 = hal[:, 0:16].rearrange("p (k t) -> p k t", t=2)
                self.tt("dve", hv3, ps[:, 0:16].rearrange("p (k t) -> p k t", t=2),
                        ab[:, 2, :].unsqueeze(2).to_broadcast([128, 8, 2]), ALU.mult, (pt, abk), (hal_tk,))
                self.tt("dve", H[:, :, 0:CH + 2:CH + 1], hv3, ab[:, 3, :].unsqueeze(2).to_broadcast([128, 8, 2]), ALU.add, (hal_tk, abk), (Ht,))
                if c == 0:
                    self.memset("dve", H[:, :, 0:1], 0.0, (Ht,))
                if c == nch - 1:
                    self.memset("dve", H[:, :, CH + 1:CH + 2], 0.0, (Ht,))

            def stA(i, fb):
                H, Ht = h2[i % 2], h2_tk[i % 2]
                bb = cnt["b"] % NB
                cnt["b"] += 1
                psg, _, ptg = self.psx(0, 6)
                for k in range(8):
                    self.mm(psg[:, 0:CH + 2], wu[:, k, fb * 128:(fb + 1) * 128], H[:, k, :], k == 0, k == 7, (Ht, wu_tk[fb]), (ptg,))
                psu, _, ptu = self.psx(0, 6)
                for k in range(8):
                    self.mm(psu[:, 0:CH], wu[:, k, FF + fb * 128:FF + (fb + 1) * 128], H[:, k, 1:CH + 1], k == 0, k == 7,
                            (Ht, wu_tk[NFB + fb]), (ptu,))
                T0, T0t = t0b[bb], t0_tk[bb]
                self.act(T0[:], psg[:, 0:CH], AF.Identity, (ptg, ctk), (T0t,), scale=self.wf[:, l, fb, 0:1], bias=self.bf[:, l, fb:fb + 1])
                self.stt(T0[:], psg[:, 1:CH + 1], self.wf[:, l, fb, 1:2], T0[:], ALU.mult, ALU.add, (ptg, T0t, ctk), (T0t,))
                self.stt(T0[:], psg[:, 2:CH + 2], self.wf[:, l, fb, 2:3], T0[:], ALU.mult, ALU.add, (ptg, T0t, ctk), (T0t,))
                return (bb, psu, ptu)

            def stB(i, fb, info):
                bb, psu, ptu = info
                self.act(t0b[bb][:], t0b[bb][:], AF.Silu, (t0_tk[bb],), (t0_tk[bb],))
                self.tt("dve", aT[:, fb, :], psu[:, 0:CH], t0b[bb][:], ALU.mult, (ptu, t0_tk[bb]), (aT_tk[fb],))

            def stDl(i):
                s, c = chunks[i]
                for t2 in range(CH // 128):
                    r0 = c * CH + t2 * 128
                    self.dma(xr[t2][:], self.xa_d[s, r0:r0 + 128, :], (), (xr_tk[t2],))

            def stD(i):
                s, c = chunks[i]
                c0 = c * CH
                if c == 0 and s >= 1:
                    row = (l * self.nseq + s) * 2 + 1
                    self.dma(g2b[:], self.gs_d[row:row + 1, :].to_broadcast([128, D]), (), (g_tk,))
                for t2 in range(CH // 128):
                    rb = t2
                    r0 = c0 + t2 * 128
                    for hh in range(2):
                        ps, _, pt = self.psx(6, 8)
                        for fb in range(NFB):
                            self.mm(ps[:, :], aT[:, fb, t2 * 128:(t2 + 1) * 128], wd[:, fb, hh * 512:(hh + 1) * 512], fb == 0, fb == NFB - 1,
                                    (aT_tk[fb], wd_tk[fb]), (pt,))
                        self.tt("dve", tmpo[hh][:], ps[:, :], g2b[:, hh * 512:(hh + 1) * 512], ALU.mult, (pt, g_tk), (tmpo_tk[hh],))
                        self.tt("pool", xr[rb][:, hh * 512:(hh + 1) * 512], tmpo[hh][:], xr[rb][:, hh * 512:(hh + 1) * 512], ALU.add,
                                (tmpo_tk[hh], xr_tk[rb]), (xr_tk[rb],))
                    self.dma(xout[s][r0:r0 + 128, :], xr[rb][:], (xr_tk[rb],), ())

            n = len(chunks)
            pend = []
            stPNd(0)
            stPNc(0)
            stPT(0)
            pc = 0
            for fp in range(NFB // 2):
                for part in range(2):
                    for kh in range(2):
                        b = pc % 5
                        pc += 1
                        col = part * FF + fp * 256
                        self.dma(wst[b][:].rearrange("p (k c) -> p k c", c=256),
                                 W["w_up"][l][kh * 512:(kh + 1) * 512, col:col + 256].rearrange("(k p) c -> p k c", p=128), (), (wst_tk[b],))
                        self.cp("act" if pc % 2 else "pool", wu[:, 4 * kh:4 * kh + 4, col:col + 256],
                                wst[b][:].rearrange("p (k c) -> p k c", c=256), (wst_tk[b],),
                                (wu_tk[part * NFB + 2 * fp], wu_tk[part * NFB + 2 * fp + 1]))
            for i in range(n):
                for fb in range(NFB):
                    info = stA(i, fb)
                    pend.append(lambda i=i, fb=fb, info=info: stB(i, fb, info))
                    if fb == NFB - 1:
                        pend.append(lambda i=i: stD(i))
                    while len(pend) > 2:
                        pend.pop(0)()
                    if fb == 12:
                        stDl(i)
                    if i + 1 < n:
                        if fb == 1:
                            stPNd(i + 1)
                            pnops = stPNc_ops(i + 1)
                        if 5 <= fb < 17:
                            pnops[fb - 5]()
                        if fb == NFB - 1:
                            stPT(i + 1)
            while pend:
                pend.pop(0)()


def sch_emit_engine(sch, eng, h, sems, dsems):
    if not getattr(sch, "_prepared", False):
        for e in sch.ENGS:
            c = 0
            for ent in sch.q[e]:
                if ent[2] is True:
                    c += 1
                ent[3] = c
        sch._prepared = True
    seen = {}
    for fn, deps, sig, _ in sch.q[eng]:
        for k, v in deps.items():
            if isinstance(k, tuple):
                sem, tgt = dsems[k[1]], v
            else:
                ent = sch.q[k][v]
                sem, tgt = sems[k], ent[3]
            if seen.get(k, 0) >= tgt:
                continue
            seen[k] = tgt
            h.wait_ge(sem, tgt)
        ins = fn(h)
        if sig is True:
            ins.then_inc(sems[eng], 1)
        elif isinstance(sig, tuple):
            ins.then_inc(dsems[sig[1]], 16)
    for k, v in sch.pending[eng].items():
        if isinstance(k, tuple):
            sem, tgt = dsems[k[1]], v
        else:
            sem, tgt = sems[k], sch.q[k][v][3]
        if seen.get(k, 0) >= tgt:
            continue
        h.wait_ge(sem, tgt)


def rope_tables():
    half = 32
    inv = (1.0 / (np.float32(10000.0) ** (np.arange(half, dtype=np.float32) / np.float32(half)))).astype(np.float32)
    ang = (np.arange(S, dtype=np.float32)[:, None] * inv[None, :]).astype(np.float32)
    c = np.cos(ang).astype(np.float32)
    s_ = np.sin(ang).astype(np.float32)
    return np.concatenate([c, c], axis=1), np.concatenate([s_, s_], axis=1)


_CACHE = {}


def kernel(**inputs):
    f = lambda a: np.ascontiguousarray(np.asarray(a), dtype=np.float32)
    xp, xs = f(inputs["x_prompt"]), f(inputs["x_sample"])
    cp, cs = f(inputs["c_prompt"]), f(inputs["c_sample"])
    wnames = ["norm1_g", "norm2_g", "w_mod", "b_mod", "w_in", "qn_a", "kn_a", "sink_a", "qn_b", "kn_b", "conv_c_w",
              "conv_c_b", "ln_c_g", "ln_c_b", "w_out", "w_up", "conv_f_w", "conv_f_b", "w_down"]
    wts = {n: f(inputs[n]) for n in wnames}
    cc, ss = rope_tables()
    if "nc" not in _CACHE:
        _CACHE["nc"] = Prog().build()
    nc = _CACHE["nc"]
    in_maps = []
    for i in range(8):
        m = dict(wts)
        m["x"] = np.ascontiguousarray(np.stack([xp[i], xs[2 * i], xs[2 * i + 1]]))
        m["c"] = np.ascontiguousarray(np.stack([cp[i], cs[2 * i], cs[2 * i + 1]]))
        m["ropec"] = cc
        m["ropes"] = ss
        in_maps.append(m)
    res = run_bass_kernel_spmd(nc, in_maps, core_ids=list(range(8)))
    yp = np.empty_like(xp)
    ys = np.empty_like(xs)
    for i in range(8):
        y = np.asarray(res.results[i]["y"], dtype=np.float32)
        yp[i] = y[0]
        ys[2 * i] = y[1]
        ys[2 * i + 1] = y[2]
    return (yp, ys)
```
